# Optimizing a Trainium2 kernel written in Bass

```python
import math
import jax
import jax.numpy as jnp
from jax import lax
import numpy as np

D_MODEL = 2048
BATCH = 4
SEQ = 8192
DEPTH = 4

D_MIX = D_MODEL
MLA_HEADS = 8
QK_NOPE_DIM = 128
QK_ROPE_DIM = 64
V_HEAD_DIM = 128
Q_LORA_RANK = 512
KV_LORA_RANK = 512
ROPE_THETA = 10000.0
Q_BLOCK = 128
MLA_WIDTH = MLA_HEADS * V_HEAD_DIM
CONV_CH = 512
CONV_WIDTH = 31
HGRN_HEADS = 4
HGRN_DK = 128
HGRN_DV = 128
HGRN_WIDTH = HGRN_HEADS * HGRN_DV
HGRN_CHUNK = 64
D_FF = ((8 * D_MODEL // 3 + 255) // 256) * 256
RMS_EPS = 1e-6
LN_EPS = 1e-5

COLS_MLA = (Q_LORA_RANK, KV_LORA_RANK, QK_ROPE_DIM)
COLS_CONV = (2 * CONV_CH,)
COLS_HGRN = (HGRN_HEADS * HGRN_DK, HGRN_HEADS * HGRN_DK,
             HGRN_WIDTH, HGRN_WIDTH)
IN_COLS = sum(COLS_MLA) + sum(COLS_CONV) + sum(COLS_HGRN)
SPLIT_POINTS = tuple(int(v) for v in np.cumsum(COLS_MLA + COLS_CONV + COLS_HGRN)[:-1])

kernel_name = "hymba_style_mla_conformer_hgrn2_trunk"


def rms_norm(x, w, eps=RMS_EPS):
    xf = x.astype(jnp.float32)
    y = xf * lax.rsqrt(jnp.mean(xf * xf, axis=-1, keepdims=True) + eps)
    return (y * w.astype(jnp.float32)).astype(x.dtype)


def layer_norm(x, w, b, eps=LN_EPS):
    xf = x.astype(jnp.float32)
    mu = jnp.mean(xf, axis=-1, keepdims=True)
    var = jnp.mean(jnp.square(xf - mu), axis=-1, keepdims=True)
    y = (xf - mu) * lax.rsqrt(var + eps)
    return (y * w.astype(jnp.float32) + b.astype(jnp.float32)).astype(x.dtype)


def rope_tables(positions):
    inv_freq = ROPE_THETA ** (-jnp.arange(0, QK_ROPE_DIM, 2, dtype=jnp.float32) / QK_ROPE_DIM)
    ang = positions.astype(jnp.float32)[..., None] * inv_freq
    return jnp.cos(ang), jnp.sin(ang)


def apply_rope(x, cos, sin):
    x1, x2 = jnp.split(x, 2, axis=-1)
    out = jnp.concatenate([x1 * cos - x2 * sin, x2 * cos + x1 * sin], axis=-1)
    return out.astype(x.dtype)


def mla_group(c_q, c_kv, k_rope_raw, cos, sin, q_norm_w, w_uq, kv_norm_w, w_ukv, out_norm_w):
    B, S, _ = c_q.shape
    q = (rms_norm(c_q, q_norm_w) @ w_uq).reshape(B, S, MLA_HEADS, QK_NOPE_DIM + QK_ROPE_DIM)
    q_nope, q_pe = q[..., :QK_NOPE_DIM], q[..., QK_NOPE_DIM:]
    q_pe = apply_rope(q_pe, cos[:, :, None, :], sin[:, :, None, :])
    kv = (rms_norm(c_kv, kv_norm_w) @ w_ukv).reshape(B, S, MLA_HEADS, QK_NOPE_DIM + V_HEAD_DIM)
    k_nope, v = kv[..., :QK_NOPE_DIM], kv[..., QK_NOPE_DIM:]
    k_pe = apply_rope(k_rope_raw, cos, sin)
    scale = (QK_NOPE_DIM + QK_ROPE_DIM) ** -0.5
    nb = S // Q_BLOCK
    qn_b = q_nope.reshape(B, nb, Q_BLOCK, MLA_HEADS, QK_NOPE_DIM).transpose(1, 0, 3, 2, 4)
    qp_b = q_pe.reshape(B, nb, Q_BLOCK, MLA_HEADS, QK_ROPE_DIM).transpose(1, 0, 3, 2, 4)
    key_idx = jnp.arange(S)

    def attend(args):
        qn, qp, blk = args
        s = (jnp.einsum('bhqd,bkhd->bhqk', qn, k_nope)
             + jnp.einsum('bhqr,bkr->bhqk', qp, k_pe)).astype(jnp.float32) * scale
        q_idx = blk * Q_BLOCK + jnp.arange(Q_BLOCK)
        mask = key_idx[None, :] <= q_idx[:, None]
        p = jax.nn.softmax(jnp.where(mask, s, -jnp.inf), axis=-1).astype(v.dtype)
        return jnp.einsum('bhqk,bkhd->bqhd', p, v)

    o = lax.map(attend, (qn_b, qp_b, jnp.arange(nb)))
    o = o.transpose(1, 0, 2, 3, 4).reshape(B, S, MLA_HEADS, V_HEAD_DIM)
    o = rms_norm(o, out_norm_w)
    return o.reshape(B, S, MLA_WIDTH)


def conv_group(u, conv_w, conv_b, ln_w, ln_b):
    a, gate = jnp.split(u, 2, axis=-1)
    h = a * jax.nn.sigmoid(gate)
    h = lax.conv_general_dilated(
        h, conv_w[:, None, :].astype(h.dtype), window_strides=(1,),
        padding=[(CONV_WIDTH - 1, 0)],
        dimension_numbers=('NWC', 'WIO', 'NWC'),
        feature_group_count=CONV_CH) + conv_b
    h = layer_norm(h, ln_w, ln_b)
    return jax.nn.silu(h)


def hgrn2_group(q_raw, f_raw, i_raw, g_raw, lower_bound, norm_w):
    B, S, _ = q_raw.shape
    H, DK, DV, C = HGRN_HEADS, HGRN_DK, HGRN_DV, HGRN_CHUNK
    nc = S // C
    q = jax.nn.silu(q_raw.astype(jnp.float32)).reshape(B, S, H, DK)
    zf = f_raw.astype(jnp.float32).reshape(B, S, H, DK)
    lb = lower_bound.reshape(H, DK)
    log_f = jnp.logaddexp(jnp.log(lb), jnp.log1p(-lb) + jax.nn.log_sigmoid(zf))
    k = -jnp.expm1(log_f)
    v = i_raw.astype(jnp.float32).reshape(B, S, H, DV)

    def to_chunks(t):
        return t.reshape(B, nc, C, H, t.shape[-1]).transpose(1, 0, 3, 2, 4)

    causal = jnp.tril(jnp.ones((C, C), dtype=bool))

    def step(state, xs):
        qc, kc, vc, lf = xs
        b = jnp.cumsum(lf, axis=-2)
        inter = jnp.einsum('bhtk,bhkv->bhtv', qc * jnp.exp(b), state)
        diff = b[:, :, :, None, :] - b[:, :, None, :, :]
        decay = jnp.exp(jnp.where(causal[:, :, None], diff, -jnp.inf))
        scores = jnp.einsum('bhtk,bhsk,bhtsk->bhts', qc, kc, decay)
        intra = jnp.einsum('bhts,bhsv->bhtv', scores, vc)
        b_last = b[:, :, -1:, :]
        new_state = (jnp.exp(b_last)[:, :, 0, :, None] * state
                     + jnp.einsum('bhsk,bhsv->bhkv', kc * jnp.exp(b_last - b), vc))
        return new_state, inter + intra

    state0 = jnp.zeros((B, H, DK, DV), jnp.float32)
    _, o = lax.scan(step, state0, (to_chunks(q), to_chunks(k), to_chunks(v), to_chunks(log_f)))
    o = o.transpose(1, 0, 3, 2, 4).reshape(B, S, H, DV).astype(q_raw.dtype)
    g = jax.nn.silu(g_raw).reshape(B, S, H, DV)
    o = rms_norm(o, norm_w) * g
    return o.reshape(B, S, HGRN_WIDTH)


def setup_inputs(seed: int = 0) -> dict:
    key = jax.random.key(seed)
    ks = jax.random.split(key, 24)
    f32 = jnp.float32

    def nrm(k, shape, scale):
        return jax.random.normal(k, shape, f32) * scale

    def gain(k, shape):
        return 1.0 + 0.02 * jax.random.normal(k, shape, f32)

    x = jax.random.normal(ks[0], (BATCH, SEQ, D_MODEL), f32)
    start = jax.random.randint(ks[1], (BATCH, 1), 0, 4096, dtype=jnp.int32)
    positions = (start + jnp.arange(SEQ, dtype=jnp.int32)[None, :]).astype(jnp.int32)
    return {
        "x": x,
        "positions": positions,
        "attn_norm_w": gain(ks[2], (DEPTH, D_MODEL)),
        "w_in": nrm(ks[3], (DEPTH, D_MODEL, IN_COLS), D_MODEL ** -0.5),
        "q_norm_w": gain(ks[4], (DEPTH, Q_LORA_RANK)),
        "w_uq": nrm(ks[5], (DEPTH, Q_LORA_RANK, MLA_HEADS * (QK_NOPE_DIM + QK_ROPE_DIM)), Q_LORA_RANK ** -0.5),
        "kv_norm_w": gain(ks[6], (DEPTH, KV_LORA_RANK)),
        "w_ukv": nrm(ks[7], (DEPTH, KV_LORA_RANK, MLA_HEADS * (QK_NOPE_DIM + V_HEAD_DIM)), KV_LORA_RANK ** -0.5),
        "mla_out_norm_w": gain(ks[8], (DEPTH, V_HEAD_DIM)),
        "conv_w": nrm(ks[9], (DEPTH, CONV_WIDTH, CONV_CH), CONV_WIDTH ** -0.5),
        "conv_b": nrm(ks[10], (DEPTH, CONV_CH), 0.02),
        "conv_ln_w": gain(ks[11], (DEPTH, CONV_CH)),
        "conv_ln_b": nrm(ks[12], (DEPTH, CONV_CH), 0.02),
        "hgrn_lower_bounds": nrm(ks[13], (DEPTH, HGRN_HEADS * HGRN_DK), 0.1),
        "hgrn_norm_w": gain(ks[14], (DEPTH, HGRN_DV)),
        "w_out": nrm(ks[15], (DEPTH, D_MIX, D_MODEL), D_MIX ** -0.5),
        "ffn_norm_w": gain(ks[16], (DEPTH, D_MODEL)),
        "w_gate": nrm(ks[17], (DEPTH, D_MODEL, D_FF), D_MODEL ** -0.5),
        "w_up": nrm(ks[18], (DEPTH, D_MODEL, D_FF), D_MODEL ** -0.5),
        "w_down": nrm(ks[19], (DEPTH, D_FF, D_MODEL), D_FF ** -0.5),
        "final_norm_w": gain(ks[20], (D_MODEL,)),
    }


def reference(x, positions, attn_norm_w, w_in, q_norm_w, w_uq, kv_norm_w, w_ukv, mla_out_norm_w,
              conv_w, conv_b, conv_ln_w, conv_ln_b, hgrn_lower_bounds, hgrn_norm_w, w_out,
              ffn_norm_w, w_gate, w_up, w_down, final_norm_w):
    cos, sin = rope_tables(positions)
    lb_all = jax.nn.softmax(hgrn_lower_bounds.astype(jnp.float32), axis=0)
    lb_all = jnp.cumsum(lb_all, axis=0)
    lb_all = lb_all - lb_all[0:1]

    h = x
    for l in range(DEPTH):
        xn = rms_norm(h, attn_norm_w[l])
        proj = xn @ w_in[l]
        (c_q, c_kv, k_rope_raw, conv_in,
         hq, hf, hi, hg) = jnp.split(proj, SPLIT_POINTS, axis=-1)
        y_a = mla_group(c_q, c_kv, k_rope_raw, cos, sin, q_norm_w[l], w_uq[l],
                        kv_norm_w[l], w_ukv[l], mla_out_norm_w[l])
        y_b = conv_group(conv_in, conv_w[l], conv_b[l], conv_ln_w[l], conv_ln_b[l])
        y_c = hgrn2_group(hq, hf, hi, hg, lb_all[l], hgrn_norm_w[l])
        mix = jnp.concatenate([y_a, y_b, y_c], axis=-1)
        h = h + mix @ w_out[l]

        xn = rms_norm(h, ffn_norm_w[l])
        ff = jax.nn.silu(xn @ w_gate[l]) * (xn @ w_up[l])
        h = h + ff @ w_down[l]
    return rms_norm(h, final_norm_w)
```

```python
import numpy as np
import ml_dtypes
import concourse.bass as bass
import concourse.mybir as mybir
from concourse.bass_utils import run_bass_kernel_spmd

F32, BF16, I32 = mybir.dt.float32, mybir.dt.bfloat16, mybir.dt.int32
ALU = mybir.AluOpType
AF = mybir.ActivationFunctionType
AX = mybir.AxisListType

D = 2048
DFF = 5632
NH = 8
HH = 4
RMS_EPS = 1e-6
LN_EPS = 1e-5
PROJ_ROWS = 4224
R_CQ, R_CKV, R_CONV, R_HQ, R_HF, R_HI, R_HG, R_KR = 0, 512, 1024, 2048, 2560, 3072, 3584, 4096
TT = 512
TWO_PI = 6.283185307179586


class Sem:
    def __init__(self, nc, name):
        self.h = nc.alloc_semaphore(name)
        self.n = 0


class Eng:
    def __init__(self, nc, e, name):
        self.e = e
        self.sem = Sem(nc, "s_" + name)
        self.seen = {}

    def wait(self, *deps):
        for d in deps:
            if d is None:
                continue
            if isinstance(d, list):
                self.wait(*d)
                continue
            sem, val = d
            if self.seen.get(sem, 0) < val:
                self.e.wait_ge(sem.h, val)
                self.seen[sem] = val

    def op(self, ins):
        ins.then_inc(self.sem.h, 1)
        self.sem.n += 1
        return (self.sem, self.sem.n)


class Slot:
    def __init__(self, ap, sem=None):
        self.ap = ap
        self.sem = sem
        self.w = None
        self.r = []

    def deps_for_write(self):
        return [self.w] + self.r

    def wrote(self, t):
        self.w = t
        self.r = []

    def read(self, t):
        self.r.append(t)
        if len(self.r) > 6:
            self.r = self.r[-6:]


class Ring:
    def __init__(self, slots):
        self.slots = slots
        self.i = 0

    def next(self):
        s = self.slots[self.i % len(self.slots)]
        self.i += 1
        return s


class KB:
    def __init__(self, S, L, debug=False, pair=False):
        self.S, self.L, self.debug, self.pair = S, L, debug, pair
        self.NT = (S // 2 if pair else S) // TT
        self.nbar = 0
        nc = self.nc = bass.Bass("TRN2", target_bir_lowering=False)
        self.PE = Eng(nc, nc.tensor, "pe")
        self.ACT = Eng(nc, nc.scalar, "act")
        self.DVE = Eng(nc, nc.vector, "dve")
        self.POOL = Eng(nc, nc.gpsimd, "pool")
        self.SP = Eng(nc, nc.sync, "sp")
        self.engs = [self.PE, self.ACT, self.DVE, self.POOL, self.SP]
        self.dsems = []
        self.nsem = 0
        self.sb_stack = []
        self.phase_sems = []
        self.banks = [Slot(nc.alloc_psum_tensor(f"bank{i}", [128, 512], F32).ap()) for i in range(8)]
        self.pring = Ring(self.banks)

    def dsem(self, name=None):
        if getattr(self, "free_sems", None):
            s = self.free_sems.pop()
        else:
            self.nsem += 1
            s = Sem(self.nc, f"d{self.nsem}")
            self.dsems.append(s)
        self.phase_sems.append(s)
        return s

    def sem_mark(self):
        return len(self.phase_sems)

    def sem_release(self, mark):
        if not hasattr(self, "free_sems"):
            self.free_sems = []
        while len(self.phase_sems) > mark:
            self.free_sems.append(self.phase_sems.pop())

    def dma(self, q, out, in_, deps, sem):
        q.wait(*deps)
        ins = q.e.dma_start(out=out, in_=in_)
        ins.then_inc(sem.h, 16)
        sem.n += 16
        return (sem, sem.n)

    def dmar(self, q, out_fn, in_fn, deps, sem):
        if not self.pair:
            return self.dma(q, out_fn(0), in_fn(0), deps, sem)
        q.wait(*deps)
        with q.e.If(self.half == 0):
            q.e.dma_start(out=out_fn(0), in_=in_fn(0)).then_inc(sem.h, 16)
        with q.e.Else():
            q.e.dma_start(out=out_fn(1), in_=in_fn(1)).then_inc(sem.h, 16)
        sem.n += 16
        return (sem, sem.n)

    def barrier(self):
        ts = [(e.sem, e.sem.n) for e in self.engs if e.sem.n > 0]
        ts += [(s, s.n) for s in self.dsems if s.n > 0]
        for e in self.engs:
            e.wait(*ts)
        for b in self.banks:
            b.w = None
            b.r = []

    def sb(self, name, shape, dt):
        self.nsem += 1
        g = self.nc.sbuf_tensor(f"sb{self.nsem}_{name}", list(shape), dt)
        t = g.__enter__()
        self.sb_stack.append(g)
        return t.ap() if hasattr(t, "ap") else t

    def sb_mark(self):
        return (len(self.sb_stack), len(self.phase_sems))

    def sb_release(self, mark):
        mark, smark = mark
        while len(self.sb_stack) > mark:
            g = self.sb_stack.pop()
            g.__exit__(None, None, None)
        self.sem_release(smark)

    def dram(self, name, shape, dt, kind="Internal", shared=False):
        if shared and self.pair and kind == "Internal":
            return self.nc.dram_tensor(name, list(shape), dt, kind=kind, addr_space="Shared").ap()
        return self.nc.dram_tensor(name, list(shape), dt, kind=kind).ap()

    def pair_barrier(self):
        self.barrier()
        if not self.pair:
            return
        nc = self.nc
        POOL = self.POOL
        k = self.nbar
        self.nbar += 1
        PT = mybir.EngineType.Pool
        t = self.dmar(POOL, lambda r: self.FL[r:r + 1, k:k + 1], lambda r: self.fv[0:1, 0:1], [], self.flsem)
        POOL.wait(t)
        for r in (0, 1):
            lid = nc.next_id()
            ls, le = f"spin_{lid}_loop", f"spin_{lid}_end"
            regs = nc.alloc_registers(f"spin_{lid}_r", engines=[PT])
            nc.br(ls, engines=[PT])
            with nc.body(ls, valid_engines=[PT]):
                nc.reg_load(regs, self.FL[r:r + 1, k:k + 1])
                nc.br_ne(regs, self.nonce, on_true=ls, on_false=le, engines=[PT])
            nc.switch_bb(le)
        t = POOL.op(POOL.e.memset(self.bdummy, 0.0))
        for e in self.engs:
            e.wait(t)

    def mm_group(self, bank, pairs, deps, out_ap=None):
        PE = self.PE
        PE.wait(*deps, *bank.deps_for_write())
        o = bank.ap if out_ap is None else out_ap
        n = len(pairs)
        ins = None
        for i, (l, r) in enumerate(pairs):
            ins = PE.e.matmul(o, lhsT=l, rhs=r, start=(i == 0), stop=(i == n - 1))
        t = PE.op(ins)
        bank.wrote(t)
        return t


def build(S, L, debug=False, pair=False, nonce=12345):
    kb = KB(S, L, debug, pair)
    nc = kb.nc
    PE, ACT, DVE, POOL, SP = kb.PE, kb.ACT, kb.DVE, kb.POOL, kb.SP
    NT = kb.NT
    SL = NT * TT
    NB = S // 128
    L2 = L // 2 if pair else L
    NHL = NH // 2 if pair else NH
    HHL = HH // 2 if pair else HH
    bf = BF16
    kb.nonce = nonce
    if pair:
        half = (nc.partition_id() + FLIP) % 2
    else:
        half = 0
    kb.half = half


    def rsl(r, off, n, mult):
        return slice(r * mult + off, r * mult + off + n)

    x = kb.dram("x", [SL, D], F32, "ExternalInput")
    pos = kb.dram("pos", [1, SL], I32, "ExternalInput")
    w_in = kb.dram("w_in", [L2, D, 4160], F32, "ExternalInput")
    w_uq = kb.dram("w_uq", [L2, 512, 1536], F32, "ExternalInput")
    w_ukv = kb.dram("w_ukv", [L2, 512, 2048], F32, "ExternalInput")
    w_out = kb.dram("w_out", [L2, D, D], F32, "ExternalInput")
    w_gate = kb.dram("w_gate", [L2, D, DFF], F32, "ExternalInput")
    w_up = kb.dram("w_up", [L2, D, DFF], F32, "ExternalInput")
    w_down = kb.dram("w_down", [L2, DFF, D], F32, "ExternalInput")
    anw_d = kb.dram("anw", [128, L * 16], F32, "ExternalInput")
    fnw_d = kb.dram("fnw", [128, L * 16], F32, "ExternalInput")
    finw_d = kb.dram("finw", [1, D], F32, "ExternalInput")
    qnw_d = kb.dram("qnw", [128, L * 4], F32, "ExternalInput")
    kvnw_d = kb.dram("kvnw", [128, L * 4], F32, "ExternalInput")
    monw_d = kb.dram("monw", [128, L], F32, "ExternalInput")
    hnw_d = kb.dram("hnw", [128, L], F32, "ExternalInput")
    cw_d = kb.dram("cw", [128, L * 4 * 31], F32, "ExternalInput")
    cb_d = kb.dram("cb", [128, L * 4], F32, "ExternalInput")
    lnw_d = kb.dram("lnw", [128, L * 4], F32, "ExternalInput")
    lnb_d = kb.dram("lnb", [128, L * 4], F32, "ExternalInput")
    lb_d = kb.dram("lbraw", [128, HHL * L], F32, "ExternalInput")
    rankf_d = kb.dram("rankf", [128, 1], F32, "ExternalInput")
    kb.fv = kb.dram("fv", [1, 8], I32, "ExternalInput")
    zz_d = kb.dram("zz", [1, 64], I32, "ExternalInput")
    ident_d = kb.dram("ident", [128, 128], F32, "ExternalInput")
    cmask_d = kb.dram("cmask", [128, 4 * 512], F32, "ExternalInput")
    m64_d = kb.dram("m64", [64, 64], F32, "ExternalInput")
    invf_d = kb.dram("invf", [64, 2], F32, "ExternalInput")
    out = kb.dram("out", [SL, D], F32, "ExternalOutput")

    dk = "ExternalOutput" if debug else "Internal"
    PROJT = kb.dram("projt", [PROJ_ROWS, S], bf, dk, shared=True)
    MIXT = kb.dram("mixt", [D, S], bf, dk, shared=True)
    H = kb.dram("hres", [SL, D], F32, dk)
    QN = kb.dram("qn", [NH, 128, S], bf, dk, shared=True)
    QP = kb.dram("qp", [NH, 64, S], bf, dk, shared=True)
    KN = kb.dram("kn", [NH, 128, S], bf, dk, shared=True)
    KP = kb.dram("kp", [64, S], bf, dk, shared=True)
    VS = kb.dram("vs", [NB, 128, NH, 129], bf, dk, shared=True)
    COS = kb.dram("cosd", [64, SL], F32, dk)
    SIN = kb.dram("sind", [64, SL], F32, dk)
    WBin = kb.dram("wbin", [L, D, PROJ_ROWS], bf, dk, shared=True)
    WBuq = kb.dram("wbuq", [L, 512, 2048], bf, dk, shared=True)
    WBukv = kb.dram("wbukv", [L, 512, 2048], bf, shared=True)
    WBout = kb.dram("wbout", [L, D, D], bf, shared=True)
    WBg = kb.dram("wbg", [L, D, DFF], bf, shared=True)
    WBu = kb.dram("wbu", [L, D, DFF], bf, shared=True)
    WBd = kb.dram("wbd", [L, DFF, D], bf, shared=True)

    wsem = kb.dsem("wcast")

    cast_hist = []

    def cast2d(W, i, cs, src, rows, view=None):
        step = 256
        W2 = W.rearrange("l r c -> (l r) c")
        nr = W.shape[1]
        for r0 in range(0, rows, step):
            def dst(r):
                li = 2 * i + r if pair else i
                rs = slice(li * nr + r0, li * nr + r0 + step)
                return W2[rs, cs] if view is None else view(W2[rs])
            tk = kb.dmar(POOL, dst, lambda r: src[r0:r0 + step], [], wsem)

    if pair:
        kb.FL = kb.dram("flags", [2, 64], I32, shared=True)
        kb.flsem = kb.dsem("flag")
        t = kb.dmar(POOL, lambda r: kb.FL[r:r + 1], lambda r: zz_d, [], kb.flsem)
        POOL.wait(t)

    for i in range(L2):
        cast2d(WBin, i, slice(0, 1024), w_in[i][:, 0:1024], D)
        cast2d(WBin, i, slice(1024, 4096), w_in[i][:, 1088:4160], D)
        cast2d(WBin, i, slice(4096, 4160), w_in[i][:, 1024:1088], D)
        cast2d(WBin, i, slice(4160, 4192), w_in[i][:, 1056:1088], D)
        cast2d(WBin, i, slice(4192, 4224), w_in[i][:, 1024:1056], D)
        sq = w_uq[i].rearrange("k (h c) -> k h c", c=192)
        cast2d(WBuq, i, None, sq[:, :, 0:192], 512, view=lambda a_: a_.rearrange("k (h c) -> k h c", c=256)[:, :, 0:192])
        cast2d(WBuq, i, None, sq[:, :, 160:192], 512, view=lambda a_: a_.rearrange("k (h c) -> k h c", c=256)[:, :, 192:224])
        cast2d(WBuq, i, None, sq[:, :, 128:160], 512, view=lambda a_: a_.rearrange("k (h c) -> k h c", c=256)[:, :, 224:256])
        cast2d(WBukv, i, slice(0, 2048), w_ukv[i], 512)
        cast2d(WBout, i, slice(0, D), w_out[i], D)
        cast2d(WBg, i, slice(0, DFF), w_gate[i], D)
        cast2d(WBu, i, slice(0, DFF), w_up[i], D)
        cast2d(WBd, i, slice(0, D), w_down[i], DFF)

    csem = kb.dsem("const")
    ident = kb.sb("ident", [128, 128], bf)
    ident32 = kb.sb("ident32", [128, 128], F32)
    ones_bf = kb.sb("ones_bf", [128, 128], bf)
    ones32 = kb.sb("ones32", [128, 128], F32)
    anw = kb.sb("anw", [128, L * 16], F32)
    fnw = kb.sb("fnw", [128, L * 16], F32)
    qnw = kb.sb("qnw", [128, L * 4], F32)
    kvnw = kb.sb("kvnw", [128, L * 4], F32)
    monw = kb.sb("monw", [128, L], F32)
    hnw = kb.sb("hnw", [128, L], F32)
    cb = kb.sb("cb", [128, L * 4], F32)
    lnw = kb.sb("lnw", [128, L * 4], F32)
    lnb = kb.sb("lnb", [128, L * 4], F32)
    lbt = kb.sb("lbt", [128, HHL, L], F32)
    oml = kb.sb("oml", [128, HHL, L], F32)
    rankf = kb.sb("rankf", [128, 1], F32)
    kb.bdummy = kb.sb("bdummy", [128, 1], F32)
    m64 = kb.sb("m64", [64, 64], F32)
    invf = kb.sb("invf", [64, 2], F32)
    kb.dma(POOL, ident, ident_d, [], csem)
    kb.dma(SP, ident32, ident_d, [], csem)
    for (a, b_) in [(anw, anw_d), (fnw, fnw_d), (qnw, qnw_d), (kvnw, kvnw_d), (monw, monw_d), (hnw, hnw_d),
                    (cb, cb_d), (lnw, lnw_d), (lnb, lnb_d), (lbt.rearrange("p h l -> p (h l)"), lb_d),
                    (m64, m64_d), (invf, invf_d), (rankf, rankf_d)]:
        kb.dma(SP, a, b_, [], csem)
    tconst = (csem, csem.n)
    t = DVE.op(DVE.e.memset(ones_bf, 1.0))
    t = DVE.op(DVE.e.memset(ones32, 1.0))
    DVE.wait(tconst)
    ACT.wait(tconst)
    t = ACT.op(ACT.e.activation(out=lbt, in_=lbt, func=AF.Exp))
    lsum = kb.sb("lsum", [128, HHL, 1], F32)
    DVE.wait(t)
    t = DVE.op(DVE.e.tensor_reduce(out=lsum, in_=lbt, axis=AX.X, op=ALU.add))
    DVE.wait(t)
    t = DVE.op(DVE.e.reciprocal(out=lsum, in_=lsum))
    DVE.wait(t)
    t = DVE.op(DVE.e.tensor_tensor(out=lbt, in0=lbt, in1=lsum.to_broadcast([128, HHL, L]), op=ALU.mult))
    DVE.wait(t)
    t = DVE.op(DVE.e.memset(lbt[:, :, 0:1], 0.0))
    for l in range(2, L):
        DVE.wait(t)
        t = DVE.op(DVE.e.tensor_tensor(out=lbt[:, :, l:l + 1], in0=lbt[:, :, l:l + 1], in1=lbt[:, :, l - 1:l], op=ALU.add))
    DVE.wait(t)
    t = DVE.op(DVE.e.tensor_scalar(out=oml, in0=lbt, scalar1=-1.0, scalar2=1.0, op0=ALU.mult, op1=ALU.add))
    t_lb = t

    mk = kb.sb_mark()
    RC = min(SL, 2048)
    posi = kb.sb("posi", [64, RC], I32)
    ang = kb.sb("ang", [64, RC], F32)
    ang2 = kb.sb("ang2", [64, RC], F32)
    rsem = kb.dsem("rope")
    rst = kb.dsem("ropest")
    tl = None
    mk1 = kb.sb("mk1", [64, RC], F32)
    ki = kb.sb("ki", [64, RC], I32)
    C1 = 6.28125
    C2 = TWO_PI - 6.28125
    PI = float(np.pi)

    def dv(ins, *deps):
        DVE.wait(*deps)
        return DVE.op(ins())

    for r0 in range(0, SL, RC):
        tl = kb.dma(SP, posi, pos[0:1, r0:r0 + RC].partition_broadcast(64), [tl, (rst, rst.n), (DVE.sem, DVE.sem.n)], rsem)
        DVE.wait(tl, tconst)
        t = DVE.op(DVE.e.tensor_copy(out=ang, in_=posi))
        t = dv(lambda: DVE.e.tensor_scalar(out=ang, in0=ang, scalar1=invf[:, 0:1], scalar2=None, op0=ALU.mult), t)
        t = dv(lambda: DVE.e.tensor_scalar(out=ang2, in0=ang, scalar1=1.0 / TWO_PI, scalar2=None, op0=ALU.mult), t)
        t = dv(lambda: DVE.e.tensor_copy(out=ki, in_=ang2), t)
        t = dv(lambda: DVE.e.tensor_copy(out=ang2, in_=ki), t)
        t = dv(lambda: DVE.e.scalar_tensor_tensor(out=ang, in0=ang2, scalar=-C1, in1=ang, op0=ALU.mult, op1=ALU.add), t)
        t = dv(lambda: DVE.e.scalar_tensor_tensor(out=ang, in0=ang2, scalar=-C2, in1=ang, op0=ALU.mult, op1=ALU.add), t)
        t = dv(lambda: DVE.e.tensor_scalar(out=mk1, in0=ang, scalar1=PI, scalar2=None, op0=ALU.is_gt), t)
        t = dv(lambda: DVE.e.scalar_tensor_tensor(out=ang, in0=mk1, scalar=-TWO_PI, in1=ang, op0=ALU.mult, op1=ALU.add), t)
        t = dv(lambda: DVE.e.tensor_scalar(out=mk1, in0=ang, scalar1=-PI, scalar2=None, op0=ALU.is_lt), t)
        t = dv(lambda: DVE.e.scalar_tensor_tensor(out=ang, in0=mk1, scalar=TWO_PI, in1=ang, op0=ALU.mult, op1=ALU.add), t)
        t = dv(lambda: DVE.e.tensor_scalar(out=ang2, in0=ang, scalar1=PI / 2, scalar2=None, op0=ALU.add), t)
        t = dv(lambda: DVE.e.tensor_scalar(out=mk1, in0=ang2, scalar1=PI, scalar2=None, op0=ALU.is_gt), t)
        t = dv(lambda: DVE.e.scalar_tensor_tensor(out=ang2, in0=mk1, scalar=-TWO_PI, in1=ang2, op0=ALU.mult, op1=ALU.add), t)
        t = dv(lambda: DVE.e.tensor_scalar(out=ang, in0=ang, scalar1=-PI, scalar2=PI, op0=ALU.max, op1=ALU.min), t)
        t = dv(lambda: DVE.e.tensor_scalar(out=ang2, in0=ang2, scalar1=-PI, scalar2=PI, op0=ALU.max, op1=ALU.min), t)
        ACT.wait(t)
        t2 = ACT.op(ACT.e.activation(out=ang, in_=ang, func=AF.Sin))
        t2 = ACT.op(ACT.e.activation(out=ang2, in_=ang2, func=AF.Sin))
        t3 = dv(lambda: DVE.e.tensor_scalar(out=ang[0:32], in0=ang[0:32], scalar1=-1.0, scalar2=None, op0=ALU.mult), t2)
        kb.dma(POOL, SIN[:, r0:r0 + RC], ang, [t3], rst)
        kb.dma(POOL, COS[:, r0:r0 + RC], ang2, [t2], rst)
    kb.pair_barrier()
    kb.sb_release(mk)

    def dvo(fn, *deps):
        DVE.wait(*deps)
        return DVE.op(fn())

    def aco(fn, *deps):
        ACT.wait(*deps)
        return ACT.op(fn())

    def poo(fn, *deps):
        POOL.wait(*deps)
        return POOL.op(fn())

    def sweep(l):
        mk_ = kb.sb_mark()
        h = Slot(kb.sb("h", [128, 4, D], F32), kb.dsem())
        xs = Slot(kb.sb("xs", [128, 4, D], bf))
        actT = Slot(kb.sb("actT", [128, 16, TT], bf), kb.dsem())
        junk = kb.sb("junk", [128, D], bf)
        ss = kb.sb("ss", [128, 4], F32)
        rstd = kb.sb("rstd", [128, 4], F32)
        wring = Ring([Slot(kb.sb(f"w{i}", [128, 16, 512], bf), kb.dsem()) for i in range(4)])
        stg = Ring([Slot(kb.sb(f"stg{i}", [128, 4, TT], bf), kb.dsem()) for i in range(2)])
        hst = kb.dsem()
        if l > 0:
            ffT = Slot(kb.sb("ffT", [128, 44, TT], bf))
            sg = Ring([Slot(kb.sb(f"sg{i}", [128, TT], F32)) for i in range(2)])
        if l == L:
            finw = kb.sb("finw", [128, D], F32)
            tfin = kb.dma(SP, finw, finw_d.partition_broadcast(128), [], kb.dsem())
            ostg = Slot(kb.sb("ostg", [128, D], F32), kb.dsem())
        evk = [0]

        def evac_copy(out_ap, in_ap, deps, scale=None):
            evk[0] += 1
            if evk[0] % 2 == 0:
                if scale is None:
                    return aco(lambda: ACT.e.activation(out=out_ap, in_=in_ap, func=AF.Copy), *deps)
                return aco(lambda: ACT.e.activation(out=out_ap, in_=in_ap, func=AF.Copy, scale=scale), *deps)
            if scale is None:
                return dvo(lambda: DVE.e.tensor_copy(out=out_ap, in_=in_ap), *deps)
            return dvo(lambda: DVE.e.tensor_scalar(out=out_ap, in0=in_ap, scalar1=scale, scalar2=None, op0=ALU.mult), *deps)

        def load_w(src3, k, ncols):
            sl = wring.next()
            t = kb.dma(SP, sl.ap[:, 0:k, 0:ncols], src3, sl.deps_for_write(), sl.sem)
            sl.wrote(t)
            return sl

        def norm_to_actT(nw_col0, nw):
            t = dvo(lambda: DVE.e.memset(ss, 0.0), h.w)
            for tb in range(4):
                t = aco(lambda: ACT.e.activation(out=junk, in_=h.ap[:, tb, :], func=AF.Square, accum_out=ss[:, tb:tb + 1]), t, h.w)
            t = dvo(lambda: DVE.e.tensor_scalar(out=rstd, in0=ss, scalar1=1.0 / D, scalar2=RMS_EPS, op0=ALU.mult, op1=ALU.add), t)
            t = aco(lambda: ACT.e.activation(out=rstd, in_=rstd, func=AF.Sqrt), t)
            t = dvo(lambda: DVE.e.reciprocal(out=rstd, in_=rstd), t)
            txs = None
            for tb in range(4):
                txs = dvo(lambda: DVE.e.tensor_scalar(out=xs.ap[:, tb, :], in0=h.ap[:, tb, :], scalar1=rstd[:, tb:tb + 1], scalar2=None, op0=ALU.mult), t, *xs.deps_for_write())
            xs.wrote(txs)
            tl = None
            for c in range(16):
                bank = kb.pring.next()
                PE.wait(txs, tconst, *bank.deps_for_write())
                pv = bank.ap.bitcast(bf)
                for tb in range(4):
                    ins = PE.e.transpose(out=pv[:, tb * 128:(tb + 1) * 128], in_=xs.ap[:, tb, c * 128:(c + 1) * 128], identity=ident)
                tp = PE.op(ins)
                bank.wrote(tp)
                te = evac_copy(actT.ap[:, c, :], pv[:, 0:TT], [tp] + (actT.deps_for_write() if c == 0 else []), scale=nw[:, nw_col0 + c:nw_col0 + c + 1])
                bank.read(te)
                tl = te
            xs.read(tp)
            actT.wrote(None)
            actT.w = [(ACT.sem, ACT.sem.n), (DVE.sem, DVE.sem.n)]
            return actT.w

        for j in range(NT):
            tok = slice(j * TT, (j + 1) * TT)
            gtok = lambda r: rsl(r, j * TT, TT, SL)
            src = x if l <= 1 else H
            th = kb.dma(SP, h.ap, src[tok, :].rearrange("(tb p) d -> p tb d", p=128), h.deps_for_write() + [(hst, hst.n)], h.sem)
            h.wrote(th)
            if l > 0:
                lw = l - 1
                tm = kb.dmar(SP, lambda r: actT.ap, lambda r: MIXT[:, gtok(r)].rearrange("(c p) t -> p c t", p=128), actT.deps_for_write(), actT.sem)
                actT.wrote(tm)
                tacc = None
                for cg in range(4):
                    sl = load_w(WBout[lw][:, cg * 512:(cg + 1) * 512].rearrange("(c p) n -> p c n", p=128), 16, 512)
                    for tb in range(4):
                        bank = kb.pring.next()
                        tp = kb.mm_group(bank, [(actT.ap[:, c, tb * 128:(tb + 1) * 128], sl.ap[:, c, :]) for c in range(16)], [sl.w, tm])
                        hv = h.ap[:, tb, cg * 512:(cg + 1) * 512]
                        tacc = dvo(lambda: DVE.e.tensor_tensor(out=hv, in0=hv, in1=bank.ap, op=ALU.add), tp, th)
                        bank.read(tacc)
                    sl.read(tp)
                actT.read(tp)
                h.wrote(tacc)
                ta = norm_to_actT(lw * 16, fnw)
                tff = None
                for g in range(11):
                    slg = load_w(WBg[lw][:, g * 512:(g + 1) * 512].rearrange("(c p) n -> p c n", p=128), 16, 512)
                    slu = load_w(WBu[lw][:, g * 512:(g + 1) * 512].rearrange("(c p) n -> p c n", p=128), 16, 512)
                    for q in range(4):
                        bg = kb.pring.next()
                        tg = kb.mm_group(bg, [(slg.ap[:, c, q * 128:(q + 1) * 128], actT.ap[:, c, :]) for c in range(16)], [slg.w] + ta)
                        bu = kb.pring.next()
                        tu = kb.mm_group(bu, [(slu.ap[:, c, q * 128:(q + 1) * 128], actT.ap[:, c, :]) for c in range(16)], [slu.w] + ta)
                        sgs = sg.next()
                        ts_ = aco(lambda: ACT.e.activation(out=sgs.ap, in_=bg.ap, func=AF.Silu), tg, *sgs.deps_for_write())
                        sgs.wrote(ts_)
                        bg.read(ts_)
                        tff = dvo(lambda: DVE.e.tensor_tensor(out=ffT.ap[:, g * 4 + q, :], in0=sgs.ap, in1=bu.ap, op=ALU.mult), ts_, tu, *(ffT.deps_for_write() if (g == 0 and q == 0) else []))
                        sgs.read(tff)
                        bu.read(tff)
                    slg.read(tg)
                    slu.read(tu)
                actT.read(tu)
                ffT.wrote(tff)
                tacc = None
                for cg in range(4):
                    sls = []
                    for (c0, k) in [(0, 16), (16, 16), (32, 12)]:
                        sls.append((c0, k, load_w(WBd[lw][c0 * 128:(c0 + k) * 128, cg * 512:(cg + 1) * 512].rearrange("(c p) n -> p c n", p=128), k, 512)))
                    for tb in range(4):
                        bank = kb.pring.next()
                        pairs = []
                        for (c0, k, sl) in sls:
                            pairs += [(ffT.ap[:, c0 + c, tb * 128:(tb + 1) * 128], sl.ap[:, c, :]) for c in range(k)]
                        tp = kb.mm_group(bank, pairs, [sl.w for (_, _, sl) in sls] + [tff])
                        hv = h.ap[:, tb, cg * 512:(cg + 1) * 512]
                        tacc = dvo(lambda: DVE.e.tensor_tensor(out=hv, in0=hv, in1=bank.ap, op=ALU.add), tp, h.w)
                        bank.read(tacc)
                    for (_, _, sl) in sls:
                        sl.read(tp)
                ffT.read(tp)
                h.wrote(tacc)
            if l == L:
                t = dvo(lambda: DVE.e.memset(ss, 0.0), h.w)
                for tb in range(4):
                    t = aco(lambda: ACT.e.activation(out=junk, in_=h.ap[:, tb, :], func=AF.Square, accum_out=ss[:, tb:tb + 1]), t, h.w)
                t = dvo(lambda: DVE.e.tensor_scalar(out=rstd, in0=ss, scalar1=1.0 / D, scalar2=RMS_EPS, op0=ALU.mult, op1=ALU.add), t)
                t = aco(lambda: ACT.e.activation(out=rstd, in_=rstd, func=AF.Sqrt), t)
                t = dvo(lambda: DVE.e.reciprocal(out=rstd, in_=rstd), t)
                for tb in range(4):
                    to = dvo(lambda: DVE.e.scalar_tensor_tensor(out=ostg.ap, in0=h.ap[:, tb, :], scalar=rstd[:, tb:tb + 1], in1=finw, op0=ALU.mult, op1=ALU.mult), t, tfin, *ostg.deps_for_write())
                    tst = kb.dma(POOL, out[j * TT + tb * 128: j * TT + (tb + 1) * 128, :], ostg.ap, [to], ostg.sem)
                    ostg.wrote(to)
                    ostg.read(tst)
                h.read(to)
                continue
            if l > 0:
                tst = kb.dma(POOL, H[tok, :].rearrange("(tb p) d -> p tb d", p=128), h.ap, [h.w], hst)
            ta = norm_to_actT(l * 16, anw)
            h.read(ta[0]); h.read(ta[1])
            for sidx in range(9):
                ncols = 512 if sidx < 8 else 128
                sl = load_w(WBin[l][:, sidx * 512: sidx * 512 + ncols].rearrange("(c p) n -> p c n", p=128), 16, ncols)
                st_ = stg.next()
                te = None
                nq = ncols // 128
                for q in range(nq):
                    bank = kb.pring.next()
                    tp = kb.mm_group(bank, [(sl.ap[:, c, q * 128:(q + 1) * 128], actT.ap[:, c, :]) for c in range(16)], [sl.w] + ta)
                    te = evac_copy(st_.ap[:, q, :], bank.ap, [tp] + (st_.deps_for_write() if q == 0 else []))
                    bank.read(te)
                sl.read(tp)
                st_.wrote(None)
                tst = kb.dmar(POOL, lambda r: PROJT[sidx * 512: sidx * 512 + ncols, gtok(r)].rearrange("(q p) t -> p q t", p=128), lambda r: st_.ap[:, 0:nq, :],
                             [(ACT.sem, ACT.sem.n), (DVE.sem, DVE.sem.n)], st_.sem)
                st_.read(tst)
            actT.read(tp)
        kb.barrier()
        kb.sb_release(mk_)

    def mla(l):
        mk_ = kb.sb_mark()
        wuq = kb.sb("wuq", [128, 4, 2048], bf)
        wukv = kb.sb("wukv", [128, 4, 2048], bf)
        wsm = kb.dsem()
        kb.dma(SP, wuq, WBuq[l].rearrange("(c p) n -> p c n", p=128), [], wsm)
        tw = kb.dma(SP, wukv, WBukv[l].rearrange("(c p) n -> p c n", p=128), [], wsm)
        cq = Slot(kb.sb("cq", [128, 4, TT], bf), kb.dsem())
        ckv = Slot(kb.sb("ckv", [128, 4, TT], bf), kb.dsem())
        kr = Slot(kb.sb("kr", [64, 2, TT], bf), kb.dsem())
        cst = Slot(kb.sb("cst", [64, 2, TT], F32), kb.dsem())
        sq = kb.sb("sq", [128, 4, TT], bf)
        rbc = kb.sb("rbc", [128, TT], F32)
        cqn = Slot(kb.sb("cqn", [128, 4, TT], bf))
        ckvn = Slot(kb.sb("ckvn", [128, 4, TT], bf))
        qn_st = Slot(kb.sb("qn_st", [128, NH, TT], bf), kb.dsem())
        kn_st = Slot(kb.sb("kn_st", [128, NH, TT], bf), kb.dsem())
        qp_st = Slot(kb.sb("qp_st", [64, NH, TT], bf), kb.dsem())
        kp_st = Slot(kb.sb("kp_st", [64, TT], bf), kb.dsem())
        v_st = Slot(kb.sb("v_st", [128, 4, NH, 129], bf), kb.dsem())
        r1 = kb.sb("r1", [64, TT], F32)
        r2 = kb.sb("r2", [64, TT], F32)
        t1s = dvo(lambda: DVE.e.memset(v_st.ap[:, :, :, 128:129], 1.0))
        COSv = cst.ap[:, 0, :]
        SINv = cst.ap[:, 1, :]
        wq3 = wuq.rearrange("p c (h e) -> p c h e", e=256)
        wkv3 = wukv.rearrange("p c (h e) -> p c h e", e=256)

        def normed(src, dst, nw, col0):
            t = dvo(lambda: DVE.e.tensor_tensor(out=sq, in0=src.ap, in1=src.ap, op=ALU.mult), src.w)
            bank = kb.pring.next()
            tp = kb.mm_group(bank, [(ones_bf, sq[:, c, :]) for c in range(4)], [t])
            t = dvo(lambda: DVE.e.tensor_scalar(out=rbc, in0=bank.ap, scalar1=1.0 / 512, scalar2=RMS_EPS, op0=ALU.mult, op1=ALU.add), tp)
            bank.read(t)
            t = aco(lambda: ACT.e.activation(out=rbc, in_=rbc, func=AF.Sqrt), t)
            t = dvo(lambda: DVE.e.reciprocal(out=rbc, in_=rbc), t)
            for c in range(4):
                t2 = dvo(lambda: DVE.e.scalar_tensor_tensor(out=dst.ap[:, c, :], in0=src.ap[:, c, :], scalar=nw[:, col0 + c:col0 + c + 1], in1=rbc, op0=ALU.mult, op1=ALU.mult), t, *(dst.deps_for_write() if c == 0 else []))
            dst.wrote(t2)
            src.read(t2)
            return t2

        for j in range(NT):
            tok = slice(j * TT, (j + 1) * TT)
            gtok = lambda r: rsl(r, j * TT, TT, SL)
            t = kb.dmar(SP, lambda r: cq.ap, lambda r: PROJT[R_CQ:R_CQ + 512, gtok(r)].rearrange("(c p) t -> p c t", p=128), cq.deps_for_write(), cq.sem); cq.wrote(t)
            t = kb.dmar(SP, lambda r: ckv.ap, lambda r: PROJT[R_CKV:R_CKV + 512, gtok(r)].rearrange("(c p) t -> p c t", p=128), ckv.deps_for_write(), ckv.sem); ckv.wrote(t)
            t = kb.dmar(SP, lambda r: kr.ap, lambda r: PROJT[R_KR:R_KR + 128, gtok(r)].rearrange("(c p) t -> p c t", p=64), kr.deps_for_write(), kr.sem); kr.wrote(t)
            kb.dma(SP, cst.ap[:, 0, :], COS[:, tok], cst.deps_for_write(), cst.sem)
            t = kb.dma(SP, cst.ap[:, 1, :], SIN[:, tok], [], cst.sem); cst.wrote(t)
            tq = normed(cq, cqn, qnw, l * 4)
            tk = normed(ckv, ckvn, kvnw, l * 4)
            t = dvo(lambda: DVE.e.tensor_tensor(out=r1, in0=kr.ap[:, 0, :], in1=COSv, op=ALU.mult), kr.w, cst.w)
            t = dvo(lambda: DVE.e.tensor_tensor(out=r2, in0=kr.ap[:, 1, :], in1=SINv, op=ALU.mult), t)
            t = dvo(lambda: DVE.e.tensor_tensor(out=kp_st.ap, in0=r1, in1=r2, op=ALU.add), t, *kp_st.deps_for_write())
            kr.read(t)
            kp_st.wrote(t)
            kp_st.read(kb.dmar(POOL, lambda r: KP[:, gtok(r)], lambda r: kp_st.ap, [t], kp_st.sem))
            for hh in range(NH):
                bank = kb.pring.next()
                tp = kb.mm_group(bank, [(wq3[:, c, hh, 0:128], cqn.ap[:, c, :]) for c in range(4)], [tq, tw])
                te = aco(lambda: ACT.e.activation(out=qn_st.ap[:, hh, :], in_=bank.ap, func=AF.Copy), tp, *(qn_st.deps_for_write() if hh == 0 else []))
                bank.read(te)
                b1 = kb.pring.next()
                tp1 = kb.mm_group(b1, [(wq3[:, c, hh, 128:192], cqn.ap[:, c, :]) for c in range(4)], [tq], out_ap=b1.ap[0:64, :])
                b2 = kb.pring.next()
                tp2 = kb.mm_group(b2, [(wq3[:, c, hh, 192:256], cqn.ap[:, c, :]) for c in range(4)], [tq], out_ap=b2.ap[0:64, :])
                t = dvo(lambda: DVE.e.tensor_tensor(out=r1, in0=b1.ap[0:64, :], in1=COSv, op=ALU.mult), tp1, cst.w)
                b1.read(t)
                t = dvo(lambda: DVE.e.tensor_tensor(out=r2, in0=b2.ap[0:64, :], in1=SINv, op=ALU.mult), tp2, t)
                b2.read(t)
                t = dvo(lambda: DVE.e.tensor_tensor(out=qp_st.ap[:, hh, :], in0=r1, in1=r2, op=ALU.add), t, *(qp_st.deps_for_write() if hh == 0 else []))
                bank = kb.pring.next()
                tp = kb.mm_group(bank, [(wkv3[:, c, hh, 0:128], ckvn.ap[:, c, :]) for c in range(4)], [tk, tw])
                te = aco(lambda: ACT.e.activation(out=kn_st.ap[:, hh, :], in_=bank.ap, func=AF.Copy), tp, *(kn_st.deps_for_write() if hh == 0 else []))
                bank.read(te)
            cst.read(t)
            tA = (ACT.sem, ACT.sem.n)
            tD = (DVE.sem, DVE.sem.n)
            qn_st.wrote(tA); kn_st.wrote(tA); qp_st.wrote(tD)
            qn_st.read(kb.dmar(POOL, lambda r: QN[:, :, gtok(r)].rearrange("h p t -> p h t"), lambda r: qn_st.ap, [tA], qn_st.sem))
            kn_st.read(kb.dmar(POOL, lambda r: KN[:, :, gtok(r)].rearrange("h p t -> p h t"), lambda r: kn_st.ap, [tA], kn_st.sem))
            qp_st.read(kb.dmar(POOL, lambda r: QP[:, :, gtok(r)].rearrange("h p t -> p h t"), lambda r: qp_st.ap, [tD], qp_st.sem))
            for tb in range(4):
                for hg in range(2):
                    bank = kb.pring.next()
                    tp = kb.mm_group(bank, [(ckvn.ap[:, c, tb * 128:(tb + 1) * 128], wkv3[:, c, hg * 4:(hg + 1) * 4, 128:256]) for c in range(4)], [tk, tw])
                    te = aco(lambda: ACT.e.activation(out=v_st.ap[:, tb, hg * 4:(hg + 1) * 4, 0:128], in_=bank.ap.rearrange("p (h e) -> p h e", e=128), func=AF.Copy),
                             tp, t1s, *(v_st.deps_for_write() if (tb == 0 and hg == 0) else []))
                    bank.read(te)
            cqn.read(tp); ckvn.read(tp)
            v_st.wrote(te)
            v_st.read(kb.dmar(POOL, lambda r: VS[rsl(r, j * 4, 4, SL // 128)].rearrange("b p h e -> p b h e"), lambda r: v_st.ap, [te], v_st.sem))
        kb.pair_barrier()
        kb.sb_release(mk_)

        mk_ = kb.sb_mark()
        NQ = S // 512
        kp = kb.sb("kp", [64, S], bf)
        kps = kb.dsem()
        tkp = kb.dma(SP, kp, KP, [], kps)
        hb = [dict(kn=Slot(kb.sb(f"kn{i}", [128, S], bf), kb.dsem()), qn=Slot(kb.sb(f"qnb{i}", [128, S], bf)),
                   qp=Slot(kb.sb(f"qpb{i}", [64, S], bf)), v=Slot(kb.sb(f"v{i}", [128, NB, 129], bf))) for i in range(2)]
        cm = kb.sb("cm", [128, 4, 512], bf)
        tcm = kb.dma(POOL, cm.rearrange("p r f -> p (r f)"), cmask_d, [], kps)
        ptr = Ring([Slot(kb.sb(f"pt{i}", [128, 512], bf)) for i in range(3)])
        o32 = Slot(kb.sb("o32", [128, 4, 129], F32))
        rec = kb.sb("rec", [128, 4], F32)
        osq = kb.sb("osq", [128, 4, 128], F32)
        oss = kb.sb("oss", [128, 4], F32)
        onb = Slot(kb.sb("onb", [128, 4, 128], bf))
        ya = Ring([Slot(kb.sb(f"ya{i}", [128, 512], bf), kb.dsem()) for i in range(2)])
        stb = Ring(kb.banks[0:4])
        oacc = Ring([kb.banks[4:6], kb.banks[6:8]])
        scale = float((128 + 64) ** -0.5)
        for hh in range(NHL):
            B = hb[hh % 2]
            hg_ = lambda r: r * NHL + hh
            dw = B["kn"].deps_for_write() + B["qn"].r + B["qp"].r + B["v"].r
            kb.dmar(SP, lambda r: B["kn"].ap, lambda r: KN[hg_(r)], dw, B["kn"].sem)
            kb.dmar(SP, lambda r: B["qn"].ap, lambda r: QN[hg_(r)], [], B["kn"].sem)
            kb.dmar(SP, lambda r: B["qp"].ap, lambda r: QP[hg_(r)], [], B["kn"].sem)
            tl = kb.dmar(SP, lambda r: B["v"].ap, lambda r: VS[:, :, hg_(r), :].rearrange("b p e -> p b e"), [], B["kn"].sem)
            for k_ in ("kn", "qn", "qp", "v"):
                B[k_].wrote(tl)
            knS, qnS, qpS, vS = B["kn"].ap, B["qn"].ap, B["qp"].ap, B["v"].ap
            for i in range(NQ):
                qs = slice(i * 512, (i + 1) * 512)
                nkb = 4 * i + 4
                ob = oacc.next()
                ov = [ob[0].ap[:, 0:258].rearrange("p (s e) -> p s e", e=129), ob[1].ap[:, 0:258].rearrange("p (s e) -> p s e", e=129)]

                def qk(kbk):
                    bank = stb.next()
                    ks = slice(kbk * 128, (kbk + 1) * 128)
                    return bank, kb.mm_group(bank, [(knS[:, ks], qnS[:, qs]), (kp[:, ks], qpS[:, qs])], [tl, tkp])

                nxt = qk(0)
                tpv = None
                for kbk in range(nkb):
                    bank, tqk = nxt
                    if kbk + 1 < nkb:
                        nxt = qk(kbk + 1)
                    pt = ptr.next()
                    te = aco(lambda: ACT.e.activation(out=pt.ap, in_=bank.ap, func=AF.Exp, scale=scale), tqk, *pt.deps_for_write())
                    bank.read(te)
                    r = kbk - 4 * i
                    if r >= 0:
                        te = poo(lambda: POOL.e.tensor_tensor(out=pt.ap, in0=pt.ap, in1=cm[:, r, :], op=ALU.mult), te, tcm)
                    pt.wrote(te)
                    PE.wait(te, *(ob[0].deps_for_write() + ob[1].deps_for_write() if kbk == 0 else []))
                    for s_ in range(4):
                        last = 4 * i + s_
                        if kbk > last:
                            continue
                        ins = PE.e.matmul(ov[s_ // 2][:, s_ % 2, :], lhsT=pt.ap[:, s_ * 128:(s_ + 1) * 128], rhs=vS[:, kbk, :], start=(kbk == 0 and s_ % 2 == 0), stop=(kbk == last))
                    tpv = PE.op(ins)
                    pt.read(tpv)
                ob[0].wrote(tpv); ob[1].wrote(tpv)
                t = dvo(lambda: DVE.e.tensor_copy(out=o32.ap[:, 0:2, :], in_=ov[0]), tpv, *o32.deps_for_write())
                t = dvo(lambda: DVE.e.tensor_copy(out=o32.ap[:, 2:4, :], in_=ov[1]), t)
                ob[0].read(t); ob[1].read(t)
                t = dvo(lambda: DVE.e.reciprocal(out=rec, in_=o32.ap[:, :, 128]), t)
                t = dvo(lambda: DVE.e.tensor_tensor(out=o32.ap[:, :, 0:128], in0=o32.ap[:, :, 0:128], in1=rec.to_broadcast([128, 4, 128]), op=ALU.mult), t)
                t = dvo(lambda: DVE.e.tensor_tensor(out=osq, in0=o32.ap[:, :, 0:128], in1=o32.ap[:, :, 0:128], op=ALU.mult), t)
                t = dvo(lambda: DVE.e.tensor_reduce(out=oss, in_=osq, axis=AX.X, op=ALU.add), t)
                t = dvo(lambda: DVE.e.tensor_scalar(out=oss, in0=oss, scalar1=1.0 / 128, scalar2=RMS_EPS, op0=ALU.mult, op1=ALU.add), t)
                t = aco(lambda: ACT.e.activation(out=oss, in_=oss, func=AF.Sqrt), t)
                t = dvo(lambda: DVE.e.reciprocal(out=oss, in_=oss), t)
                t = dvo(lambda: DVE.e.tensor_tensor(out=onb.ap, in0=o32.ap[:, :, 0:128], in1=oss.to_broadcast([128, 4, 128]), op=ALU.mult), t, *onb.deps_for_write())
                onb.wrote(t)
                o32.wrote(t)
                bank = stb.next()
                PE.wait(t, tconst, *bank.deps_for_write())
                pv = bank.ap.bitcast(bf)
                for s_ in range(4):
                    ins = PE.e.transpose(out=pv[:, s_ * 128:(s_ + 1) * 128], in_=onb.ap[:, s_, :], identity=ident)
                tp = PE.op(ins)
                bank.wrote(tp)
                onb.read(tp)
                yb = ya.next()
                te = aco(lambda: ACT.e.activation(out=yb.ap, in_=pv[:, 0:512], func=AF.Copy, scale=monw[:, l:l + 1]), tp, *yb.deps_for_write())
                bank.read(te)
                yb.wrote(te)
                yb.read(kb.dmar(POOL, lambda r: MIXT[hg_(r) * 128:(hg_(r) + 1) * 128, qs], lambda r: yb.ap, [te], yb.sem))
            for k_ in ("kn", "qn", "qp", "v"):
                B[k_].read(tpv)
        kb.barrier()
        kb.sb_release(mk_)

    kb.mla = mla
    def conv(l):
        mk_ = kb.sb_mark()
        PADL = 30
        G = kb.sb("G", [128, 4, PADL + SL], bf)
        cwt = kb.sb("cwt", [128, 4, 31], F32)
        Dg = kb.sb("Dg", [128, 4, 31, 128], bf)
        csm = kb.dsem()
        tcw = kb.dma(SP, cwt.rearrange("p c j -> p (c j)"), cw_d[:, l * 124:(l + 1) * 124], [], csm)
        t0 = dvo(lambda: DVE.e.memset(G[:, :, 0:PADL], 0.0))
        tD = None
        for c in range(4):
            for jj in range(31):
                tD = dvo(lambda: DVE.e.tensor_scalar(out=Dg[:, c, jj, :], in0=ident, scalar1=cwt[:, c, jj:jj + 1], scalar2=None, op0=ALU.mult), tcw, tconst)
        av = Slot(kb.sb("av", [128, 4, TT], bf), kb.dsem())
        gv = Slot(kb.sb("gv", [128, 4, TT], bf), kb.dsem())
        sgm = kb.sb("sgm", [128, 4, TT], F32)
        hsl_ = lambda r: rsl(r, 0, PADL, SL - PADL)
        t = kb.dmar(SP, lambda r: av.ap[:, :, 0:PADL], lambda r: PROJT[R_CONV:R_CONV + 512, hsl_(r)].rearrange("(c p) t -> p c t", p=128), [], av.sem)
        t = kb.dmar(SP, lambda r: gv.ap[:, :, 0:PADL], lambda r: PROJT[R_CONV + 512:R_CONV + 1024, hsl_(r)].rearrange("(c p) t -> p c t", p=128), [], av.sem)
        av.wrote(t); gv.wrote(t)
        ts_ = aco(lambda: ACT.e.activation(out=sgm[:, :, 0:PADL], in_=gv.ap[:, :, 0:PADL], func=AF.Sigmoid), t)
        t0 = dvo(lambda: DVE.e.tensor_tensor(out=sgm[:, :, 0:PADL], in0=sgm[:, :, 0:PADL], in1=av.ap[:, :, 0:PADL], op=ALU.mult), ts_, t0)
        t0 = dvo(lambda: DVE.e.tensor_scalar(out=G[:, :, 0:PADL], in0=sgm[:, :, 0:PADL], scalar1=rankf[:, 0:1], scalar2=None, op0=ALU.mult), t0, tconst)
        av.read(t0); gv.read(t0)
        tG = t0
        for j in range(NT):
            tok = slice(j * TT, (j + 1) * TT)
            gtok = lambda r: rsl(r, j * TT, TT, SL)
            t = kb.dmar(SP, lambda r: av.ap, lambda r: PROJT[R_CONV:R_CONV + 512, gtok(r)].rearrange("(c p) t -> p c t", p=128), av.deps_for_write(), av.sem); av.wrote(t)
            t = kb.dmar(SP, lambda r: gv.ap, lambda r: PROJT[R_CONV + 512:R_CONV + 1024, gtok(r)].rearrange("(c p) t -> p c t", p=128), gv.deps_for_write(), gv.sem); gv.wrote(t)
            ts_ = aco(lambda: ACT.e.activation(out=sgm, in_=gv.ap, func=AF.Sigmoid), gv.w, tG)
            gv.read(ts_)
            tG = dvo(lambda: DVE.e.tensor_tensor(out=G[:, :, PADL + j * TT: PADL + (j + 1) * TT], in0=av.ap, in1=sgm, op=ALU.mult), ts_, av.w, t0)
            av.read(tG)
        xc = Slot(kb.sb("xc", [128, 4, TT], F32))
        xq = Slot(kb.sb("xq", [128, 4, TT], F32))
        mean = kb.sb("mean", [128, TT], F32)
        var = kb.sb("var", [128, TT], F32)
        yb = Ring([Slot(kb.sb(f"yb{i}", [128, 4, TT], bf), kb.dsem()) for i in range(2)])
        for j in range(NT):
            tok = slice(j * TT, (j + 1) * TT)
            gtok = lambda r: rsl(r, j * TT, TT, SL)
            te = None
            for c in range(4):
                bank = kb.pring.next()
                tp = kb.mm_group(bank, [(Dg[:, c, jj, :], G[:, c, j * TT + jj: j * TT + jj + TT]) for jj in range(31)], [tG, tD])
                te = aco(lambda: ACT.e.activation(out=xc.ap[:, c, :], in_=bank.ap, func=AF.Identity, bias=cb[:, l * 4 + c:l * 4 + c + 1], scale=1.0), tp, tconst, *(xc.deps_for_write() if c == 0 else []))
                bank.read(te)
            xc.wrote(te)
            tq = dvo(lambda: DVE.e.tensor_tensor(out=xq.ap, in0=xc.ap, in1=xc.ap, op=ALU.mult), te, *xq.deps_for_write())
            xq.wrote(tq)
            bm = kb.pring.next()
            tpm = kb.mm_group(bm, [(ones32, xc.ap[:, c, :]) for c in range(4)], [te])
            bv = kb.pring.next()
            tpv = kb.mm_group(bv, [(ones32, xq.ap[:, c, :]) for c in range(4)], [tq])
            xq.read(tpv)
            t = dvo(lambda: DVE.e.tensor_scalar(out=mean, in0=bm.ap, scalar1=1.0 / 512, scalar2=None, op0=ALU.mult), tpm)
            bm.read(t)
            t = dvo(lambda: DVE.e.tensor_scalar(out=var, in0=bv.ap, scalar1=1.0 / 512, scalar2=LN_EPS, op0=ALU.mult, op1=ALU.add), tpv, t)
            bv.read(t)
            t = dvo(lambda: DVE.e.tensor_tensor(out=xq.ap[:, 0, :], in0=mean, in1=mean, op=ALU.mult), t)
            t = dvo(lambda: DVE.e.tensor_tensor(out=var, in0=var, in1=xq.ap[:, 0, :], op=ALU.subtract), t)
            t = aco(lambda: ACT.e.activation(out=var, in_=var, func=AF.Sqrt), t)
            t = dvo(lambda: DVE.e.reciprocal(out=var, in_=var), t)
            t = dvo(lambda: DVE.e.tensor_tensor(out=xc.ap, in0=xc.ap, in1=mean.unsqueeze(1).to_broadcast([128, 4, TT]), op=ALU.subtract), t)
            t = dvo(lambda: DVE.e.tensor_tensor(out=xc.ap, in0=xc.ap, in1=var.unsqueeze(1).to_broadcast([128, 4, TT]), op=ALU.mult), t)
            y = yb.next()
            for c in range(4):
                ta = aco(lambda: ACT.e.activation(out=y.ap[:, c, :], in_=xc.ap[:, c, :], func=AF.Silu, bias=lnb[:, l * 4 + c:l * 4 + c + 1], scale=lnw[:, l * 4 + c:l * 4 + c + 1]), t, *(y.deps_for_write() if c == 0 else []))
            xc.read(ta)
            y.wrote(ta)
            y.read(kb.dmar(POOL, lambda r: MIXT[1024:1536, gtok(r)].rearrange("(c p) t -> p c t", p=128), lambda r: y.ap, [ta], y.sem))
        kb.barrier()
        kb.sb_release(mk_)

    kb.conv = conv
    def hgrn(l):
        mk_ = kb.sb_mark()
        SEG = min(S, 2048)
        NSEG = S // SEG
        NCH = SEG // 64
        inb = {k_: Slot(kb.sb("in_" + k_, [128, SEG], bf), kb.dsem()) for k_ in ("q", "f", "i", "g")}
        F0 = kb.sb("F0", [128, SEG], F32)
        K32 = kb.sb("K32", [128, SEG], F32)
        B0 = kb.sb("B0", [128, SEG], F32)
        B1 = kb.sb("B1", [128, SEG], F32)
        QF = kb.sb("QF", [128, SEG], F32)
        E = kb.sb("E", [128, SEG], F32)
        QT = kb.sb("QT", [128, SEG], bf)
        KT = kb.sb("KT", [128, SEG], bf)
        QH = kb.sb("QH", [128, SEG], bf)
        KH = kb.sb("KH", [128, SEG], bf)
        EBL = kb.sb("EBL", [128, NCH], F32)
        vtok = kb.sb("vtok", [64, NCH, 128], bf)
        khtok = kb.sb("khtok", [64, NCH, 128], bf)
        Ot = kb.sb("Ot", [64, NCH, 128], F32)
        osq = kb.sb("osq", [64, NCH, 128], F32)
        oss = kb.sb("oss", [64, NCH], F32)
        On = kb.sb("On", [64, NCH, 128], bf)
        YC = Slot(kb.sb("YC", [128, SEG], bf), kb.dsem())
        S32 = kb.sb("S32", [128, 128], F32)
        Sbf = [kb.sb(f"Sbf{i}", [128, 128], bf) for i in range(2)]
        scm = Ring([Slot(kb.sb(f"scm{i}", [64, 64], bf)) for i in range(2)])
        m64b = kb.sb("m64b", [64, 64], F32)
        v3 = lambda ap: ap.rearrange("p (c k) -> p c k", k=64)
        tlast = None
        for hh in range(HHL):
            t = dvo(lambda: DVE.e.memset(S32, 0.0), tlast)
            tS = dvo(lambda: DVE.e.memset(Sbf[0], 0.0), t)
            cur = 0
            for sg_ in range(NSEG):
                tok = slice(sg_ * SEG, (sg_ + 1) * SEG)
                tin = None
                for k_, r0 in (("q", R_HQ), ("f", R_HF), ("i", R_HI), ("g", R_HG)):
                    tin = kb.dmar(SP, lambda r: inb[k_].ap, lambda r: PROJT[rsl(r, r0 + hh * 128, 128, HHL * 128), tok], [tlast, (PE.sem, PE.sem.n)], inb["q"].sem)
                t = aco(lambda: ACT.e.activation(out=F0, in_=inb["f"].ap, func=AF.Sigmoid), tin, tlast)
                t = dvo(lambda: DVE.e.tensor_scalar(out=F0, in0=F0, scalar1=oml[:, hh, l:l + 1], scalar2=lbt[:, hh, l:l + 1], op0=ALU.mult, op1=ALU.add), t, t_lb)
                tk = dvo(lambda: DVE.e.tensor_scalar(out=K32, in0=F0, scalar1=-1.0, scalar2=1.0, op0=ALU.mult, op1=ALU.add), t)
                t = aco(lambda: ACT.e.activation(out=B0, in_=F0, func=AF.Ln), t)
                tq = aco(lambda: ACT.e.activation(out=QF, in_=inb["q"].ap, func=AF.Silu), tin)
                src, dst = B0, B1
                for d in (1, 2, 4, 8, 16, 32):
                    t1 = dvo(lambda: DVE.e.tensor_tensor(out=v3(dst)[:, :, d:64], in0=v3(src)[:, :, d:64], in1=v3(src)[:, :, 0:64 - d], op=ALU.add), t)
                    t = dvo(lambda: DVE.e.tensor_copy(out=v3(dst)[:, :, 0:d], in_=v3(src)[:, :, 0:d]), t1)
                    src, dst = dst, src
                b3 = v3(B0)
                t = dvo(lambda: DVE.e.tensor_tensor(out=v3(B1), in0=b3, in1=b3[:, :, 31:32].to_broadcast([128, NCH, 64]), op=ALU.subtract), t)
                te = aco(lambda: ACT.e.activation(out=E, in_=B1, func=AF.Exp), t)
                t2 = dvo(lambda: DVE.e.tensor_tensor(out=QT, in0=QF, in1=E, op=ALU.mult), te, tq)
                te = aco(lambda: ACT.e.activation(out=E, in_=B1, func=AF.Exp, scale=-1.0), t2)
                t2 = dvo(lambda: DVE.e.tensor_tensor(out=KT, in0=K32, in1=E, op=ALU.mult), te, tk)
                te = aco(lambda: ACT.e.activation(out=E, in_=B0, func=AF.Exp), t2)
                t2 = dvo(lambda: DVE.e.tensor_tensor(out=QH, in0=QF, in1=E, op=ALU.mult), te)
                t = dvo(lambda: DVE.e.tensor_tensor(out=v3(B1), in0=b3, in1=b3[:, :, 63:64].to_broadcast([128, NCH, 64]), op=ALU.subtract), t2)
                te = aco(lambda: ACT.e.activation(out=E, in_=B1, func=AF.Exp, scale=-1.0), t)
                t2 = dvo(lambda: DVE.e.tensor_tensor(out=KH, in0=K32, in1=E, op=ALU.mult), te)
                te = aco(lambda: ACT.e.activation(out=EBL, in_=b3[:, :, 63], func=AF.Exp), t2)
                tsg = aco(lambda: ACT.e.activation(out=QF, in_=inb["g"].ap, func=AF.Silu), t2)
                tel = (ACT.sem, ACT.sem.n)
                tdv = (DVE.sem, DVE.sem.n)
                tev = None
                for (srcT, dstK, dep) in ((inb["i"].ap, vtok, tin), (KH, khtok, tdv)):
                    for c0 in range(0, NCH, 8):
                        bank = kb.pring.next()
                        PE.wait(dep, tconst, *bank.deps_for_write())
                        pv = bank.ap.bitcast(bf)
                        for c in range(8):
                            ins = PE.e.transpose(out=pv[0:64, c * 128:(c + 1) * 128], in_=srcT[:, (c0 + c) * 64:(c0 + c + 1) * 64], identity=ident)
                        tp = PE.op(ins)
                        bank.wrote(tp)
                        tev = dvo(lambda: DVE.e.tensor_copy(out=dstK[:, c0:c0 + 8, :], in_=pv[0:64, 0:1024].rearrange("p (c e) -> p c e", e=128)), tp)
                        bank.read(tev)
                tO = None
                for c in range(NCH):
                    ch = slice(c * 64, (c + 1) * 64)
                    b1 = kb.pring.next()
                    tp1 = kb.mm_group(b1, [(KT[:, ch], QT[:, ch])], [tdv, tev], out_ap=b1.ap[0:64, 0:64])
                    sc = scm.next()
                    tm = dvo(lambda: DVE.e.tensor_tensor(out=sc.ap, in0=b1.ap[0:64, 0:64], in1=m64, op=ALU.mult), tp1, *sc.deps_for_write())
                    b1.read(tm)
                    sc.wrote(tm)
                    b2 = kb.pring.next()
                    tp2 = kb.mm_group(b2, [(QH[:, ch], Sbf[cur]), (sc.ap, vtok[:, c, :])], [tm, tS, tev], out_ap=b2.ap[0:64, 0:128])
                    sc.read(tp2)
                    b3_ = kb.pring.next()
                    tp3 = kb.mm_group(b3_, [(khtok[:, c, :], vtok[:, c, :])], [tev], out_ap=b3_.ap[:, 0:128])
                    tu = dvo(lambda: DVE.e.scalar_tensor_tensor(out=S32, in0=S32, scalar=EBL[:, c:c + 1], in1=b3_.ap[:, 0:128], op0=ALU.mult, op1=ALU.add), tp3, tel)
                    b3_.read(tu)
                    cur = 1 - cur
                    tS = aco(lambda: ACT.e.activation(out=Sbf[cur], in_=S32, func=AF.Copy), tu, tp2)
                    tO = aco(lambda: ACT.e.activation(out=Ot[:, c, :], in_=b2.ap[0:64, 0:128], func=AF.Copy), tp2)
                    b2.read(tO)
                t = dvo(lambda: DVE.e.tensor_tensor(out=osq, in0=Ot, in1=Ot, op=ALU.mult), tO)
                t = dvo(lambda: DVE.e.tensor_reduce(out=oss, in_=osq, axis=AX.X, op=ALU.add), t)
                t = dvo(lambda: DVE.e.tensor_scalar(out=oss, in0=oss, scalar1=1.0 / 128, scalar2=RMS_EPS, op0=ALU.mult, op1=ALU.add), t)
                t = aco(lambda: ACT.e.activation(out=oss, in_=oss, func=AF.Sqrt), t)
                t = dvo(lambda: DVE.e.reciprocal(out=oss, in_=oss), t)
                t = dvo(lambda: DVE.e.tensor_tensor(out=On, in0=Ot, in1=oss.to_broadcast([64, NCH, 128]), op=ALU.mult), t)
                ty = None
                for c0 in range(0, NCH, 8):
                    bank = kb.pring.next()
                    PE.wait(t, *bank.deps_for_write())
                    pv = bank.ap.bitcast(bf)
                    for c in range(8):
                        ins = PE.e.transpose(out=pv[:, c * 64:(c + 1) * 64], in_=On[:, c0 + c, :], identity=ident[0:64, 0:64])
                    tp = PE.op(ins)
                    bank.wrote(tp)
                    ty = dvo(lambda: DVE.e.scalar_tensor_tensor(out=YC.ap[:, c0 * 64:(c0 + 8) * 64], in0=pv[:, 0:512], scalar=hnw[:, l:l + 1], in1=QF[:, c0 * 64:(c0 + 8) * 64], op0=ALU.mult, op1=ALU.mult),
                             tp, tsg, *(YC.deps_for_write() if c0 == 0 else []))
                    bank.read(ty)
                YC.wrote(ty)
                tst = kb.dmar(POOL, lambda r: MIXT[rsl(r, 1536 + hh * 128, 128, HHL * 128), tok], lambda r: YC.ap, [ty], YC.sem)
                YC.read(tst)
                tlast = [(ACT.sem, ACT.sem.n), (DVE.sem, DVE.sem.n), (PE.sem, PE.sem.n)]
        kb.barrier()
        kb.sb_release(mk_)

    kb.hgrn = hgrn
    kb.sweep = sweep
    kb.loc = locals()
    return kb, kb.loc


def finish(kb):
    kb.barrier()


NONCE = 1234567
FLIP = 0


def _layout_inputs(inp, L, S, b, r=0, pair=False):
    f32 = np.float32
    SL = S // 2 if pair else S
    lsel = slice(r, None, 2) if pair else slice(None)
    hsel = slice(2 * r, 2 * r + 2) if pair else slice(None)
    HHL = 2 if pair else HH

    def pc(a, n):
        a = np.asarray(a, f32).reshape(L, n, 128)
        return np.ascontiguousarray(a.transpose(2, 0, 1).reshape(128, L * n))

    cw = np.asarray(inp["conv_w"], f32).reshape(L, 31, 4, 128).transpose(3, 0, 2, 1).reshape(128, L * 4 * 31)
    lb = np.asarray(inp["hgrn_lower_bounds"], f32).reshape(L, HH, 128)[:, hsel].transpose(2, 1, 0).reshape(128, HHL * L)
    cm = np.zeros((4, 128, 512), f32)
    p = np.arange(128)[:, None]
    f = np.arange(512)[None, :]
    for q in range(4):
        cm[q] = (p + 128 * q <= f)
    invf = (10000.0 ** (-np.arange(0, 64, 2, dtype=f32) / 64)).astype(f32)
    invf2 = np.concatenate([invf, invf])
    w = lambda k: np.ascontiguousarray(np.asarray(inp[k], f32)[lsel])
    m = {
        "x": np.ascontiguousarray(np.asarray(inp["x"], f32)[b, r * SL:(r + 1) * SL]),
        "pos": np.ascontiguousarray(np.asarray(inp["positions"], np.int32)[b, r * SL:(r + 1) * SL][None, :]),
        "w_in": w("w_in"), "w_uq": w("w_uq"), "w_ukv": w("w_ukv"), "w_out": w("w_out"),
        "w_gate": w("w_gate"), "w_up": w("w_up"), "w_down": w("w_down"),
        "anw": pc(inp["attn_norm_w"], 16), "fnw": pc(inp["ffn_norm_w"], 16),
        "finw": np.asarray(inp["final_norm_w"], f32)[None, :].copy(),
        "qnw": pc(inp["q_norm_w"], 4), "kvnw": pc(inp["kv_norm_w"], 4),
        "monw": pc(inp["mla_out_norm_w"], 1), "hnw": pc(inp["hgrn_norm_w"], 1),
        "cw": np.ascontiguousarray(cw), "cb": pc(inp["conv_b"], 4),
        "lnw": pc(inp["conv_ln_w"], 4), "lnb": pc(inp["conv_ln_b"], 4),
        "lbraw": np.ascontiguousarray(lb),
        "ident": np.eye(128, dtype=f32),
        "cmask": np.ascontiguousarray(cm.transpose(1, 0, 2).reshape(128, 4 * 512)),
        "m64": (np.arange(64)[:, None] <= np.arange(64)[None, :]).astype(f32),
        "invf": np.stack([invf2, invf2], axis=1).astype(f32),
        "rankf": np.full((128, 1), float(r), f32),
        "fv": np.full((1, 8), NONCE, np.int32),
        "zz": np.zeros((1, 64), np.int32),
    }
    return m


def build_full(S, L, debug=False, pair=False):
    kb, _ = build(S, L, debug, pair, NONCE)
    kb.sweep(0)
    for l in range(L):
        kb.mla(l)
        kb.conv(l)
        kb.hgrn(l)
        kb.pair_barrier()
        kb.sweep(l + 1)
    if pair:
        t = kb.dmar(kb.POOL, lambda r: kb.FL[r:r + 1], lambda r: kb.loc["zz_d"], [], kb.flsem)
    finish(kb)
    return kb


def kernel(**inputs):
    x = np.asarray(inputs["x"])
    B, S, _ = x.shape
    L = np.asarray(inputs["w_in"]).shape[0]
    pair = True
    kb = build_full(S, L, pair=pair)
    in_maps = [_layout_inputs(inputs, L, S, b, r, pair) for b in range(B) for r in range(2)]
    res = run_bass_kernel_spmd(kb.nc, in_maps, core_ids=list(range(2 * B)))
    outs = [np.concatenate([np.asarray(res.results[2 * b + r]["out"]) for r in range(2)], axis=0) for b in range(B)]
    return np.stack(outs, axis=0).astype(np.float32)
```

```python
import numpy as np
import ml_dtypes
import concourse.bass as bass
import concourse.mybir as mybir
from concourse.bass_utils import run_bass_kernel_spmd

F32, BF16, I32 = mybir.dt.float32, mybir.dt.bfloat16, mybir.dt.int32
ALU = mybir.AluOpType
AF = mybir.ActivationFunctionType
AX = mybir.AxisListType

D = 2048
DFF = 5632
NH = 8
HH = 4
RMS_EPS = 1e-6
LN_EPS = 1e-5
PROJ_ROWS = 4224
R_CQ, R_CKV, R_CONV, R_HQ, R_HF, R_HI, R_HG, R_KR = 0, 512, 1024, 2048, 2560, 3072, 3584, 4096
TT = 512
TWO_PI = 6.283185307179586


class Sem:
    def __init__(self, nc, name):
        self.h = nc.alloc_semaphore(name)
        self.n = 0


class Eng:
    def __init__(self, nc, e, name):
        self.e = e
        self.sem = Sem(nc, "s_" + name)
        self.seen = {}

    def wait(self, *deps):
        for d in deps:
            if d is None:
                continue
            if isinstance(d, list):
                self.wait(*d)
                continue
            sem, val = d
            if self.seen.get(sem, 0) < val:
                self.e.wait_ge(sem.h, val)
                self.seen[sem] = val

    def op(self, ins):
        ins.then_inc(self.sem.h, 1)
        self.sem.n += 1
        return (self.sem, self.sem.n)


class Slot:
    def __init__(self, ap, sem=None):
        self.ap = ap
        self.sem = sem
        self.w = None
        self.r = []

    def deps_for_write(self):
        return [self.w] + self.r

    def wrote(self, t):
        self.w = t
        self.r = []

    def read(self, t):
        self.r.append(t)
        if len(self.r) > 6:
            self.r = self.r[-6:]


class Ring:
    def __init__(self, slots):
        self.slots = slots
        self.i = 0

    def next(self):
        s = self.slots[self.i % len(self.slots)]
        self.i += 1
        return s


class KB:
    def __init__(self, S, L, debug=False, pair=False):
        self.S, self.L, self.debug, self.pair = S, L, debug, pair
        self.NT = (S // 2 if pair else S) // TT
        self.nbar = 0
        nc = self.nc = bass.Bass("TRN2", target_bir_lowering=False)
        self.PE = Eng(nc, nc.tensor, "pe")
        self.ACT = Eng(nc, nc.scalar, "act")
        self.DVE = Eng(nc, nc.vector, "dve")
        self.POOL = Eng(nc, nc.gpsimd, "pool")
        self.SP = Eng(nc, nc.sync, "sp")
        self.engs = [self.PE, self.ACT, self.DVE, self.POOL, self.SP]
        self.dsems = []
        self.nsem = 0
        self.sb_stack = []
        self.phase_sems = []
        self.banks = [Slot(nc.alloc_psum_tensor(f"bank{i}", [128, 512], F32).ap()) for i in range(8)]
        self.pring = Ring(self.banks)

    def dsem(self, name=None):
        if getattr(self, "free_sems", None):
            s = self.free_sems.pop()
        else:
            self.nsem += 1
            s = Sem(self.nc, f"d{self.nsem}")
            self.dsems.append(s)
        self.phase_sems.append(s)
        return s

    def sem_mark(self):
        return len(self.phase_sems)

    def sem_release(self, mark):
        if not hasattr(self, "free_sems"):
            self.free_sems = []
        while len(self.phase_sems) > mark:
            self.free_sems.append(self.phase_sems.pop())

    def dma(self, q, out, in_, deps, sem):
        q.wait(*deps)
        ins = q.e.dma_start(out=out, in_=in_)
        ins.then_inc(sem.h, 16)
        sem.n += 16
        return (sem, sem.n)

    def dmar(self, q, out_fn, in_fn, deps, sem):
        if not self.pair:
            return self.dma(q, out_fn(0), in_fn(0), deps, sem)
        q.wait(*deps)
        with q.e.If(self.half == 0):
            q.e.dma_start(out=out_fn(0), in_=in_fn(0)).then_inc(sem.h, 16)
        with q.e.Else():
            q.e.dma_start(out=out_fn(1), in_=in_fn(1)).then_inc(sem.h, 16)
        sem.n += 16
        return (sem, sem.n)

    def barrier(self):
        ts = [(e.sem, e.sem.n) for e in self.engs if e.sem.n > 0]
        ts += [(s, s.n) for s in self.dsems if s.n > 0]
        for e in self.engs:
            e.wait(*ts)
        for b in self.banks:
            b.w = None
            b.r = []

    def sb(self, name, shape, dt):
        self.nsem += 1
        g = self.nc.sbuf_tensor(f"sb{self.nsem}_{name}", list(shape), dt)
        t = g.__enter__()
        self.sb_stack.append(g)
        return t.ap() if hasattr(t, "ap") else t

    def sb_mark(self):
        return (len(self.sb_stack), len(self.phase_sems))

    def sb_release(self, mark):
        mark, smark = mark
        while len(self.sb_stack) > mark:
            g = self.sb_stack.pop()
            g.__exit__(None, None, None)
        self.sem_release(smark)

    def dram(self, name, shape, dt, kind="Internal", shared=False):
        if shared and self.pair and kind == "Internal":
            return self.nc.dram_tensor(name, list(shape), dt, kind=kind, addr_space="Shared").ap()
        return self.nc.dram_tensor(name, list(shape), dt, kind=kind).ap()

    def pair_barrier(self):
        self.barrier()
        if not self.pair:
            return
        nc = self.nc
        POOL = self.POOL
        k = self.nbar
        self.nbar += 1
        PT = mybir.EngineType.Pool
        t = self.dmar(POOL, lambda r: self.FL[r:r + 1, k:k + 1], lambda r: self.fv[0:1, 0:1], [], self.flsem)
        POOL.wait(t)
        for r in (0, 1):
            lid = nc.next_id()
            ls, le = f"spin_{lid}_loop", f"spin_{lid}_end"
            regs = nc.alloc_registers(f"spin_{lid}_r", engines=[PT])
            nc.br(ls, engines=[PT])
            with nc.body(ls, valid_engines=[PT]):
                nc.reg_load(regs, self.FL[r:r + 1, k:k + 1])
                nc.br_ne(regs, self.nonce, on_true=ls, on_false=le, engines=[PT])
            nc.switch_bb(le)
        t = POOL.op(POOL.e.memset(self.bdummy, 0.0))
        for e in self.engs:
            e.wait(t)

    def mm_group(self, bank, pairs, deps, out_ap=None):
        PE = self.PE
        PE.wait(*deps, *bank.deps_for_write())
        o = bank.ap if out_ap is None else out_ap
        n = len(pairs)
        ins = None
        for i, (l, r) in enumerate(pairs):
            ins = PE.e.matmul(o, lhsT=l, rhs=r, start=(i == 0), stop=(i == n - 1))
        t = PE.op(ins)
        bank.wrote(t)
        return t


def build(S, L, debug=False, pair=False, nonce=12345):
    kb = KB(S, L, debug, pair)
    nc = kb.nc
    PE, ACT, DVE, POOL, SP = kb.PE, kb.ACT, kb.DVE, kb.POOL, kb.SP
    NT = kb.NT
    SL = NT * TT
    NB = S // 128
    L2 = L // 2 if pair else L
    NHL = NH // 2 if pair else NH
    HHL = HH // 2 if pair else HH
    bf = BF16
    kb.nonce = nonce
    if pair:
        half = (nc.partition_id() + FLIP) % 2
    else:
        half = 0
    kb.half = half


    def rsl(r, off, n, mult):
        return slice(r * mult + off, r * mult + off + n)

    x = kb.dram("x", [SL, D], F32, "ExternalInput")
    pos = kb.dram("pos", [1, SL], I32, "ExternalInput")
    w_in = kb.dram("w_in", [L2, D, 4160], F32, "ExternalInput")
    w_uq = kb.dram("w_uq", [L2, 512, 1536], F32, "ExternalInput")
    w_ukv = kb.dram("w_ukv", [L2, 512, 2048], F32, "ExternalInput")
    w_out = kb.dram("w_out", [L2, D, D], F32, "ExternalInput")
    w_gate = kb.dram("w_gate", [L2, D, DFF], F32, "ExternalInput")
    w_up = kb.dram("w_up", [L2, D, DFF], F32, "ExternalInput")
    w_down = kb.dram("w_down", [L2, DFF, D], F32, "ExternalInput")
    anw_d = kb.dram("anw", [128, L * 16], F32, "ExternalInput")
    fnw_d = kb.dram("fnw", [128, L * 16], F32, "ExternalInput")
    finw_d = kb.dram("finw", [1, D], F32, "ExternalInput")
    qnw_d = kb.dram("qnw", [128, L * 4], F32, "ExternalInput")
    kvnw_d = kb.dram("kvnw", [128, L * 4], F32, "ExternalInput")
    monw_d = kb.dram("monw", [128, L], F32, "ExternalInput")
    hnw_d = kb.dram("hnw", [128, L], F32, "ExternalInput")
    cw_d = kb.dram("cw", [128, L * 4 * 31], F32, "ExternalInput")
    cb_d = kb.dram("cb", [128, L * 4], F32, "ExternalInput")
    lnw_d = kb.dram("lnw", [128, L * 4], F32, "ExternalInput")
    lnb_d = kb.dram("lnb", [128, L * 4], F32, "ExternalInput")
    lb_d = kb.dram("lbraw", [128, HHL * L], F32, "ExternalInput")
    rankf_d = kb.dram("rankf", [128, 1], F32, "ExternalInput")
    kb.fv = kb.dram("fv", [1, 8], I32, "ExternalInput")
    zz_d = kb.dram("zz", [1, 64], I32, "ExternalInput")
    ident_d = kb.dram("ident", [128, 128], F32, "ExternalInput")
    cmask_d = kb.dram("cmask", [128, 4 * 512], F32, "ExternalInput")
    m64_d = kb.dram("m64", [64, 64], F32, "ExternalInput")
    invf_d = kb.dram("invf", [64, 2], F32, "ExternalInput")
    out = kb.dram("out", [SL, D], F32, "ExternalOutput")

    dk = "ExternalOutput" if debug else "Internal"
    PROJT = kb.dram("projt", [PROJ_ROWS, S], bf, dk, shared=True)
    MIXT = kb.dram("mixt", [D, S], bf, dk, shared=True)
    H = kb.dram("hres", [SL, D], F32, dk)
    QN = kb.dram("qn", [NH, 128, S], bf, dk, shared=True)
    QP = kb.dram("qp", [NH, 64, S], bf, dk, shared=True)
    KN = kb.dram("kn", [NH, 128, S], bf, dk, shared=True)
    KP = kb.dram("kp", [64, S], bf, dk, shared=True)
    VS = kb.dram("vs", [NB, 128, NH, 129], bf, dk, shared=True)
    COS = kb.dram("cosd", [64, SL], F32, dk)
    SIN = kb.dram("sind", [64, SL], F32, dk)
    WBin = kb.dram("wbin", [L, D, PROJ_ROWS], bf, dk, shared=True)
    WBuq = kb.dram("wbuq", [L, 512, 2048], bf, dk, shared=True)
    WBukv = kb.dram("wbukv", [L, 512, 2048], bf, shared=True)
    WBout = kb.dram("wbout", [L, D, D], bf, shared=True)
    WBg = kb.dram("wbg", [L, D, DFF], bf, shared=True)
    WBu = kb.dram("wbu", [L, D, DFF], bf, shared=True)
    WBd = kb.dram("wbd", [L, DFF, D], bf, shared=True)

    wsem = kb.dsem("wcast")

    cast_hist = []

    def cast2d(W, i, cs, src, rows, view=None):
        step = 256
        W2 = W.rearrange("l r c -> (l r) c")
        nr = W.shape[1]
        for r0 in range(0, rows, step):
            def dst(r):
                li = 2 * i + r if pair else i
                rs = slice(li * nr + r0, li * nr + r0 + step)
                return W2[rs, cs] if view is None else view(W2[rs])
            tk = kb.dmar(POOL, dst, lambda r: src[r0:r0 + step], [], wsem)

    if pair:
        kb.FL = kb.dram("flags", [2, 64], I32, shared=True)
        kb.flsem = kb.dsem("flag")
        t = kb.dmar(POOL, lambda r: kb.FL[r:r + 1], lambda r: zz_d, [], kb.flsem)
        POOL.wait(t)

    for i in range(L2):
        cast2d(WBin, i, slice(0, 1024), w_in[i][:, 0:1024], D)
        cast2d(WBin, i, slice(1024, 4096), w_in[i][:, 1088:4160], D)
        cast2d(WBin, i, slice(4096, 4160), w_in[i][:, 1024:1088], D)
        cast2d(WBin, i, slice(4160, 4192), w_in[i][:, 1056:1088], D)
        cast2d(WBin, i, slice(4192, 4224), w_in[i][:, 1024:1056], D)
        sq = w_uq[i].rearrange("k (h c) -> k h c", c=192)
        cast2d(WBuq, i, None, sq[:, :, 0:192], 512, view=lambda a_: a_.rearrange("k (h c) -> k h c", c=256)[:, :, 0:192])
        cast2d(WBuq, i, None, sq[:, :, 160:192], 512, view=lambda a_: a_.rearrange("k (h c) -> k h c", c=256)[:, :, 192:224])
        cast2d(WBuq, i, None, sq[:, :, 128:160], 512, view=lambda a_: a_.rearrange("k (h c) -> k h c", c=256)[:, :, 224:256])
        cast2d(WBukv, i, slice(0, 2048), w_ukv[i], 512)
        cast2d(WBout, i, slice(0, D), w_out[i], D)
        cast2d(WBg, i, slice(0, DFF), w_gate[i], D)
        cast2d(WBu, i, slice(0, DFF), w_up[i], D)
        cast2d(WBd, i, slice(0, D), w_down[i], DFF)

    csem = kb.dsem("const")
    ident = kb.sb("ident", [128, 128], bf)
    ident32 = kb.sb("ident32", [128, 128], F32)
    ones_bf = kb.sb("ones_bf", [128, 128], bf)
    ones32 = kb.sb("ones32", [128, 128], F32)
    anw = kb.sb("anw", [128, L * 16], F32)
    fnw = kb.sb("fnw", [128, L * 16], F32)
    qnw = kb.sb("qnw", [128, L * 4], F32)
    kvnw = kb.sb("kvnw", [128, L * 4], F32)
    monw = kb.sb("monw", [128, L], F32)
    hnw = kb.sb("hnw", [128, L], F32)
    cb = kb.sb("cb", [128, L * 4], F32)
    lnw = kb.sb("lnw", [128, L * 4], F32)
    lnb = kb.sb("lnb", [128, L * 4], F32)
    lbt = kb.sb("lbt", [128, HHL, L], F32)
    oml = kb.sb("oml", [128, HHL, L], F32)
    rankf = kb.sb("rankf", [128, 1], F32)
    kb.bdummy = kb.sb("bdummy", [128, 1], F32)
    m64 = kb.sb("m64", [64, 64], F32)
    invf = kb.sb("invf", [64, 2], F32)
    kb.dma(POOL, ident, ident_d, [], csem)
    kb.dma(SP, ident32, ident_d, [], csem)
    for (a, b_) in [(anw, anw_d), (fnw, fnw_d), (qnw, qnw_d), (kvnw, kvnw_d), (monw, monw_d), (hnw, hnw_d),
                    (cb, cb_d), (lnw, lnw_d), (lnb, lnb_d), (lbt.rearrange("p h l -> p (h l)"), lb_d),
                    (m64, m64_d), (invf, invf_d), (rankf, rankf_d)]:
        kb.dma(SP, a, b_, [], csem)
    tconst = (csem, csem.n)
    t = DVE.op(DVE.e.memset(ones_bf, 1.0))
    t = DVE.op(DVE.e.memset(ones32, 1.0))
    DVE.wait(tconst)
    ACT.wait(tconst)
    t = ACT.op(ACT.e.activation(out=lbt, in_=lbt, func=AF.Exp))
    lsum = kb.sb("lsum", [128, HHL, 1], F32)
    DVE.wait(t)
    t = DVE.op(DVE.e.tensor_reduce(out=lsum, in_=lbt, axis=AX.X, op=ALU.add))
    DVE.wait(t)
    t = DVE.op(DVE.e.reciprocal(out=lsum, in_=lsum))
    DVE.wait(t)
    t = DVE.op(DVE.e.tensor_tensor(out=lbt, in0=lbt, in1=lsum.to_broadcast([128, HHL, L]), op=ALU.mult))
    DVE.wait(t)
    t = DVE.op(DVE.e.memset(lbt[:, :, 0:1], 0.0))
    for l in range(2, L):
        DVE.wait(t)
        t = DVE.op(DVE.e.tensor_tensor(out=lbt[:, :, l:l + 1], in0=lbt[:, :, l:l + 1], in1=lbt[:, :, l - 1:l], op=ALU.add))
    DVE.wait(t)
    t = DVE.op(DVE.e.tensor_scalar(out=oml, in0=lbt, scalar1=-1.0, scalar2=1.0, op0=ALU.mult, op1=ALU.add))
    t_lb = t

    mk = kb.sb_mark()
    RC = min(SL, 2048)
    posi = kb.sb("posi", [64, RC], I32)
    ang = kb.sb("ang", [64, RC], F32)
    ang2 = kb.sb("ang2", [64, RC], F32)
    rsem = kb.dsem("rope")
    rst = kb.dsem("ropest")
    tl = None
    mk1 = kb.sb("mk1", [64, RC], F32)
    ki = kb.sb("ki", [64, RC], I32)
    C1 = 6.28125
    C2 = TWO_PI - 6.28125
    PI = float(np.pi)

    def dv(ins, *deps):
        DVE.wait(*deps)
        return DVE.op(ins())

    for r0 in range(0, SL, RC):
        tl = kb.dma(SP, posi, pos[0:1, r0:r0 + RC].partition_broadcast(64), [tl, (rst, rst.n), (DVE.sem, DVE.sem.n)], rsem)
        DVE.wait(tl, tconst)
        t = DVE.op(DVE.e.tensor_copy(out=ang, in_=posi))
        t = dv(lambda: DVE.e.tensor_scalar(out=ang, in0=ang, scalar1=invf[:, 0:1], scalar2=None, op0=ALU.mult), t)
        t = dv(lambda: DVE.e.tensor_scalar(out=ang2, in0=ang, scalar1=1.0 / TWO_PI, scalar2=None, op0=ALU.mult), t)
        t = dv(lambda: DVE.e.tensor_copy(out=ki, in_=ang2), t)
        t = dv(lambda: DVE.e.tensor_copy(out=ang2, in_=ki), t)
        t = dv(lambda: DVE.e.scalar_tensor_tensor(out=ang, in0=ang2, scalar=-C1, in1=ang, op0=ALU.mult, op1=ALU.add), t)
        t = dv(lambda: DVE.e.scalar_tensor_tensor(out=ang, in0=ang2, scalar=-C2, in1=ang, op0=ALU.mult, op1=ALU.add), t)
        t = dv(lambda: DVE.e.tensor_scalar(out=mk1, in0=ang, scalar1=PI, scalar2=None, op0=ALU.is_gt), t)
        t = dv(lambda: DVE.e.scalar_tensor_tensor(out=ang, in0=mk1, scalar=-TWO_PI, in1=ang, op0=ALU.mult, op1=ALU.add), t)
        t = dv(lambda: DVE.e.tensor_scalar(out=mk1, in0=ang, scalar1=-PI, scalar2=None, op0=ALU.is_lt), t)
        t = dv(lambda: DVE.e.scalar_tensor_tensor(out=ang, in0=mk1, scalar=TWO_PI, in1=ang, op0=ALU.mult, op1=ALU.add), t)
        t = dv(lambda: DVE.e.tensor_scalar(out=ang2, in0=ang, scalar1=PI / 2, scalar2=None, op0=ALU.add), t)
        t = dv(lambda: DVE.e.tensor_scalar(out=mk1, in0=ang2, scalar1=PI, scalar2=None, op0=ALU.is_gt), t)
        t = dv(lambda: DVE.e.scalar_tensor_tensor(out=ang2, in0=mk1, scalar=-TWO_PI, in1=ang2, op0=ALU.mult, op1=ALU.add), t)
        t = dv(lambda: DVE.e.tensor_scalar(out=ang, in0=ang, scalar1=-PI, scalar2=PI, op0=ALU.max, op1=ALU.min), t)
        t = dv(lambda: DVE.e.tensor_scalar(out=ang2, in0=ang2, scalar1=-PI, scalar2=PI, op0=ALU.max, op1=ALU.min), t)
        ACT.wait(t)
        t2 = ACT.op(ACT.e.activation(out=ang, in_=ang, func=AF.Sin))
        t2 = ACT.op(ACT.e.activation(out=ang2, in_=ang2, func=AF.Sin))
        t3 = dv(lambda: DVE.e.tensor_scalar(out=ang[0:32], in0=ang[0:32], scalar1=-1.0, scalar2=None, op0=ALU.mult), t2)
        kb.dma(POOL, SIN[:, r0:r0 + RC], ang, [t3], rst)
        kb.dma(POOL, COS[:, r0:r0 + RC], ang2, [t2], rst)
    kb.pair_barrier()
    kb.sb_release(mk)

    def dvo(fn, *deps):
        DVE.wait(*deps)
        return DVE.op(fn())

    def aco(fn, *deps):
        ACT.wait(*deps)
        return ACT.op(fn())

    def poo(fn, *deps):
        POOL.wait(*deps)
        return POOL.op(fn())

    def sweep(l):
        mk_ = kb.sb_mark()
        h = Slot(kb.sb("h", [128, 4, D], F32), kb.dsem())
        xs = Slot(kb.sb("xs", [128, 4, D], bf))
        actT = Slot(kb.sb("actT", [128, 16, TT], bf), kb.dsem())
        junk = kb.sb("junk", [128, D], bf)
        ss = kb.sb("ss", [128, 4], F32)
        rstd = kb.sb("rstd", [128, 4], F32)
        wring = Ring([Slot(kb.sb(f"w{i}", [128, 16, 512], bf), kb.dsem()) for i in range(4)])
        stg = Ring([Slot(kb.sb(f"stg{i}", [128, 4, TT], bf), kb.dsem()) for i in range(2)])
        hst = kb.dsem()
        if l > 0:
            ffT = Slot(kb.sb("ffT", [128, 44, TT], bf))
            sg = Ring([Slot(kb.sb(f"sg{i}", [128, TT], F32)) for i in range(2)])
        if l == L:
            finw = kb.sb("finw", [128, D], F32)
            tfin = kb.dma(SP, finw, finw_d.partition_broadcast(128), [], kb.dsem())
            ostg = Slot(kb.sb("ostg", [128, D], F32), kb.dsem())
        evk = [0]

        def evac_copy(out_ap, in_ap, deps, scale=None):
            evk[0] += 1
            if evk[0] % 2 == 0:
                if scale is None:
                    return aco(lambda: ACT.e.activation(out=out_ap, in_=in_ap, func=AF.Copy), *deps)
                return aco(lambda: ACT.e.activation(out=out_ap, in_=in_ap, func=AF.Copy, scale=scale), *deps)
            if scale is None:
                return dvo(lambda: DVE.e.tensor_copy(out=out_ap, in_=in_ap), *deps)
            return dvo(lambda: DVE.e.tensor_scalar(out=out_ap, in0=in_ap, scalar1=scale, scalar2=None, op0=ALU.mult), *deps)

        def load_w(src3, k, ncols):
            sl = wring.next()
            t = kb.dma(SP, sl.ap[:, 0:k, 0:ncols], src3, sl.deps_for_write(), sl.sem)
            sl.wrote(t)
            return sl

        def norm_to_actT(nw_col0, nw):
            t = dvo(lambda: DVE.e.memset(ss, 0.0), h.w)
            for tb in range(4):
                t = aco(lambda: ACT.e.activation(out=junk, in_=h.ap[:, tb, :], func=AF.Square, accum_out=ss[:, tb:tb + 1]), t, h.w)
            t = dvo(lambda: DVE.e.tensor_scalar(out=rstd, in0=ss, scalar1=1.0 / D, scalar2=RMS_EPS, op0=ALU.mult, op1=ALU.add), t)
            t = aco(lambda: ACT.e.activation(out=rstd, in_=rstd, func=AF.Sqrt), t)
            t = dvo(lambda: DVE.e.reciprocal(out=rstd, in_=rstd), t)
            txs = None
            for tb in range(4):
                txs = dvo(lambda: DVE.e.tensor_scalar(out=xs.ap[:, tb, :], in0=h.ap[:, tb, :], scalar1=rstd[:, tb:tb + 1], scalar2=None, op0=ALU.mult), t, *xs.deps_for_write())
            xs.wrote(txs)
            tl = None
            for c in range(16):
                bank = kb.pring.next()
                PE.wait(txs, tconst, *bank.deps_for_write())
                pv = bank.ap.bitcast(bf)
                for tb in range(4):
                    ins = PE.e.transpose(out=pv[:, tb * 128:(tb + 1) * 128], in_=xs.ap[:, tb, c * 128:(c + 1) * 128], identity=ident)
                tp = PE.op(ins)
                bank.wrote(tp)
                te = evac_copy(actT.ap[:, c, :], pv[:, 0:TT], [tp] + (actT.deps_for_write() if c == 0 else []), scale=nw[:, nw_col0 + c:nw_col0 + c + 1])
                bank.read(te)
                tl = te
            xs.read(tp)
            actT.wrote(None)
            actT.w = [(ACT.sem, ACT.sem.n), (DVE.sem, DVE.sem.n)]
            return actT.w

        for j in range(NT):
            tok = slice(j * TT, (j + 1) * TT)
            gtok = lambda r: rsl(r, j * TT, TT, SL)
            src = x if l <= 1 else H
            th = kb.dma(SP, h.ap, src[tok, :].rearrange("(tb p) d -> p tb d", p=128), h.deps_for_write() + [(hst, hst.n)], h.sem)
            h.wrote(th)
            if l > 0:
                lw = l - 1
                tm = kb.dmar(SP, lambda r: actT.ap, lambda r: MIXT[:, gtok(r)].rearrange("(c p) t -> p c t", p=128), actT.deps_for_write(), actT.sem)
                actT.wrote(tm)
                tacc = None
                for cg in range(4):
                    sl = load_w(WBout[lw][:, cg * 512:(cg + 1) * 512].rearrange("(c p) n -> p c n", p=128), 16, 512)
                    for tb in range(4):
                        bank = kb.pring.next()
                        tp = kb.mm_group(bank, [(actT.ap[:, c, tb * 128:(tb + 1) * 128], sl.ap[:, c, :]) for c in range(16)], [sl.w, tm])
                        hv = h.ap[:, tb, cg * 512:(cg + 1) * 512]
                        tacc = dvo(lambda: DVE.e.tensor_tensor(out=hv, in0=hv, in1=bank.ap, op=ALU.add), tp, th)
                        bank.read(tacc)
                    sl.read(tp)
                actT.read(tp)
                h.wrote(tacc)
                ta = norm_to_actT(lw * 16, fnw)
                tff = None
                for g in range(11):
                    slg = load_w(WBg[lw][:, g * 512:(g + 1) * 512].rearrange("(c p) n -> p c n", p=128), 16, 512)
                    slu = load_w(WBu[lw][:, g * 512:(g + 1) * 512].rearrange("(c p) n -> p c n", p=128), 16, 512)
                    for q in range(4):
                        bg = kb.pring.next()
                        tg = kb.mm_group(bg, [(slg.ap[:, c, q * 128:(q + 1) * 128], actT.ap[:, c, :]) for c in range(16)], [slg.w] + ta)
                        bu = kb.pring.next()
                        tu = kb.mm_group(bu, [(slu.ap[:, c, q * 128:(q + 1) * 128], actT.ap[:, c, :]) for c in range(16)], [slu.w] + ta)
                        sgs = sg.next()
                        ts_ = aco(lambda: ACT.e.activation(out=sgs.ap, in_=bg.ap, func=AF.Silu), tg, *sgs.deps_for_write())
                        sgs.wrote(ts_)
                        bg.read(ts_)
                        tff = dvo(lambda: DVE.e.tensor_tensor(out=ffT.ap[:, g * 4 + q, :], in0=sgs.ap, in1=bu.ap, op=ALU.mult), ts_, tu, *(ffT.deps_for_write() if (g == 0 and q == 0) else []))
                        sgs.read(tff)
                        bu.read(tff)
                    slg.read(tg)
                    slu.read(tu)
                actT.read(tu)
                ffT.wrote(tff)
                tacc = None
                for cg in range(4):
                    sls = []
                    for (c0, k) in [(0, 16), (16, 16), (32, 12)]:
                        sls.append((c0, k, load_w(WBd[lw][c0 * 128:(c0 + k) * 128, cg * 512:(cg + 1) * 512].rearrange("(c p) n -> p c n", p=128), k, 512)))
                    for tb in range(4):
                        bank = kb.pring.next()
                        pairs = []
                        for (c0, k, sl) in sls:
                            pairs += [(ffT.ap[:, c0 + c, tb * 128:(tb + 1) * 128], sl.ap[:, c, :]) for c in range(k)]
                        tp = kb.mm_group(bank, pairs, [sl.w for (_, _, sl) in sls] + [tff])
                        hv = h.ap[:, tb, cg * 512:(cg + 1) * 512]
                        tacc = dvo(lambda: DVE.e.tensor_tensor(out=hv, in0=hv, in1=bank.ap, op=ALU.add), tp, h.w)
                        bank.read(tacc)
                    for (_, _, sl) in sls:
                        sl.read(tp)
                ffT.read(tp)
                h.wrote(tacc)
            if l == L:
                t = dvo(lambda: DVE.e.memset(ss, 0.0), h.w)
                for tb in range(4):
                    t = aco(lambda: ACT.e.activation(out=junk, in_=h.ap[:, tb, :], func=AF.Square, accum_out=ss[:, tb:tb + 1]), t, h.w)
                t = dvo(lambda: DVE.e.tensor_scalar(out=rstd, in0=ss, scalar1=1.0 / D, scalar2=RMS_EPS, op0=ALU.mult, op1=ALU.add), t)
                t = aco(lambda: ACT.e.activation(out=rstd, in_=rstd, func=AF.Sqrt), t)
                t = dvo(lambda: DVE.e.reciprocal(out=rstd, in_=rstd), t)
                for tb in range(4):
                    to = dvo(lambda: DVE.e.scalar_tensor_tensor(out=ostg.ap, in0=h.ap[:, tb, :], scalar=rstd[:, tb:tb + 1], in1=finw, op0=ALU.mult, op1=ALU.mult), t, tfin, *ostg.deps_for_write())
                    tst = kb.dma(POOL, out[j * TT + tb * 128: j * TT + (tb + 1) * 128, :], ostg.ap, [to], ostg.sem)
                    ostg.wrote(to)
                    ostg.read(tst)
                h.read(to)
                continue
            if l > 0:
                tst = kb.dma(POOL, H[tok, :].rearrange("(tb p) d -> p tb d", p=128), h.ap, [h.w], hst)
            ta = norm_to_actT(l * 16, anw)
            h.read(ta[0]); h.read(ta[1])
            for sidx in range(9):
                ncols = 512 if sidx < 8 else 128
                sl = load_w(WBin[l][:, sidx * 512: sidx * 512 + ncols].rearrange("(c p) n -> p c n", p=128), 16, ncols)
                st_ = stg.next()
                te = None
                nq = ncols // 128
                for q in range(nq):
                    bank = kb.pring.next()
                    tp = kb.mm_group(bank, [(sl.ap[:, c, q * 128:(q + 1) * 128], actT.ap[:, c, :]) for c in range(16)], [sl.w] + ta)
                    te = evac_copy(st_.ap[:, q, :], bank.ap, [tp] + (st_.deps_for_write() if q == 0 else []))
                    bank.read(te)
                sl.read(tp)
                st_.wrote(None)
                tst = kb.dmar(POOL, lambda r: PROJT[sidx * 512: sidx * 512 + ncols, gtok(r)].rearrange("(q p) t -> p q t", p=128), lambda r: st_.ap[:, 0:nq, :],
                             [(ACT.sem, ACT.sem.n), (DVE.sem, DVE.sem.n)], st_.sem)
                st_.read(tst)
            actT.read(tp)
        kb.barrier()
        kb.sb_release(mk_)

    def mla(l):
        mk_ = kb.sb_mark()
        wuq = kb.sb("wuq", [128, 4, 2048], bf)
        wukv = kb.sb("wukv", [128, 4, 2048], bf)
        wsm = kb.dsem()
        kb.dma(SP, wuq, WBuq[l].rearrange("(c p) n -> p c n", p=128), [], wsm)
        tw = kb.dma(SP, wukv, WBukv[l].rearrange("(c p) n -> p c n", p=128), [], wsm)
        cq = Slot(kb.sb("cq", [128, 4, TT], bf), kb.dsem())
        ckv = Slot(kb.sb("ckv", [128, 4, TT], bf), kb.dsem())
        kr = Slot(kb.sb("kr", [64, 2, TT], bf), kb.dsem())
        cst = Slot(kb.sb("cst", [64, 2, TT], F32), kb.dsem())
        sq = kb.sb("sq", [128, 4, TT], bf)
        rbc = kb.sb("rbc", [128, TT], F32)
        cqn = Slot(kb.sb("cqn", [128, 4, TT], bf))
        ckvn = Slot(kb.sb("ckvn", [128, 4, TT], bf))
        qn_st = Slot(kb.sb("qn_st", [128, NH, TT], bf), kb.dsem())
        kn_st = Slot(kb.sb("kn_st", [128, NH, TT], bf), kb.dsem())
        qp_st = Slot(kb.sb("qp_st", [64, NH, TT], bf), kb.dsem())
        kp_st = Slot(kb.sb("kp_st", [64, TT], bf), kb.dsem())
        v_st = Slot(kb.sb("v_st", [128, 4, NH, 129], bf), kb.dsem())
        r1 = kb.sb("r1", [64, TT], F32)
        r2 = kb.sb("r2", [64, TT], F32)
        t1s = dvo(lambda: DVE.e.memset(v_st.ap[:, :, :, 128:129], 1.0))
        COSv = cst.ap[:, 0, :]
        SINv = cst.ap[:, 1, :]
        wq3 = wuq.rearrange("p c (h e) -> p c h e", e=256)
        wkv3 = wukv.rearrange("p c (h e) -> p c h e", e=256)

        def normed(src, dst, nw, col0):
            t = dvo(lambda: DVE.e.tensor_tensor(out=sq, in0=src.ap, in1=src.ap, op=ALU.mult), src.w)
            bank = kb.pring.next()
            tp = kb.mm_group(bank, [(ones_bf, sq[:, c, :]) for c in range(4)], [t])
            t = dvo(lambda: DVE.e.tensor_scalar(out=rbc, in0=bank.ap, scalar1=1.0 / 512, scalar2=RMS_EPS, op0=ALU.mult, op1=ALU.add), tp)
            bank.read(t)
            t = aco(lambda: ACT.e.activation(out=rbc, in_=rbc, func=AF.Sqrt), t)
            t = dvo(lambda: DVE.e.reciprocal(out=rbc, in_=rbc), t)
            for c in range(4):
                t2 = dvo(lambda: DVE.e.scalar_tensor_tensor(out=dst.ap[:, c, :], in0=src.ap[:, c, :], scalar=nw[:, col0 + c:col0 + c + 1], in1=rbc, op0=ALU.mult, op1=ALU.mult), t, *(dst.deps_for_write() if c == 0 else []))
            dst.wrote(t2)
            src.read(t2)
            return t2

        for j in range(NT):
            tok = slice(j * TT, (j + 1) * TT)
            gtok = lambda r: rsl(r, j * TT, TT, SL)
            t = kb.dmar(SP, lambda r: cq.ap, lambda r: PROJT[R_CQ:R_CQ + 512, gtok(r)].rearrange("(c p) t -> p c t", p=128), cq.deps_for_write(), cq.sem); cq.wrote(t)
            t = kb.dmar(SP, lambda r: ckv.ap, lambda r: PROJT[R_CKV:R_CKV + 512, gtok(r)].rearrange("(c p) t -> p c t", p=128), ckv.deps_for_write(), ckv.sem); ckv.wrote(t)
            t = kb.dmar(SP, lambda r: kr.ap, lambda r: PROJT[R_KR:R_KR + 128, gtok(r)].rearrange("(c p) t -> p c t", p=64), kr.deps_for_write(), kr.sem); kr.wrote(t)
            kb.dma(SP, cst.ap[:, 0, :], COS[:, tok], cst.deps_for_write(), cst.sem)
            t = kb.dma(SP, cst.ap[:, 1, :], SIN[:, tok], [], cst.sem); cst.wrote(t)
            tq = normed(cq, cqn, qnw, l * 4)
            tk = normed(ckv, ckvn, kvnw, l * 4)
            t = dvo(lambda: DVE.e.tensor_tensor(out=r1, in0=kr.ap[:, 0, :], in1=COSv, op=ALU.mult), kr.w, cst.w)
            t = dvo(lambda: DVE.e.tensor_tensor(out=r2, in0=kr.ap[:, 1, :], in1=SINv, op=ALU.mult), t)
            t = dvo(lambda: DVE.e.tensor_tensor(out=kp_st.ap, in0=r1, in1=r2, op=ALU.add), t, *kp_st.deps_for_write())
            kr.read(t)
            kp_st.wrote(t)
            kp_st.read(kb.dmar(POOL, lambda r: KP[:, gtok(r)], lambda r: kp_st.ap, [t], kp_st.sem))
            for hh in range(NH):
                bank = kb.pring.next()
                tp = kb.mm_group(bank, [(wq3[:, c, hh, 0:128], cqn.ap[:, c, :]) for c in range(4)], [tq, tw])
                te = aco(lambda: ACT.e.activation(out=qn_st.ap[:, hh, :], in_=bank.ap, func=AF.Copy), tp, *(qn_st.deps_for_write() if hh == 0 else []))
                bank.read(te)
                b1 = kb.pring.next()
                tp1 = kb.mm_group(b1, [(wq3[:, c, hh, 128:192], cqn.ap[:, c, :]) for c in range(4)], [tq], out_ap=b1.ap[0:64, :])
                b2 = kb.pring.next()
                tp2 = kb.mm_group(b2, [(wq3[:, c, hh, 192:256], cqn.ap[:, c, :]) for c in range(4)], [tq], out_ap=b2.ap[0:64, :])
                t = dvo(lambda: DVE.e.tensor_tensor(out=r1, in0=b1.ap[0:64, :], in1=COSv, op=ALU.mult), tp1, cst.w)
                b1.read(t)
                t = dvo(lambda: DVE.e.tensor_tensor(out=r2, in0=b2.ap[0:64, :], in1=SINv, op=ALU.mult), tp2, t)
                b2.read(t)
                t = dvo(lambda: DVE.e.tensor_tensor(out=qp_st.ap[:, hh, :], in0=r1, in1=r2, op=ALU.add), t, *(qp_st.deps_for_write() if hh == 0 else []))
                bank = kb.pring.next()
                tp = kb.mm_group(bank, [(wkv3[:, c, hh, 0:128], ckvn.ap[:, c, :]) for c in range(4)], [tk, tw])
                te = aco(lambda: ACT.e.activation(out=kn_st.ap[:, hh, :], in_=bank.ap, func=AF.Copy), tp, *(kn_st.deps_for_write() if hh == 0 else []))
                bank.read(te)
            cst.read(t)
            tA = (ACT.sem, ACT.sem.n)
            tD = (DVE.sem, DVE.sem.n)
            qn_st.wrote(tA); kn_st.wrote(tA); qp_st.wrote(tD)
            qn_st.read(kb.dmar(POOL, lambda r: QN[:, :, gtok(r)].rearrange("h p t -> p h t"), lambda r: qn_st.ap, [tA], qn_st.sem))
            kn_st.read(kb.dmar(POOL, lambda r: KN[:, :, gtok(r)].rearrange("h p t -> p h t"), lambda r: kn_st.ap, [tA], kn_st.sem))
            qp_st.read(kb.dmar(POOL, lambda r: QP[:, :, gtok(r)].rearrange("h p t -> p h t"), lambda r: qp_st.ap, [tD], qp_st.sem))
            for tb in range(4):
                for hg in range(2):
                    bank = kb.pring.next()
                    tp = kb.mm_group(bank, [(ckvn.ap[:, c, tb * 128:(tb + 1) * 128], wkv3[:, c, hg * 4:(hg + 1) * 4, 128:256]) for c in range(4)], [tk, tw])
                    te = aco(lambda: ACT.e.activation(out=v_st.ap[:, tb, hg * 4:(hg + 1) * 4, 0:128], in_=bank.ap.rearrange("p (h e) -> p h e", e=128), func=AF.Copy),
                             tp, t1s, *(v_st.deps_for_write() if (tb == 0 and hg == 0) else []))
                    bank.read(te)
            cqn.read(tp); ckvn.read(tp)
            v_st.wrote(te)
            v_st.read(kb.dmar(POOL, lambda r: VS[rsl(r, j * 4, 4, SL // 128)].rearrange("b p h e -> p b h e"), lambda r: v_st.ap, [te], v_st.sem))
        kb.pair_barrier()
        kb.sb_release(mk_)

        mk_ = kb.sb_mark()
        NQ = S // 512
        kp = kb.sb("kp", [128, S], bf)
        kps = kb.dsem()
        tkp = kb.dma(SP, kp[0:64], KP, [], kps)
        tz = dvo(lambda: DVE.e.memset(kp[64:128], 0.0))
        hb = [dict(kn=Slot(kb.sb(f"kn{i}", [128, S], bf), kb.dsem()), qn=Slot(kb.sb(f"qnb{i}", [128, S], bf)),
                   qp=Slot(kb.sb(f"qpb{i}", [128, S], bf)), v=Slot(kb.sb(f"v{i}", [128, NB, 129], bf))) for i in range(2)]
        for B_ in hb:
            tz = dvo(lambda: DVE.e.memset(B_["qp"].ap[64:128], 0.0))
        cm = kb.sb("cm", [128, 4, 512], bf)
        tcm = kb.dma(POOL, cm.rearrange("p r f -> p (r f)"), cmask_d, [], kps)
        ptr = Ring([Slot(kb.sb(f"pt{i}", [128, 512], bf)) for i in range(3)])
        o32 = Slot(kb.sb("o32", [128, 4, 129], F32))
        rec = kb.sb("rec", [128, 4], F32)
        osq = kb.sb("osq", [128, 4, 128], F32)
        oss = kb.sb("oss", [128, 4], F32)
        onb = Slot(kb.sb("onb", [128, 4, 128], bf))
        ya = Ring([Slot(kb.sb(f"ya{i}", [128, 512], bf), kb.dsem()) for i in range(2)])
        stb = Ring(kb.banks[0:4])
        oacc = Ring([kb.banks[4:6], kb.banks[6:8]])
        scale = float((128 + 64) ** -0.5)
        for hh in range(NHL):
            B = hb[hh % 2]
            hg_ = lambda r: r * NHL + hh
            dw = B["kn"].deps_for_write() + B["qn"].r + B["qp"].r + B["v"].r
            kb.dmar(SP, lambda r: B["kn"].ap, lambda r: KN[hg_(r)], dw, B["kn"].sem)
            kb.dmar(SP, lambda r: B["qn"].ap, lambda r: QN[hg_(r)], [], B["kn"].sem)
            kb.dmar(SP, lambda r: B["qp"].ap[0:64], lambda r: QP[hg_(r)], [], B["kn"].sem)
            tl = kb.dmar(SP, lambda r: B["v"].ap, lambda r: VS[:, :, hg_(r), :].rearrange("b p e -> p b e"), [], B["kn"].sem)
            for k_ in ("kn", "qn", "qp", "v"):
                B[k_].wrote(tl)
            knS, qnS, qpS, vS = B["kn"].ap, B["qn"].ap, B["qp"].ap, B["v"].ap
            for i in range(NQ):
                qs = slice(i * 512, (i + 1) * 512)
                nkb = 4 * i + 4
                ob = oacc.next()
                ov = [ob[0].ap[:, 0:258].rearrange("p (s e) -> p s e", e=129), ob[1].ap[:, 0:258].rearrange("p (s e) -> p s e", e=129)]

                def qk(kbk):
                    bank = stb.next()
                    ks = slice(kbk * 128, (kbk + 1) * 128)
                    return bank, kb.mm_group(bank, [(knS[:, ks], qnS[:, qs]), (kp[:, ks], qpS[:, qs])], [tl, tkp, tz])

                nxt = qk(0)
                tpv = None
                for kbk in range(nkb):
                    bank, tqk = nxt
                    if kbk + 1 < nkb:
                        nxt = qk(kbk + 1)
                    pt = ptr.next()
                    te = aco(lambda: ACT.e.activation(out=pt.ap, in_=bank.ap, func=AF.Exp, scale=scale), tqk, *pt.deps_for_write())
                    bank.read(te)
                    r = kbk - 4 * i
                    if r >= 0:
                        te = poo(lambda: POOL.e.tensor_tensor(out=pt.ap, in0=pt.ap, in1=cm[:, r, :], op=ALU.mult), te, tcm)
                    pt.wrote(te)
                    PE.wait(te, *(ob[0].deps_for_write() + ob[1].deps_for_write() if kbk == 0 else []))
                    for s_ in range(4):
                        last = 4 * i + s_
                        if kbk > last:
                            continue
                        ins = PE.e.matmul(ov[s_ // 2][:, s_ % 2, :], lhsT=pt.ap[:, s_ * 128:(s_ + 1) * 128], rhs=vS[:, kbk, :], start=(kbk == 0 and s_ % 2 == 0), stop=(kbk == last))
                    tpv = PE.op(ins)
                    pt.read(tpv)
                ob[0].wrote(tpv); ob[1].wrote(tpv)
                t = dvo(lambda: DVE.e.tensor_copy(out=o32.ap[:, 0:2, :], in_=ov[0]), tpv, *o32.deps_for_write())
                t = dvo(lambda: DVE.e.tensor_copy(out=o32.ap[:, 2:4, :], in_=ov[1]), t)
                ob[0].read(t); ob[1].read(t)
                t = dvo(lambda: DVE.e.reciprocal(out=rec, in_=o32.ap[:, :, 128]), t)
                t = dvo(lambda: DVE.e.tensor_tensor(out=o32.ap[:, :, 0:128], in0=o32.ap[:, :, 0:128], in1=rec.to_broadcast([128, 4, 128]), op=ALU.mult), t)
                t = dvo(lambda: DVE.e.tensor_tensor(out=osq, in0=o32.ap[:, :, 0:128], in1=o32.ap[:, :, 0:128], op=ALU.mult), t)
                t = dvo(lambda: DVE.e.tensor_reduce(out=oss, in_=osq, axis=AX.X, op=ALU.add), t)
                t = dvo(lambda: DVE.e.tensor_scalar(out=oss, in0=oss, scalar1=1.0 / 128, scalar2=RMS_EPS, op0=ALU.mult, op1=ALU.add), t)
                t = aco(lambda: ACT.e.activation(out=oss, in_=oss, func=AF.Sqrt), t)
                t = dvo(lambda: DVE.e.reciprocal(out=oss, in_=oss), t)
                t = dvo(lambda: DVE.e.tensor_tensor(out=onb.ap, in0=o32.ap[:, :, 0:128], in1=oss.to_broadcast([128, 4, 128]), op=ALU.mult), t, *onb.deps_for_write())
                onb.wrote(t)
                o32.wrote(t)
                bank = stb.next()
                PE.wait(t, tconst, *bank.deps_for_write())
                pv = bank.ap.bitcast(bf)
                for s_ in range(4):
                    ins = PE.e.transpose(out=pv[:, s_ * 128:(s_ + 1) * 128], in_=onb.ap[:, s_, :], identity=ident)
                tp = PE.op(ins)
                bank.wrote(tp)
                onb.read(tp)
                yb = ya.next()
                te = aco(lambda: ACT.e.activation(out=yb.ap, in_=pv[:, 0:512], func=AF.Copy, scale=monw[:, l:l + 1]), tp, *yb.deps_for_write())
                bank.read(te)
                yb.wrote(te)
                yb.read(kb.dmar(POOL, lambda r: MIXT[hg_(r) * 128:(hg_(r) + 1) * 128, qs], lambda r: yb.ap, [te], yb.sem))
            for k_ in ("kn", "qn", "qp", "v"):
                B[k_].read(tpv)
        kb.barrier()
        kb.sb_release(mk_)

    kb.mla = mla
    def conv(l):
        mk_ = kb.sb_mark()
        PADL = 30
        G = kb.sb("G", [128, 4, PADL + SL], bf)
        cwt = kb.sb("cwt", [128, 4, 31], F32)
        Dg = kb.sb("Dg", [128, 4, 31, 128], bf)
        csm = kb.dsem()
        tcw = kb.dma(SP, cwt.rearrange("p c j -> p (c j)"), cw_d[:, l * 124:(l + 1) * 124], [], csm)
        t0 = dvo(lambda: DVE.e.memset(G[:, :, 0:PADL], 0.0))
        tD = None
        for c in range(4):
            for jj in range(31):
                tD = dvo(lambda: DVE.e.tensor_scalar(out=Dg[:, c, jj, :], in0=ident, scalar1=cwt[:, c, jj:jj + 1], scalar2=None, op0=ALU.mult), tcw, tconst)
        av = Slot(kb.sb("av", [128, 4, TT], bf), kb.dsem())
        gv = Slot(kb.sb("gv", [128, 4, TT], bf), kb.dsem())
        sgm = kb.sb("sgm", [128, 4, TT], F32)
        hsl_ = lambda r: rsl(r, 0, PADL, SL - PADL)
        t = kb.dmar(SP, lambda r: av.ap[:, :, 0:PADL], lambda r: PROJT[R_CONV:R_CONV + 512, hsl_(r)].rearrange("(c p) t -> p c t", p=128), [], av.sem)
        t = kb.dmar(SP, lambda r: gv.ap[:, :, 0:PADL], lambda r: PROJT[R_CONV + 512:R_CONV + 1024, hsl_(r)].rearrange("(c p) t -> p c t", p=128), [], av.sem)
        av.wrote(t); gv.wrote(t)
        ts_ = aco(lambda: ACT.e.activation(out=sgm[:, :, 0:PADL], in_=gv.ap[:, :, 0:PADL], func=AF.Sigmoid), t)
        t0 = dvo(lambda: DVE.e.tensor_tensor(out=sgm[:, :, 0:PADL], in0=sgm[:, :, 0:PADL], in1=av.ap[:, :, 0:PADL], op=ALU.mult), ts_, t0)
        t0 = dvo(lambda: DVE.e.tensor_scalar(out=G[:, :, 0:PADL], in0=sgm[:, :, 0:PADL], scalar1=rankf[:, 0:1], scalar2=None, op0=ALU.mult), t0, tconst)
        av.read(t0); gv.read(t0)
        tG = t0
        for j in range(NT):
            tok = slice(j * TT, (j + 1) * TT)
            gtok = lambda r: rsl(r, j * TT, TT, SL)
            t = kb.dmar(SP, lambda r: av.ap, lambda r: PROJT[R_CONV:R_CONV + 512, gtok(r)].rearrange("(c p) t -> p c t", p=128), av.deps_for_write(), av.sem); av.wrote(t)
            t = kb.dmar(SP, lambda r: gv.ap, lambda r: PROJT[R_CONV + 512:R_CONV + 1024, gtok(r)].rearrange("(c p) t -> p c t", p=128), gv.deps_for_write(), gv.sem); gv.wrote(t)
            ts_ = aco(lambda: ACT.e.activation(out=sgm, in_=gv.ap, func=AF.Sigmoid), gv.w, tG)
            gv.read(ts_)
            tG = dvo(lambda: DVE.e.tensor_tensor(out=G[:, :, PADL + j * TT: PADL + (j + 1) * TT], in0=av.ap, in1=sgm, op=ALU.mult), ts_, av.w, t0)
            av.read(tG)
        xc = Slot(kb.sb("xc", [128, 4, TT], F32))
        xq = Slot(kb.sb("xq", [128, 4, TT], F32))
        mean = kb.sb("mean", [128, TT], F32)
        var = kb.sb("var", [128, TT], F32)
        yb = Ring([Slot(kb.sb(f"yb{i}", [128, 4, TT], bf), kb.dsem()) for i in range(2)])
        for j in range(NT):
            tok = slice(j * TT, (j + 1) * TT)
            gtok = lambda r: rsl(r, j * TT, TT, SL)
            te = None
            for c in range(4):
                bank = kb.pring.next()
                tp = kb.mm_group(bank, [(Dg[:, c, jj, :], G[:, c, j * TT + jj: j * TT + jj + TT]) for jj in range(31)], [tG, tD])
                te = aco(lambda: ACT.e.activation(out=xc.ap[:, c, :], in_=bank.ap, func=AF.Identity, bias=cb[:, l * 4 + c:l * 4 + c + 1], scale=1.0), tp, tconst, *(xc.deps_for_write() if c == 0 else []))
                bank.read(te)
            xc.wrote(te)
            tq = dvo(lambda: DVE.e.tensor_tensor(out=xq.ap, in0=xc.ap, in1=xc.ap, op=ALU.mult), te, *xq.deps_for_write())
            xq.wrote(tq)
            bm = kb.pring.next()
            tpm = kb.mm_group(bm, [(ones32, xc.ap[:, c, :]) for c in range(4)], [te])
            bv = kb.pring.next()
            tpv = kb.mm_group(bv, [(ones32, xq.ap[:, c, :]) for c in range(4)], [tq])
            xq.read(tpv)
            t = dvo(lambda: DVE.e.tensor_scalar(out=mean, in0=bm.ap, scalar1=1.0 / 512, scalar2=None, op0=ALU.mult), tpm)
            bm.read(t)
            t = dvo(lambda: DVE.e.tensor_scalar(out=var, in0=bv.ap, scalar1=1.0 / 512, scalar2=LN_EPS, op0=ALU.mult, op1=ALU.add), tpv, t)
            bv.read(t)
            t = dvo(lambda: DVE.e.tensor_tensor(out=xq.ap[:, 0, :], in0=mean, in1=mean, op=ALU.mult), t)
            t = dvo(lambda: DVE.e.tensor_tensor(out=var, in0=var, in1=xq.ap[:, 0, :], op=ALU.subtract), t)
            t = aco(lambda: ACT.e.activation(out=var, in_=var, func=AF.Sqrt), t)
            t = dvo(lambda: DVE.e.reciprocal(out=var, in_=var), t)
            t = dvo(lambda: DVE.e.tensor_tensor(out=xc.ap, in0=xc.ap, in1=mean.unsqueeze(1).to_broadcast([128, 4, TT]), op=ALU.subtract), t)
            t = dvo(lambda: DVE.e.tensor_tensor(out=xc.ap, in0=xc.ap, in1=var.unsqueeze(1).to_broadcast([128, 4, TT]), op=ALU.mult), t)
            y = yb.next()
            for c in range(4):
                ta = aco(lambda: ACT.e.activation(out=y.ap[:, c, :], in_=xc.ap[:, c, :], func=AF.Silu, bias=lnb[:, l * 4 + c:l * 4 + c + 1], scale=lnw[:, l * 4 + c:l * 4 + c + 1]), t, *(y.deps_for_write() if c == 0 else []))
            xc.read(ta)
            y.wrote(ta)
            y.read(kb.dmar(POOL, lambda r: MIXT[1024:1536, gtok(r)].rearrange("(c p) t -> p c t", p=128), lambda r: y.ap, [ta], y.sem))
        kb.barrier()
        kb.sb_release(mk_)

    kb.conv = conv
    def hgrn(l):
        mk_ = kb.sb_mark()
        SEG = min(S, 2048)
        NSEG = S // SEG
        NCH = SEG // 64
        inb = {k_: Slot(kb.sb("in_" + k_, [128, SEG], bf), kb.dsem()) for k_ in ("q", "f", "i", "g")}
        F0 = kb.sb("F0", [128, SEG], F32)
        K32 = kb.sb("K32", [128, SEG], F32)
        B0 = kb.sb("B0", [128, SEG], F32)
        B1 = kb.sb("B1", [128, SEG], F32)
        QF = kb.sb("QF", [128, SEG], F32)
        E = kb.sb("E", [128, SEG], F32)
        QT = kb.sb("QT", [128, SEG], bf)
        KT = kb.sb("KT", [128, SEG], bf)
        QH = kb.sb("QH", [128, SEG], bf)
        KH = kb.sb("KH", [128, SEG], bf)
        EBL = kb.sb("EBL", [128, NCH], F32)
        vtok = kb.sb("vtok", [64, NCH, 128], bf)
        khtok = kb.sb("khtok", [64, NCH, 128], bf)
        Ot = kb.sb("Ot", [64, NCH, 128], F32)
        osq = kb.sb("osq", [64, NCH, 128], F32)
        oss = kb.sb("oss", [64, NCH], F32)
        On = kb.sb("On", [64, NCH, 128], bf)
        YC = Slot(kb.sb("YC", [128, SEG], bf), kb.dsem())
        S32 = kb.sb("S32", [128, 128], F32)
        Sbf = [kb.sb(f"Sbf{i}", [128, 128], bf) for i in range(2)]
        scm = Ring([Slot(kb.sb(f"scm{i}", [64, 64], bf)) for i in range(2)])
        m64b = kb.sb("m64b", [64, 64], F32)
        v3 = lambda ap: ap.rearrange("p (c k) -> p c k", k=64)
        tlast = None
        for hh in range(HHL):
            t = dvo(lambda: DVE.e.memset(S32, 0.0), tlast)
            tS = dvo(lambda: DVE.e.memset(Sbf[0], 0.0), t)
            cur = 0
            for sg_ in range(NSEG):
                tok = slice(sg_ * SEG, (sg_ + 1) * SEG)
                tin = None
                for k_, r0 in (("q", R_HQ), ("f", R_HF), ("i", R_HI), ("g", R_HG)):
                    tin = kb.dmar(SP, lambda r: inb[k_].ap, lambda r: PROJT[rsl(r, r0 + hh * 128, 128, HHL * 128), tok], [tlast, (PE.sem, PE.sem.n)], inb["q"].sem)
                t = aco(lambda: ACT.e.activation(out=F0, in_=inb["f"].ap, func=AF.Sigmoid), tin, tlast)
                t = dvo(lambda: DVE.e.tensor_scalar(out=F0, in0=F0, scalar1=oml[:, hh, l:l + 1], scalar2=lbt[:, hh, l:l + 1], op0=ALU.mult, op1=ALU.add), t, t_lb)
                tk = dvo(lambda: DVE.e.tensor_scalar(out=K32, in0=F0, scalar1=-1.0, scalar2=1.0, op0=ALU.mult, op1=ALU.add), t)
                t = aco(lambda: ACT.e.activation(out=B0, in_=F0, func=AF.Ln), t)
                tq = aco(lambda: ACT.e.activation(out=QF, in_=inb["q"].ap, func=AF.Silu), tin)
                src, dst = B0, B1
                for d in (1, 2, 4, 8, 16, 32):
                    t1 = dvo(lambda: DVE.e.tensor_tensor(out=v3(dst)[:, :, d:64], in0=v3(src)[:, :, d:64], in1=v3(src)[:, :, 0:64 - d], op=ALU.add), t)
                    t = dvo(lambda: DVE.e.tensor_copy(out=v3(dst)[:, :, 0:d], in_=v3(src)[:, :, 0:d]), t1)
                    src, dst = dst, src
                b3 = v3(B0)
                t = dvo(lambda: DVE.e.tensor_tensor(out=v3(B1), in0=b3, in1=b3[:, :, 31:32].to_broadcast([128, NCH, 64]), op=ALU.subtract), t)
                te = aco(lambda: ACT.e.activation(out=E, in_=B1, func=AF.Exp), t)
                t2 = dvo(lambda: DVE.e.tensor_tensor(out=QT, in0=QF, in1=E, op=ALU.mult), te, tq)
                te = aco(lambda: ACT.e.activation(out=E, in_=B1, func=AF.Exp, scale=-1.0), t2)
                t2 = dvo(lambda: DVE.e.tensor_tensor(out=KT, in0=K32, in1=E, op=ALU.mult), te, tk)
                te = aco(lambda: ACT.e.activation(out=E, in_=B0, func=AF.Exp), t2)
                t2 = dvo(lambda: DVE.e.tensor_tensor(out=QH, in0=QF, in1=E, op=ALU.mult), te)
                t = dvo(lambda: DVE.e.tensor_tensor(out=v3(B1), in0=b3, in1=b3[:, :, 63:64].to_broadcast([128, NCH, 64]), op=ALU.subtract), t2)
                te = aco(lambda: ACT.e.activation(out=E, in_=B1, func=AF.Exp, scale=-1.0), t)
                t2 = dvo(lambda: DVE.e.tensor_tensor(out=KH, in0=K32, in1=E, op=ALU.mult), te)
                te = aco(lambda: ACT.e.activation(out=EBL, in_=b3[:, :, 63], func=AF.Exp), t2)
                tsg = aco(lambda: ACT.e.activation(out=QF, in_=inb["g"].ap, func=AF.Silu), t2)
                tel = (ACT.sem, ACT.sem.n)
                tdv = (DVE.sem, DVE.sem.n)
                tev = None
                for (srcT, dstK, dep) in ((inb["i"].ap, vtok, tin), (KH, khtok, tdv)):
                    for c0 in range(0, NCH, 8):
                        bank = kb.pring.next()
                        PE.wait(dep, tconst, *bank.deps_for_write())
                        pv = bank.ap.bitcast(bf)
                        for c in range(8):
                            ins = PE.e.transpose(out=pv[0:64, c * 128:(c + 1) * 128], in_=srcT[:, (c0 + c) * 64:(c0 + c + 1) * 64], identity=ident)
                        tp = PE.op(ins)
                        bank.wrote(tp)
                        tev = dvo(lambda: DVE.e.tensor_copy(out=dstK[:, c0:c0 + 8, :], in_=pv[0:64, 0:1024].rearrange("p (c e) -> p c e", e=128)), tp)
                        bank.read(tev)
                tO = None
                for c in range(NCH):
                    ch = slice(c * 64, (c + 1) * 64)
                    b1 = kb.pring.next()
                    tp1 = kb.mm_group(b1, [(KT[:, ch], QT[:, ch])], [tdv, tev], out_ap=b1.ap[0:64, 0:64])
                    sc = scm.next()
                    tm = dvo(lambda: DVE.e.tensor_tensor(out=sc.ap, in0=b1.ap[0:64, 0:64], in1=m64, op=ALU.mult), tp1, *sc.deps_for_write())
                    b1.read(tm)
                    sc.wrote(tm)
                    b2 = kb.pring.next()
                    tp2 = kb.mm_group(b2, [(QH[:, ch], Sbf[cur]), (sc.ap, vtok[:, c, :])], [tm, tS, tev], out_ap=b2.ap[0:64, 0:128])
                    sc.read(tp2)
                    b3_ = kb.pring.next()
                    tp3 = kb.mm_group(b3_, [(khtok[:, c, :], vtok[:, c, :])], [tev], out_ap=b3_.ap[:, 0:128])
                    tu = dvo(lambda: DVE.e.scalar_tensor_tensor(out=S32, in0=S32, scalar=EBL[:, c:c + 1], in1=b3_.ap[:, 0:128], op0=ALU.mult, op1=ALU.add), tp3, tel)
                    b3_.read(tu)
                    cur = 1 - cur
                    tS = aco(lambda: ACT.e.activation(out=Sbf[cur], in_=S32, func=AF.Copy), tu, tp2)
                    tO = aco(lambda: ACT.e.activation(out=Ot[:, c, :], in_=b2.ap[0:64, 0:128], func=AF.Copy), tp2)
                    b2.read(tO)
                t = dvo(lambda: DVE.e.tensor_tensor(out=osq, in0=Ot, in1=Ot, op=ALU.mult), tO)
                t = dvo(lambda: DVE.e.tensor_reduce(out=oss, in_=osq, axis=AX.X, op=ALU.add), t)
                t = dvo(lambda: DVE.e.tensor_scalar(out=oss, in0=oss, scalar1=1.0 / 128, scalar2=RMS_EPS, op0=ALU.mult, op1=ALU.add), t)
                t = aco(lambda: ACT.e.activation(out=oss, in_=oss, func=AF.Sqrt), t)
                t = dvo(lambda: DVE.e.reciprocal(out=oss, in_=oss), t)
                t = dvo(lambda: DVE.e.tensor_tensor(out=On, in0=Ot, in1=oss.to_broadcast([64, NCH, 128]), op=ALU.mult), t)
                ty = None
                for c0 in range(0, NCH, 8):
                    bank = kb.pring.next()
                    PE.wait(t, *bank.deps_for_write())
                    pv = bank.ap.bitcast(bf)
                    for c in range(8):
                        ins = PE.e.transpose(out=pv[:, c * 64:(c + 1) * 64], in_=On[:, c0 + c, :], identity=ident[0:64, 0:64])
                    tp = PE.op(ins)
                    bank.wrote(tp)
                    ty = dvo(lambda: DVE.e.scalar_tensor_tensor(out=YC.ap[:, c0 * 64:(c0 + 8) * 64], in0=pv[:, 0:512], scalar=hnw[:, l:l + 1], in1=QF[:, c0 * 64:(c0 + 8) * 64], op0=ALU.mult, op1=ALU.mult),
                             tp, tsg, *(YC.deps_for_write() if c0 == 0 else []))
                    bank.read(ty)
                YC.wrote(ty)
                tst = kb.dmar(POOL, lambda r: MIXT[rsl(r, 1536 + hh * 128, 128, HHL * 128), tok], lambda r: YC.ap, [ty], YC.sem)
                YC.read(tst)
                tlast = [(ACT.sem, ACT.sem.n), (DVE.sem, DVE.sem.n), (PE.sem, PE.sem.n)]
        kb.barrier()
        kb.sb_release(mk_)

    kb.hgrn = hgrn
    kb.sweep = sweep
    kb.loc = locals()
    return kb, kb.loc


def finish(kb):
    kb.barrier()


NONCE = 1234567
FLIP = 0


def _layout_inputs(inp, L, S, b, r=0, pair=False):
    f32 = np.float32
    SL = S // 2 if pair else S
    lsel = slice(r, None, 2) if pair else slice(None)
    hsel = slice(2 * r, 2 * r + 2) if pair else slice(None)
    HHL = 2 if pair else HH

    def pc(a, n):
        a = np.asarray(a, f32).reshape(L, n, 128)
        return np.ascontiguousarray(a.transpose(2, 0, 1).reshape(128, L * n))

    cw = np.asarray(inp["conv_w"], f32).reshape(L, 31, 4, 128).transpose(3, 0, 2, 1).reshape(128, L * 4 * 31)
    lb = np.asarray(inp["hgrn_lower_bounds"], f32).reshape(L, HH, 128)[:, hsel].transpose(2, 1, 0).reshape(128, HHL * L)
    cm = np.zeros((4, 128, 512), f32)
    p = np.arange(128)[:, None]
    f = np.arange(512)[None, :]
    for q in range(4):
        cm[q] = (p + 128 * q <= f)
    invf = (10000.0 ** (-np.arange(0, 64, 2, dtype=f32) / 64)).astype(f32)
    invf2 = np.concatenate([invf, invf])
    w = lambda k: np.ascontiguousarray(np.asarray(inp[k], f32)[lsel])
    m = {
        "x": np.ascontiguousarray(np.asarray(inp["x"], f32)[b, r * SL:(r + 1) * SL]),
        "pos": np.ascontiguousarray(np.asarray(inp["positions"], np.int32)[b, r * SL:(r + 1) * SL][None, :]),
        "w_in": w("w_in"), "w_uq": w("w_uq"), "w_ukv": w("w_ukv"), "w_out": w("w_out"),
        "w_gate": w("w_gate"), "w_up": w("w_up"), "w_down": w("w_down"),
        "anw": pc(inp["attn_norm_w"], 16), "fnw": pc(inp["ffn_norm_w"], 16),
        "finw": np.asarray(inp["final_norm_w"], f32)[None, :].copy(),
        "qnw": pc(inp["q_norm_w"], 4), "kvnw": pc(inp["kv_norm_w"], 4),
        "monw": pc(inp["mla_out_norm_w"], 1), "hnw": pc(inp["hgrn_norm_w"], 1),
        "cw": np.ascontiguousarray(cw), "cb": pc(inp["conv_b"], 4),
        "lnw": pc(inp["conv_ln_w"], 4), "lnb": pc(inp["conv_ln_b"], 4),
        "lbraw": np.ascontiguousarray(lb),
        "ident": np.eye(128, dtype=f32),
        "cmask": np.ascontiguousarray(cm.transpose(1, 0, 2).reshape(128, 4 * 512)),
        "m64": (np.arange(64)[:, None] <= np.arange(64)[None, :]).astype(f32),
        "invf": np.stack([invf2, invf2], axis=1).astype(f32),
        "rankf": np.full((128, 1), float(r), f32),
        "fv": np.full((1, 8), NONCE, np.int32),
        "zz": np.zeros((1, 64), np.int32),
    }
    return m


def build_full(S, L, debug=False, pair=False):
    kb, _ = build(S, L, debug, pair, NONCE)
    kb.sweep(0)
    for l in range(L):
        kb.mla(l)
        kb.conv(l)
        kb.hgrn(l)
        kb.pair_barrier()
        kb.sweep(l + 1)
    if pair:
        t = kb.dmar(kb.POOL, lambda r: kb.FL[r:r + 1], lambda r: kb.loc["zz_d"], [], kb.flsem)
    finish(kb)
    return kb


def kernel(**inputs):
    x = np.asarray(inputs["x"])
    B, S, _ = x.shape
    L = np.asarray(inputs["w_in"]).shape[0]
    pair = True
    kb = build_full(S, L, pair=pair)
    in_maps = [_layout_inputs(inputs, L, S, b, r, pair) for b in range(B) for r in range(2)]
    res = run_bass_kernel_spmd(kb.nc, in_maps, core_ids=list(range(2 * B)))
    outs = [np.concatenate([np.asarray(res.results[2 * b + r]["out"]) for r in range(2)], axis=0) for b in range(B)]
    return np.stack(outs, axis=0).astype(np.float32)
```

```python
import numpy as np
import ml_dtypes
import concourse.bass as bass
import concourse.mybir as mybir
from concourse.bass_utils import run_bass_kernel_spmd

F32, BF16, I32 = mybir.dt.float32, mybir.dt.bfloat16, mybir.dt.int32
ALU = mybir.AluOpType
AF = mybir.ActivationFunctionType
AX = mybir.AxisListType

D = 2048
DFF = 5632
NH = 8
HH = 4
RMS_EPS = 1e-6
LN_EPS = 1e-5
PROJ_ROWS = 4224
R_CQ, R_CKV, R_CONV, R_HQ, R_HF, R_HI, R_HG, R_KR = 0, 512, 1024, 2048, 2560, 3072, 3584, 4096
TT = 512
TWO_PI = 6.283185307179586


class Sem:
    def __init__(self, nc, name):
        self.h = nc.alloc_semaphore(name)
        self.n = 0


class Eng:
    def __init__(self, nc, e, name):
        self.e = e
        self.sem = Sem(nc, "s_" + name)
        self.seen = {}

    def wait(self, *deps):
        for d in deps:
            if d is None:
                continue
            if isinstance(d, list):
                self.wait(*d)
                continue
            sem, val = d
            if self.seen.get(sem, 0) < val:
                self.e.wait_ge(sem.h, val)
                self.seen[sem] = val

    def op(self, ins):
        ins.then_inc(self.sem.h, 1)
        self.sem.n += 1
        return (self.sem, self.sem.n)


class Slot:
    def __init__(self, ap, sem=None):
        self.ap = ap
        self.sem = sem
        self.w = None
        self.r = []

    def deps_for_write(self):
        return [self.w] + self.r

    def wrote(self, t):
        self.w = t
        self.r = []

    def read(self, t):
        self.r.append(t)
        if len(self.r) > 6:
            self.r = self.r[-6:]


class Ring:
    def __init__(self, slots):
        self.slots = slots
        self.i = 0

    def next(self):
        s = self.slots[self.i % len(self.slots)]
        self.i += 1
        return s


class KB:
    def __init__(self, S, L, debug=False, pair=False):
        self.S, self.L, self.debug, self.pair = S, L, debug, pair
        self.NT = (S // 2 if pair else S) // TT
        self.nbar = 0
        nc = self.nc = bass.Bass("TRN2", target_bir_lowering=False)
        self.PE = Eng(nc, nc.tensor, "pe")
        self.ACT = Eng(nc, nc.scalar, "act")
        self.DVE = Eng(nc, nc.vector, "dve")
        self.POOL = Eng(nc, nc.gpsimd, "pool")
        self.SP = Eng(nc, nc.sync, "sp")
        self.engs = [self.PE, self.ACT, self.DVE, self.POOL, self.SP]
        self.dsems = []
        self.nsem = 0
        self.sb_stack = []
        self.phase_sems = []
        self.banks = [Slot(nc.alloc_psum_tensor(f"bank{i}", [128, 512], F32).ap()) for i in range(8)]
        self.pring = Ring(self.banks)

    def dsem(self, name=None):
        if getattr(self, "free_sems", None):
            s = self.free_sems.pop()
        else:
            self.nsem += 1
            s = Sem(self.nc, f"d{self.nsem}")
            self.dsems.append(s)
        self.phase_sems.append(s)
        return s

    def sem_mark(self):
        return len(self.phase_sems)

    def sem_release(self, mark):
        if not hasattr(self, "free_sems"):
            self.free_sems = []
        while len(self.phase_sems) > mark:
            self.free_sems.append(self.phase_sems.pop())

    def dma(self, q, out, in_, deps, sem):
        q.wait(*deps)
        ins = q.e.dma_start(out=out, in_=in_)
        ins.then_inc(sem.h, 16)
        sem.n += 16
        return (sem, sem.n)

    def dmar(self, q, out_fn, in_fn, deps, sem):
        if not self.pair:
            return self.dma(q, out_fn(0), in_fn(0), deps, sem)
        q.wait(*deps)
        with q.e.If(self.half == 0):
            q.e.dma_start(out=out_fn(0), in_=in_fn(0)).then_inc(sem.h, 16)
        with q.e.Else():
            q.e.dma_start(out=out_fn(1), in_=in_fn(1)).then_inc(sem.h, 16)
        sem.n += 16
        return (sem, sem.n)

    def barrier(self):
        ts = [(e.sem, e.sem.n) for e in self.engs if e.sem.n > 0]
        ts += [(s, s.n) for s in self.dsems if s.n > 0]
        for e in self.engs:
            e.wait(*ts)
        for b in self.banks:
            b.w = None
            b.r = []

    def sb(self, name, shape, dt):
        self.nsem += 1
        g = self.nc.sbuf_tensor(f"sb{self.nsem}_{name}", list(shape), dt)
        t = g.__enter__()
        self.sb_stack.append(g)
        return t.ap() if hasattr(t, "ap") else t

    def sb_mark(self):
        return (len(self.sb_stack), len(self.phase_sems))

    def sb_release(self, mark):
        mark, smark = mark
        while len(self.sb_stack) > mark:
            g = self.sb_stack.pop()
            g.__exit__(None, None, None)
        self.sem_release(smark)

    def dram(self, name, shape, dt, kind="Internal", shared=False):
        if shared and self.pair and kind == "Internal":
            return self.nc.dram_tensor(name, list(shape), dt, kind=kind, addr_space="Shared").ap()
        return self.nc.dram_tensor(name, list(shape), dt, kind=kind).ap()

    def pair_barrier(self):
        self.barrier()
        if not self.pair:
            return
        nc = self.nc
        POOL = self.POOL
        k = self.nbar
        self.nbar += 1
        PT = mybir.EngineType.Pool
        t = self.dmar(POOL, lambda r: self.FL[r:r + 1, k:k + 1], lambda r: self.fv[0:1, 0:1], [], self.flsem)
        POOL.wait(t)
        for r in (0, 1):
            lid = nc.next_id()
            ls, le = f"spin_{lid}_loop", f"spin_{lid}_end"
            regs = nc.alloc_registers(f"spin_{lid}_r", engines=[PT])
            nc.br(ls, engines=[PT])
            with nc.body(ls, valid_engines=[PT]):
                nc.reg_load(regs, self.FL[r:r + 1, k:k + 1])
                nc.br_ne(regs, self.nonce, on_true=ls, on_false=le, engines=[PT])
            nc.switch_bb(le)
        t = POOL.op(POOL.e.memset(self.bdummy, 0.0))
        for e in self.engs:
            e.wait(t)

    def mm_group(self, bank, pairs, deps, out_ap=None):
        PE = self.PE
        PE.wait(*deps, *bank.deps_for_write())
        o = bank.ap if out_ap is None else out_ap
        n = len(pairs)
        ins = None
        for i, (l, r) in enumerate(pairs):
            ins = PE.e.matmul(o, lhsT=l, rhs=r, start=(i == 0), stop=(i == n - 1))
        t = PE.op(ins)
        bank.wrote(t)
        return t


def build(S, L, debug=False, pair=False, nonce=12345):
    kb = KB(S, L, debug, pair)
    nc = kb.nc
    PE, ACT, DVE, POOL, SP = kb.PE, kb.ACT, kb.DVE, kb.POOL, kb.SP
    NT = kb.NT
    SL = NT * TT
    NB = S // 128
    L2 = L // 2 if pair else L
    NHL = NH // 2 if pair else NH
    HHL = HH // 2 if pair else HH
    bf = BF16
    kb.nonce = nonce
    if pair:
        half = (nc.partition_id() + FLIP) % 2
    else:
        half = 0
    kb.half = half


    def rsl(r, off, n, mult):
        return slice(r * mult + off, r * mult + off + n)

    x = kb.dram("x", [SL, D], F32, "ExternalInput")
    pos = kb.dram("pos", [1, SL], I32, "ExternalInput")
    w_in = kb.dram("w_in", [L2, D, 4160], F32, "ExternalInput")
    w_uq = kb.dram("w_uq", [L2, 512, 1536], F32, "ExternalInput")
    w_ukv = kb.dram("w_ukv", [L2, 512, 2048], F32, "ExternalInput")
    w_out = kb.dram("w_out", [L2, D, D], F32, "ExternalInput")
    w_gate = kb.dram("w_gate", [L2, D, DFF], F32, "ExternalInput")
    w_up = kb.dram("w_up", [L2, D, DFF], F32, "ExternalInput")
    w_down = kb.dram("w_down", [L2, DFF, D], F32, "ExternalInput")
    anw_d = kb.dram("anw", [128, L * 16], F32, "ExternalInput")
    fnw_d = kb.dram("fnw", [128, L * 16], F32, "ExternalInput")
    finw_d = kb.dram("finw", [1, D], F32, "ExternalInput")
    qnw_d = kb.dram("qnw", [128, L * 4], F32, "ExternalInput")
    kvnw_d = kb.dram("kvnw", [128, L * 4], F32, "ExternalInput")
    monw_d = kb.dram("monw", [128, L], F32, "ExternalInput")
    hnw_d = kb.dram("hnw", [128, L], F32, "ExternalInput")
    cw_d = kb.dram("cw", [128, L * 4 * 31], F32, "ExternalInput")
    cb_d = kb.dram("cb", [128, L * 4], F32, "ExternalInput")
    lnw_d = kb.dram("lnw", [128, L * 4], F32, "ExternalInput")
    lnb_d = kb.dram("lnb", [128, L * 4], F32, "ExternalInput")
    lb_d = kb.dram("lbraw", [128, HHL * L], F32, "ExternalInput")
    rankf_d = kb.dram("rankf", [128, 1], F32, "ExternalInput")
    kb.fv = kb.dram("fv", [1, 8], I32, "ExternalInput")
    zz_d = kb.dram("zz", [1, 64], I32, "ExternalInput")
    ident_d = kb.dram("ident", [128, 128], F32, "ExternalInput")
    cmask_d = kb.dram("cmask", [128, 4 * 512], F32, "ExternalInput")
    m64_d = kb.dram("m64", [64, 64], F32, "ExternalInput")
    invf_d = kb.dram("invf", [64, 2], F32, "ExternalInput")
    out = kb.dram("out", [SL, D], F32, "ExternalOutput")

    dk = "ExternalOutput" if debug else "Internal"
    PROJT = kb.dram("projt", [PROJ_ROWS, S], bf, dk, shared=True)
    MIXT = kb.dram("mixt", [D, S], bf, dk, shared=True)
    H = kb.dram("hres", [SL, D], F32, dk)
    QN = kb.dram("qn", [NH, 128, S], bf, dk, shared=True)
    QP = kb.dram("qp", [NH, 64, S], bf, dk, shared=True)
    KN = kb.dram("kn", [NH, 128, S], bf, dk, shared=True)
    KP = kb.dram("kp", [64, S], bf, dk, shared=True)
    VS = kb.dram("vs", [NB, 128, NH, 129], bf, dk, shared=True)
    COS = kb.dram("cosd", [64, SL], F32, dk)
    SIN = kb.dram("sind", [64, SL], F32, dk)
    WBin = kb.dram("wbin", [L, D, PROJ_ROWS], bf, dk, shared=True)
    WBuq = kb.dram("wbuq", [L, 512, 2048], bf, dk, shared=True)
    WBukv = kb.dram("wbukv", [L, 512, 2048], bf, shared=True)
    WBout = kb.dram("wbout", [L, D, D], bf, shared=True)
    WBg = kb.dram("wbg", [L, D, DFF], bf, shared=True)
    WBu = kb.dram("wbu", [L, D, DFF], bf, shared=True)
    WBd = kb.dram("wbd", [L, DFF, D], bf, shared=True)

    wsem = kb.dsem("wcast")

    cast_hist = []

    def cast2d(W, i, cs, src, rows, view=None):
        step = 256
        W2 = W.rearrange("l r c -> (l r) c")
        nr = W.shape[1]
        for r0 in range(0, rows, step):
            def dst(r):
                li = 2 * i + r if pair else i
                rs = slice(li * nr + r0, li * nr + r0 + step)
                return W2[rs, cs] if view is None else view(W2[rs])
            tk = kb.dmar(POOL, dst, lambda r: src[r0:r0 + step], [], wsem)

    if pair:
        kb.FL = kb.dram("flags", [2, 64], I32, shared=True)
        kb.flsem = kb.dsem("flag")
        t = kb.dmar(POOL, lambda r: kb.FL[r:r + 1], lambda r: zz_d, [], kb.flsem)
        POOL.wait(t)

    for i in range(L2):
        cast2d(WBin, i, slice(0, 1024), w_in[i][:, 0:1024], D)
        cast2d(WBin, i, slice(1024, 4096), w_in[i][:, 1088:4160], D)
        cast2d(WBin, i, slice(4096, 4160), w_in[i][:, 1024:1088], D)
        cast2d(WBin, i, slice(4160, 4192), w_in[i][:, 1056:1088], D)
        cast2d(WBin, i, slice(4192, 4224), w_in[i][:, 1024:1056], D)
        sq = w_uq[i].rearrange("k (h c) -> k h c", c=192)
        cast2d(WBuq, i, None, sq[:, :, 0:192], 512, view=lambda a_: a_.rearrange("k (h c) -> k h c", c=256)[:, :, 0:192])
        cast2d(WBuq, i, None, sq[:, :, 160:192], 512, view=lambda a_: a_.rearrange("k (h c) -> k h c", c=256)[:, :, 192:224])
        cast2d(WBuq, i, None, sq[:, :, 128:160], 512, view=lambda a_: a_.rearrange("k (h c) -> k h c", c=256)[:, :, 224:256])
        cast2d(WBukv, i, slice(0, 2048), w_ukv[i], 512)
        cast2d(WBout, i, slice(0, D), w_out[i], D)
        cast2d(WBg, i, slice(0, DFF), w_gate[i], D)
        cast2d(WBu, i, slice(0, DFF), w_up[i], D)
        cast2d(WBd, i, slice(0, D), w_down[i], DFF)

    csem = kb.dsem("const")
    ident = kb.sb("ident", [128, 128], bf)
    ident32 = kb.sb("ident32", [128, 128], F32)
    ones_bf = kb.sb("ones_bf", [128, 128], bf)
    ones32 = kb.sb("ones32", [128, 128], F32)
    anw = kb.sb("anw", [128, L * 16], F32)
    fnw = kb.sb("fnw", [128, L * 16], F32)
    qnw = kb.sb("qnw", [128, L * 4], F32)
    kvnw = kb.sb("kvnw", [128, L * 4], F32)
    monw = kb.sb("monw", [128, L], F32)
    hnw = kb.sb("hnw", [128, L], F32)
    cb = kb.sb("cb", [128, L * 4], F32)
    lnw = kb.sb("lnw", [128, L * 4], F32)
    lnb = kb.sb("lnb", [128, L * 4], F32)
    lbt = kb.sb("lbt", [128, HHL, L], F32)
    oml = kb.sb("oml", [128, HHL, L], F32)
    rankf = kb.sb("rankf", [128, 1], F32)
    kb.bdummy = kb.sb("bdummy", [128, 1], F32)
    m64 = kb.sb("m64", [64, 64], F32)
    invf = kb.sb("invf", [64, 2], F32)
    kb.dma(POOL, ident, ident_d, [], csem)
    kb.dma(SP, ident32, ident_d, [], csem)
    for (a, b_) in [(anw, anw_d), (fnw, fnw_d), (qnw, qnw_d), (kvnw, kvnw_d), (monw, monw_d), (hnw, hnw_d),
                    (cb, cb_d), (lnw, lnw_d), (lnb, lnb_d), (lbt.rearrange("p h l -> p (h l)"), lb_d),
                    (m64, m64_d), (invf, invf_d), (rankf, rankf_d)]:
        kb.dma(SP, a, b_, [], csem)
    tconst = (csem, csem.n)
    t = DVE.op(DVE.e.memset(ones_bf, 1.0))
    t = DVE.op(DVE.e.memset(ones32, 1.0))
    DVE.wait(tconst)
    ACT.wait(tconst)
    t = ACT.op(ACT.e.activation(out=lbt, in_=lbt, func=AF.Exp))
    lsum = kb.sb("lsum", [128, HHL, 1], F32)
    DVE.wait(t)
    t = DVE.op(DVE.e.tensor_reduce(out=lsum, in_=lbt, axis=AX.X, op=ALU.add))
    DVE.wait(t)
    t = DVE.op(DVE.e.reciprocal(out=lsum, in_=lsum))
    DVE.wait(t)
    t = DVE.op(DVE.e.tensor_tensor(out=lbt, in0=lbt, in1=lsum.to_broadcast([128, HHL, L]), op=ALU.mult))
    DVE.wait(t)
    t = DVE.op(DVE.e.memset(lbt[:, :, 0:1], 0.0))
    for l in range(2, L):
        DVE.wait(t)
        t = DVE.op(DVE.e.tensor_tensor(out=lbt[:, :, l:l + 1], in0=lbt[:, :, l:l + 1], in1=lbt[:, :, l - 1:l], op=ALU.add))
    DVE.wait(t)
    t = DVE.op(DVE.e.tensor_scalar(out=oml, in0=lbt, scalar1=-1.0, scalar2=1.0, op0=ALU.mult, op1=ALU.add))
    t_lb = t

    mk = kb.sb_mark()
    RC = min(SL, 2048)
    posi = kb.sb("posi", [64, RC], I32)
    ang = kb.sb("ang", [64, RC], F32)
    ang2 = kb.sb("ang2", [64, RC], F32)
    rsem = kb.dsem("rope")
    rst = kb.dsem("ropest")
    tl = None
    mk1 = kb.sb("mk1", [64, RC], F32)
    ki = kb.sb("ki", [64, RC], I32)
    C1 = 6.28125
    C2 = TWO_PI - 6.28125
    PI = float(np.pi)

    def dv(ins, *deps):
        DVE.wait(*deps)
        return DVE.op(ins())

    for r0 in range(0, SL, RC):
        tl = kb.dma(SP, posi, pos[0:1, r0:r0 + RC].partition_broadcast(64), [tl, (rst, rst.n), (DVE.sem, DVE.sem.n)], rsem)
        DVE.wait(tl, tconst)
        t = DVE.op(DVE.e.tensor_copy(out=ang, in_=posi))
        t = dv(lambda: DVE.e.tensor_scalar(out=ang, in0=ang, scalar1=invf[:, 0:1], scalar2=None, op0=ALU.mult), t)
        t = dv(lambda: DVE.e.tensor_scalar(out=ang2, in0=ang, scalar1=1.0 / TWO_PI, scalar2=None, op0=ALU.mult), t)
        t = dv(lambda: DVE.e.tensor_copy(out=ki, in_=ang2), t)
        t = dv(lambda: DVE.e.tensor_copy(out=ang2, in_=ki), t)
        t = dv(lambda: DVE.e.scalar_tensor_tensor(out=ang, in0=ang2, scalar=-C1, in1=ang, op0=ALU.mult, op1=ALU.add), t)
        t = dv(lambda: DVE.e.scalar_tensor_tensor(out=ang, in0=ang2, scalar=-C2, in1=ang, op0=ALU.mult, op1=ALU.add), t)
        t = dv(lambda: DVE.e.tensor_scalar(out=mk1, in0=ang, scalar1=PI, scalar2=None, op0=ALU.is_gt), t)
        t = dv(lambda: DVE.e.scalar_tensor_tensor(out=ang, in0=mk1, scalar=-TWO_PI, in1=ang, op0=ALU.mult, op1=ALU.add), t)
        t = dv(lambda: DVE.e.tensor_scalar(out=mk1, in0=ang, scalar1=-PI, scalar2=None, op0=ALU.is_lt), t)
        t = dv(lambda: DVE.e.scalar_tensor_tensor(out=ang, in0=mk1, scalar=TWO_PI, in1=ang, op0=ALU.mult, op1=ALU.add), t)
        t = dv(lambda: DVE.e.tensor_scalar(out=ang2, in0=ang, scalar1=PI / 2, scalar2=None, op0=ALU.add), t)
        t = dv(lambda: DVE.e.tensor_scalar(out=mk1, in0=ang2, scalar1=PI, scalar2=None, op0=ALU.is_gt), t)
        t = dv(lambda: DVE.e.scalar_tensor_tensor(out=ang2, in0=mk1, scalar=-TWO_PI, in1=ang2, op0=ALU.mult, op1=ALU.add), t)
        t = dv(lambda: DVE.e.tensor_scalar(out=ang, in0=ang, scalar1=-PI, scalar2=PI, op0=ALU.max, op1=ALU.min), t)
        t = dv(lambda: DVE.e.tensor_scalar(out=ang2, in0=ang2, scalar1=-PI, scalar2=PI, op0=ALU.max, op1=ALU.min), t)
        ACT.wait(t)
        t2 = ACT.op(ACT.e.activation(out=ang, in_=ang, func=AF.Sin))
        t2 = ACT.op(ACT.e.activation(out=ang2, in_=ang2, func=AF.Sin))
        t3 = dv(lambda: DVE.e.tensor_scalar(out=ang[0:32], in0=ang[0:32], scalar1=-1.0, scalar2=None, op0=ALU.mult), t2)
        kb.dma(POOL, SIN[:, r0:r0 + RC], ang, [t3], rst)
        kb.dma(POOL, COS[:, r0:r0 + RC], ang2, [t2], rst)
    kb.pair_barrier()
    kb.sb_release(mk)

    def dvo(fn, *deps):
        DVE.wait(*deps)
        return DVE.op(fn())

    def aco(fn, *deps):
        ACT.wait(*deps)
        return ACT.op(fn())

    def poo(fn, *deps):
        POOL.wait(*deps)
        return POOL.op(fn())

    def sweep(l):
        mk_ = kb.sb_mark()
        h = Slot(kb.sb("h", [128, 4, D], F32), kb.dsem())
        xs = Slot(kb.sb("xs", [128, 4, D], bf))
        actT = Slot(kb.sb("actT", [128, 16, TT], bf), kb.dsem())
        junk = kb.sb("junk", [128, D], bf)
        ss = kb.sb("ss", [128, 4], F32)
        rstd = kb.sb("rstd", [128, 4], F32)
        wring = Ring([Slot(kb.sb(f"w{i}", [128, 16, 512], bf), kb.dsem()) for i in range(4)])
        stg = Ring([Slot(kb.sb(f"stg{i}", [128, 4, TT], bf), kb.dsem()) for i in range(2)])
        hst = kb.dsem()
        if l > 0:
            ffT = Slot(kb.sb("ffT", [128, 44, TT], bf))
            sg = Ring([Slot(kb.sb(f"sg{i}", [128, TT], F32)) for i in range(2)])
        if l == L:
            finw = kb.sb("finw", [128, D], F32)
            tfin = kb.dma(SP, finw, finw_d.partition_broadcast(128), [], kb.dsem())
            ostg = Slot(kb.sb("ostg", [128, D], F32), kb.dsem())
        evk = [0]

        def evac_copy(out_ap, in_ap, deps, scale=None):
            evk[0] += 1
            if evk[0] % 2 == 0:
                if scale is None:
                    return aco(lambda: ACT.e.activation(out=out_ap, in_=in_ap, func=AF.Copy), *deps)
                return aco(lambda: ACT.e.activation(out=out_ap, in_=in_ap, func=AF.Copy, scale=scale), *deps)
            if scale is None:
                return dvo(lambda: DVE.e.tensor_copy(out=out_ap, in_=in_ap), *deps)
            return dvo(lambda: DVE.e.tensor_scalar(out=out_ap, in0=in_ap, scalar1=scale, scalar2=None, op0=ALU.mult), *deps)

        def load_w(src3, k, ncols):
            sl = wring.next()
            t = kb.dma(SP, sl.ap[:, 0:k, 0:ncols], src3, sl.deps_for_write(), sl.sem)
            sl.wrote(t)
            return sl

        def norm_to_actT(nw_col0, nw):
            t = dvo(lambda: DVE.e.memset(ss, 0.0), h.w)
            for tb in range(4):
                t = aco(lambda: ACT.e.activation(out=junk, in_=h.ap[:, tb, :], func=AF.Square, accum_out=ss[:, tb:tb + 1]), t, h.w)
            t = dvo(lambda: DVE.e.tensor_scalar(out=rstd, in0=ss, scalar1=1.0 / D, scalar2=RMS_EPS, op0=ALU.mult, op1=ALU.add), t)
            t = aco(lambda: ACT.e.activation(out=rstd, in_=rstd, func=AF.Sqrt), t)
            t = dvo(lambda: DVE.e.reciprocal(out=rstd, in_=rstd), t)
            txs = None
            for tb in range(4):
                txs = dvo(lambda: DVE.e.tensor_scalar(out=xs.ap[:, tb, :], in0=h.ap[:, tb, :], scalar1=rstd[:, tb:tb + 1], scalar2=None, op0=ALU.mult), t, *xs.deps_for_write())
            xs.wrote(txs)
            tl = None
            for c in range(16):
                bank = kb.pring.next()
                PE.wait(txs, tconst, *bank.deps_for_write())
                pv = bank.ap.bitcast(bf)
                for tb in range(4):
                    ins = PE.e.transpose(out=pv[:, tb * 128:(tb + 1) * 128], in_=xs.ap[:, tb, c * 128:(c + 1) * 128], identity=ident)
                tp = PE.op(ins)
                bank.wrote(tp)
                te = evac_copy(actT.ap[:, c, :], pv[:, 0:TT], [tp] + (actT.deps_for_write() if c == 0 else []), scale=nw[:, nw_col0 + c:nw_col0 + c + 1])
                bank.read(te)
                tl = te
            xs.read(tp)
            actT.wrote(None)
            actT.w = [(ACT.sem, ACT.sem.n), (DVE.sem, DVE.sem.n)]
            return actT.w

        for j in range(NT):
            tok = slice(j * TT, (j + 1) * TT)
            gtok = lambda r: rsl(r, j * TT, TT, SL)
            src = x if l <= 1 else H
            th = kb.dma(SP, h.ap, src[tok, :].rearrange("(tb p) d -> p tb d", p=128), h.deps_for_write() + [(hst, hst.n)], h.sem)
            h.wrote(th)
            if l > 0:
                lw = l - 1
                tm = kb.dmar(SP, lambda r: actT.ap, lambda r: MIXT[:, gtok(r)].rearrange("(c p) t -> p c t", p=128), actT.deps_for_write(), actT.sem)
                actT.wrote(tm)
                tacc = None
                for cg in range(4):
                    sl = load_w(WBout[lw][:, cg * 512:(cg + 1) * 512].rearrange("(c p) n -> p c n", p=128), 16, 512)
                    for tb in range(4):
                        bank = kb.pring.next()
                        tp = kb.mm_group(bank, [(actT.ap[:, c, tb * 128:(tb + 1) * 128], sl.ap[:, c, :]) for c in range(16)], [sl.w, tm])
                        hv = h.ap[:, tb, cg * 512:(cg + 1) * 512]
                        tacc = dvo(lambda: DVE.e.tensor_tensor(out=hv, in0=hv, in1=bank.ap, op=ALU.add), tp, th)
                        bank.read(tacc)
                    sl.read(tp)
                actT.read(tp)
                h.wrote(tacc)
                ta = norm_to_actT(lw * 16, fnw)
                tff = None
                for g in range(11):
                    slg = load_w(WBg[lw][:, g * 512:(g + 1) * 512].rearrange("(c p) n -> p c n", p=128), 16, 512)
                    slu = load_w(WBu[lw][:, g * 512:(g + 1) * 512].rearrange("(c p) n -> p c n", p=128), 16, 512)
                    for q in range(4):
                        bg = kb.pring.next()
                        tg = kb.mm_group(bg, [(slg.ap[:, c, q * 128:(q + 1) * 128], actT.ap[:, c, :]) for c in range(16)], [slg.w] + ta)
                        bu = kb.pring.next()
                        tu = kb.mm_group(bu, [(slu.ap[:, c, q * 128:(q + 1) * 128], actT.ap[:, c, :]) for c in range(16)], [slu.w] + ta)
                        sgs = sg.next()
                        ts_ = aco(lambda: ACT.e.activation(out=sgs.ap, in_=bg.ap, func=AF.Silu), tg, *sgs.deps_for_write())
                        sgs.wrote(ts_)
                        bg.read(ts_)
                        tff = dvo(lambda: DVE.e.tensor_tensor(out=ffT.ap[:, g * 4 + q, :], in0=sgs.ap, in1=bu.ap, op=ALU.mult), ts_, tu, *(ffT.deps_for_write() if (g == 0 and q == 0) else []))
                        sgs.read(tff)
                        bu.read(tff)
                    slg.read(tg)
                    slu.read(tu)
                actT.read(tu)
                ffT.wrote(tff)
                tacc = None
                for cg in range(4):
                    sls = []
                    for (c0, k) in [(0, 16), (16, 16), (32, 12)]:
                        sls.append((c0, k, load_w(WBd[lw][c0 * 128:(c0 + k) * 128, cg * 512:(cg + 1) * 512].rearrange("(c p) n -> p c n", p=128), k, 512)))
                    for tb in range(4):
                        bank = kb.pring.next()
                        pairs = []
                        for (c0, k, sl) in sls:
                            pairs += [(ffT.ap[:, c0 + c, tb * 128:(tb + 1) * 128], sl.ap[:, c, :]) for c in range(k)]
                        tp = kb.mm_group(bank, pairs, [sl.w for (_, _, sl) in sls] + [tff])
                        hv = h.ap[:, tb, cg * 512:(cg + 1) * 512]
                        tacc = dvo(lambda: DVE.e.tensor_tensor(out=hv, in0=hv, in1=bank.ap, op=ALU.add), tp, h.w)
                        bank.read(tacc)
                    for (_, _, sl) in sls:
                        sl.read(tp)
                ffT.read(tp)
                h.wrote(tacc)
            if l == L:
                t = dvo(lambda: DVE.e.memset(ss, 0.0), h.w)
                for tb in range(4):
                    t = aco(lambda: ACT.e.activation(out=junk, in_=h.ap[:, tb, :], func=AF.Square, accum_out=ss[:, tb:tb + 1]), t, h.w)
                t = dvo(lambda: DVE.e.tensor_scalar(out=rstd, in0=ss, scalar1=1.0 / D, scalar2=RMS_EPS, op0=ALU.mult, op1=ALU.add), t)
                t = aco(lambda: ACT.e.activation(out=rstd, in_=rstd, func=AF.Sqrt), t)
                t = dvo(lambda: DVE.e.reciprocal(out=rstd, in_=rstd), t)
                for tb in range(4):
                    to = dvo(lambda: DVE.e.scalar_tensor_tensor(out=ostg.ap, in0=h.ap[:, tb, :], scalar=rstd[:, tb:tb + 1], in1=finw, op0=ALU.mult, op1=ALU.mult), t, tfin, *ostg.deps_for_write())
                    tst = kb.dma(POOL, out[j * TT + tb * 128: j * TT + (tb + 1) * 128, :], ostg.ap, [to], ostg.sem)
                    ostg.wrote(to)
                    ostg.read(tst)
                h.read(to)
                continue
            if l > 0:
                tst = kb.dma(POOL, H[tok, :].rearrange("(tb p) d -> p tb d", p=128), h.ap, [h.w], hst)
            ta = norm_to_actT(l * 16, anw)
            h.read(ta[0]); h.read(ta[1])
            for sidx in range(9):
                ncols = 512 if sidx < 8 else 128
                sl = load_w(WBin[l][:, sidx * 512: sidx * 512 + ncols].rearrange("(c p) n -> p c n", p=128), 16, ncols)
                st_ = stg.next()
                te = None
                nq = ncols // 128
                for q in range(nq):
                    bank = kb.pring.next()
                    tp = kb.mm_group(bank, [(sl.ap[:, c, q * 128:(q + 1) * 128], actT.ap[:, c, :]) for c in range(16)], [sl.w] + ta)
                    te = evac_copy(st_.ap[:, q, :], bank.ap, [tp] + (st_.deps_for_write() if q == 0 else []))
                    bank.read(te)
                sl.read(tp)
                st_.wrote(None)
                tst = kb.dmar(POOL, lambda r: PROJT[sidx * 512: sidx * 512 + ncols, gtok(r)].rearrange("(q p) t -> p q t", p=128), lambda r: st_.ap[:, 0:nq, :],
                             [(ACT.sem, ACT.sem.n), (DVE.sem, DVE.sem.n)], st_.sem)
                st_.read(tst)
            actT.read(tp)
        kb.barrier()
        kb.sb_release(mk_)

    def mla(l):
        mk_ = kb.sb_mark()
        wuq = kb.sb("wuq", [128, 4, 2048], bf)
        wukv = kb.sb("wukv", [128, 4, 2048], bf)
        wsm = kb.dsem()
        kb.dma(SP, wuq, WBuq[l].rearrange("(c p) n -> p c n", p=128), [], wsm)
        tw = kb.dma(SP, wukv, WBukv[l].rearrange("(c p) n -> p c n", p=128), [], wsm)
        cq = Slot(kb.sb("cq", [128, 4, TT], bf), kb.dsem())
        ckv = Slot(kb.sb("ckv", [128, 4, TT], bf), kb.dsem())
        kr = Slot(kb.sb("kr", [64, 2, TT], bf), kb.dsem())
        cst = Slot(kb.sb("cst", [64, 2, TT], F32), kb.dsem())
        sq = kb.sb("sq", [128, 4, TT], bf)
        rbc = kb.sb("rbc", [128, TT], F32)
        cqn = Slot(kb.sb("cqn", [128, 4, TT], bf))
        ckvn = Slot(kb.sb("ckvn", [128, 4, TT], bf))
        qn_st = Slot(kb.sb("qn_st", [128, NH, TT], bf), kb.dsem())
        kn_st = Slot(kb.sb("kn_st", [128, NH, TT], bf), kb.dsem())
        qp_st = Slot(kb.sb("qp_st", [64, NH, TT], bf), kb.dsem())
        kp_st = Slot(kb.sb("kp_st", [64, TT], bf), kb.dsem())
        v_st = Slot(kb.sb("v_st", [128, 4, NH, 129], bf), kb.dsem())
        r1 = kb.sb("r1", [64, TT], F32)
        r2 = kb.sb("r2", [64, TT], F32)
        t1s = dvo(lambda: DVE.e.memset(v_st.ap[:, :, :, 128:129], 1.0))
        COSv = cst.ap[:, 0, :]
        SINv = cst.ap[:, 1, :]
        wq3 = wuq.rearrange("p c (h e) -> p c h e", e=256)
        wkv3 = wukv.rearrange("p c (h e) -> p c h e", e=256)

        def normed(src, dst, nw, col0):
            t = dvo(lambda: DVE.e.tensor_tensor(out=sq, in0=src.ap, in1=src.ap, op=ALU.mult), src.w)
            bank = kb.pring.next()
            tp = kb.mm_group(bank, [(ones_bf, sq[:, c, :]) for c in range(4)], [t])
            t = dvo(lambda: DVE.e.tensor_scalar(out=rbc, in0=bank.ap, scalar1=1.0 / 512, scalar2=RMS_EPS, op0=ALU.mult, op1=ALU.add), tp)
            bank.read(t)
            t = aco(lambda: ACT.e.activation(out=rbc, in_=rbc, func=AF.Sqrt), t)
            t = dvo(lambda: DVE.e.reciprocal(out=rbc, in_=rbc), t)
            for c in range(4):
                t2 = dvo(lambda: DVE.e.scalar_tensor_tensor(out=dst.ap[:, c, :], in0=src.ap[:, c, :], scalar=nw[:, col0 + c:col0 + c + 1], in1=rbc, op0=ALU.mult, op1=ALU.mult), t, *(dst.deps_for_write() if c == 0 else []))
            dst.wrote(t2)
            src.read(t2)
            return t2

        for j in range(NT):
            tok = slice(j * TT, (j + 1) * TT)
            gtok = lambda r: rsl(r, j * TT, TT, SL)
            t = kb.dmar(SP, lambda r: cq.ap, lambda r: PROJT[R_CQ:R_CQ + 512, gtok(r)].rearrange("(c p) t -> p c t", p=128), cq.deps_for_write(), cq.sem); cq.wrote(t)
            t = kb.dmar(SP, lambda r: ckv.ap, lambda r: PROJT[R_CKV:R_CKV + 512, gtok(r)].rearrange("(c p) t -> p c t", p=128), ckv.deps_for_write(), ckv.sem); ckv.wrote(t)
            t = kb.dmar(SP, lambda r: kr.ap, lambda r: PROJT[R_KR:R_KR + 128, gtok(r)].rearrange("(c p) t -> p c t", p=64), kr.deps_for_write(), kr.sem); kr.wrote(t)
            kb.dma(SP, cst.ap[:, 0, :], COS[:, tok], cst.deps_for_write(), cst.sem)
            t = kb.dma(SP, cst.ap[:, 1, :], SIN[:, tok], [], cst.sem); cst.wrote(t)
            tq = normed(cq, cqn, qnw, l * 4)
            tk = normed(ckv, ckvn, kvnw, l * 4)
            t = dvo(lambda: DVE.e.tensor_tensor(out=r1, in0=kr.ap[:, 0, :], in1=COSv, op=ALU.mult), kr.w, cst.w)
            t = dvo(lambda: DVE.e.tensor_tensor(out=r2, in0=kr.ap[:, 1, :], in1=SINv, op=ALU.mult), t)
            t = dvo(lambda: DVE.e.tensor_tensor(out=kp_st.ap, in0=r1, in1=r2, op=ALU.add), t, *kp_st.deps_for_write())
            kr.read(t)
            kp_st.wrote(t)
            kp_st.read(kb.dmar(POOL, lambda r: KP[:, gtok(r)], lambda r: kp_st.ap, [t], kp_st.sem))
            for hh in range(NH):
                bank = kb.pring.next()
                tp = kb.mm_group(bank, [(wq3[:, c, hh, 0:128], cqn.ap[:, c, :]) for c in range(4)], [tq, tw])
                te = aco(lambda: ACT.e.activation(out=qn_st.ap[:, hh, :], in_=bank.ap, func=AF.Copy), tp, *(qn_st.deps_for_write() if hh == 0 else []))
                bank.read(te)
                b1 = kb.pring.next()
                tp1 = kb.mm_group(b1, [(wq3[:, c, hh, 128:192], cqn.ap[:, c, :]) for c in range(4)], [tq], out_ap=b1.ap[0:64, :])
                b2 = kb.pring.next()
                tp2 = kb.mm_group(b2, [(wq3[:, c, hh, 192:256], cqn.ap[:, c, :]) for c in range(4)], [tq], out_ap=b2.ap[0:64, :])
                t = dvo(lambda: DVE.e.tensor_tensor(out=r1, in0=b1.ap[0:64, :], in1=COSv, op=ALU.mult), tp1, cst.w)
                b1.read(t)
                t = dvo(lambda: DVE.e.tensor_tensor(out=r2, in0=b2.ap[0:64, :], in1=SINv, op=ALU.mult), tp2, t)
                b2.read(t)
                t = dvo(lambda: DVE.e.tensor_tensor(out=qp_st.ap[:, hh, :], in0=r1, in1=r2, op=ALU.add), t, *(qp_st.deps_for_write() if hh == 0 else []))
                bank = kb.pring.next()
                tp = kb.mm_group(bank, [(wkv3[:, c, hh, 0:128], ckvn.ap[:, c, :]) for c in range(4)], [tk, tw])
                te = aco(lambda: ACT.e.activation(out=kn_st.ap[:, hh, :], in_=bank.ap, func=AF.Copy), tp, *(kn_st.deps_for_write() if hh == 0 else []))
                bank.read(te)
            cst.read(t)
            tA = (ACT.sem, ACT.sem.n)
            tD = (DVE.sem, DVE.sem.n)
            qn_st.wrote(tA); kn_st.wrote(tA); qp_st.wrote(tD)
            qn_st.read(kb.dmar(POOL, lambda r: QN[:, :, gtok(r)].rearrange("h p t -> p h t"), lambda r: qn_st.ap, [tA], qn_st.sem))
            kn_st.read(kb.dmar(POOL, lambda r: KN[:, :, gtok(r)].rearrange("h p t -> p h t"), lambda r: kn_st.ap, [tA], kn_st.sem))
            qp_st.read(kb.dmar(POOL, lambda r: QP[:, :, gtok(r)].rearrange("h p t -> p h t"), lambda r: qp_st.ap, [tD], qp_st.sem))
            for tb in range(4):
                for hg in range(2):
                    bank = kb.pring.next()
                    tp = kb.mm_group(bank, [(ckvn.ap[:, c, tb * 128:(tb + 1) * 128], wkv3[:, c, hg * 4:(hg + 1) * 4, 128:256]) for c in range(4)], [tk, tw])
                    te = aco(lambda: ACT.e.activation(out=v_st.ap[:, tb, hg * 4:(hg + 1) * 4, 0:128], in_=bank.ap.rearrange("p (h e) -> p h e", e=128), func=AF.Copy),
                             tp, t1s, *(v_st.deps_for_write() if (tb == 0 and hg == 0) else []))
                    bank.read(te)
            cqn.read(tp); ckvn.read(tp)
            v_st.wrote(te)
            v_st.read(kb.dmar(POOL, lambda r: VS[rsl(r, j * 4, 4, SL // 128)].rearrange("b p h e -> p b h e"), lambda r: v_st.ap, [te], v_st.sem))
        kb.pair_barrier()
        kb.sb_release(mk_)

        mk_ = kb.sb_mark()
        NQ = S // 512
        kp = kb.sb("kp", [128, S], bf)
        kps = kb.dsem()
        tkp = kb.dma(SP, kp[0:64], KP, [], kps)
        tz = dvo(lambda: DVE.e.memset(kp[64:128], 0.0))
        hb = [dict(kn=Slot(kb.sb(f"kn{i}", [128, S], bf), kb.dsem()), qn=Slot(kb.sb(f"qnb{i}", [128, S], bf)),
                   qp=Slot(kb.sb(f"qpb{i}", [128, S], bf)), v=Slot(kb.sb(f"v{i}", [128, NB, 129], bf))) for i in range(1)]
        hgen = hgrn(l, hbanks=kb.banks[6:8], as_gen=True)
        next(hgen)
        hstate = dict(done=False, n=0)

        def hpull(n=1):
            for _ in range(n):
                if hstate["done"]:
                    return
                try:
                    next(hgen)
                except StopIteration:
                    hstate["done"] = True
        for B_ in hb:
            tz = dvo(lambda: DVE.e.memset(B_["qp"].ap[64:128], 0.0))
        cm = kb.sb("cm", [128, 4, 512], bf)
        tcm = kb.dma(POOL, cm.rearrange("p r f -> p (r f)"), cmask_d, [], kps)
        ptr = Ring([Slot(kb.sb(f"pt{i}", [128, 512], bf)) for i in range(3)])
        o32 = Slot(kb.sb("o32", [128, 4, 129], F32))
        rec = kb.sb("rec", [128, 4], F32)
        osq = kb.sb("osq", [128, 4, 128], F32)
        oss = kb.sb("oss", [128, 4], F32)
        onb = Slot(kb.sb("onb", [128, 4, 128], bf))
        ya = Ring([Slot(kb.sb(f"ya{i}", [128, 512], bf), kb.dsem()) for i in range(2)])
        stb = Ring(kb.banks[0:2])
        oacc = Ring([kb.banks[2:4], kb.banks[4:6]])
        scale = float((128 + 64) ** -0.5)
        for hh in range(NHL):
            B = hb[0]
            hg_ = lambda r: r * NHL + hh
            dw = B["kn"].deps_for_write() + B["qn"].r + B["qp"].r + B["v"].r
            kb.dmar(SP, lambda r: B["kn"].ap, lambda r: KN[hg_(r)], dw, B["kn"].sem)
            kb.dmar(SP, lambda r: B["qn"].ap, lambda r: QN[hg_(r)], [], B["kn"].sem)
            kb.dmar(SP, lambda r: B["qp"].ap[0:64], lambda r: QP[hg_(r)], [], B["kn"].sem)
            tl = kb.dmar(SP, lambda r: B["v"].ap, lambda r: VS[:, :, hg_(r), :].rearrange("b p e -> p b e"), [], B["kn"].sem)
            for k_ in ("kn", "qn", "qp", "v"):
                B[k_].wrote(tl)
            knS, qnS, qpS, vS = B["kn"].ap, B["qn"].ap, B["qp"].ap, B["v"].ap
            for i in range(NQ):
                qs = slice(i * 512, (i + 1) * 512)
                nkb = 4 * i + 4
                ob = oacc.next()
                ov = [ob[0].ap[:, 0:258].rearrange("p (s e) -> p s e", e=129), ob[1].ap[:, 0:258].rearrange("p (s e) -> p s e", e=129)]

                def qk(kbk):
                    bank = stb.next()
                    ks = slice(kbk * 128, (kbk + 1) * 128)
                    return bank, kb.mm_group(bank, [(knS[:, ks], qnS[:, qs]), (kp[:, ks], qpS[:, qs])], [tl, tkp, tz])

                nxt = qk(0)
                tpv = None
                for kbk in range(nkb):
                    bank, tqk = nxt
                    if kbk + 1 < nkb:
                        nxt = qk(kbk + 1)
                    pt = ptr.next()
                    te = aco(lambda: ACT.e.activation(out=pt.ap, in_=bank.ap, func=AF.Exp, scale=scale), tqk, *pt.deps_for_write())
                    bank.read(te)
                    r = kbk - 4 * i
                    if r >= 0:
                        te = poo(lambda: POOL.e.tensor_tensor(out=pt.ap, in0=pt.ap, in1=cm[:, r, :], op=ALU.mult), te, tcm)
                    pt.wrote(te)
                    PE.wait(te, *(ob[0].deps_for_write() + ob[1].deps_for_write() if kbk == 0 else []))
                    for s_ in range(4):
                        last = 4 * i + s_
                        if kbk > last:
                            continue
                        ins = PE.e.matmul(ov[s_ // 2][:, s_ % 2, :], lhsT=pt.ap[:, s_ * 128:(s_ + 1) * 128], rhs=vS[:, kbk, :], start=(kbk == 0 and s_ % 2 == 0), stop=(kbk == last))
                    tpv = PE.op(ins)
                    pt.read(tpv)
                    hstate["n"] += 1
                    if hstate["n"] % 6 == 0:
                        hpull(1)
                ob[0].wrote(tpv); ob[1].wrote(tpv)
                t = dvo(lambda: DVE.e.tensor_copy(out=o32.ap[:, 0:2, :], in_=ov[0]), tpv, *o32.deps_for_write())
                t = dvo(lambda: DVE.e.tensor_copy(out=o32.ap[:, 2:4, :], in_=ov[1]), t)
                ob[0].read(t); ob[1].read(t)
                t = dvo(lambda: DVE.e.reciprocal(out=rec, in_=o32.ap[:, :, 128]), t)
                t = dvo(lambda: DVE.e.tensor_tensor(out=o32.ap[:, :, 0:128], in0=o32.ap[:, :, 0:128], in1=rec.to_broadcast([128, 4, 128]), op=ALU.mult), t)
                t = dvo(lambda: DVE.e.tensor_tensor(out=osq, in0=o32.ap[:, :, 0:128], in1=o32.ap[:, :, 0:128], op=ALU.mult), t)
                t = dvo(lambda: DVE.e.tensor_reduce(out=oss, in_=osq, axis=AX.X, op=ALU.add), t)
                t = dvo(lambda: DVE.e.tensor_scalar(out=oss, in0=oss, scalar1=1.0 / 128, scalar2=RMS_EPS, op0=ALU.mult, op1=ALU.add), t)
                t = aco(lambda: ACT.e.activation(out=oss, in_=oss, func=AF.Sqrt), t)
                t = dvo(lambda: DVE.e.reciprocal(out=oss, in_=oss), t)
                t = dvo(lambda: DVE.e.tensor_tensor(out=onb.ap, in0=o32.ap[:, :, 0:128], in1=oss.to_broadcast([128, 4, 128]), op=ALU.mult), t, *onb.deps_for_write())
                onb.wrote(t)
                o32.wrote(t)
                bank = stb.next()
                PE.wait(t, tconst, *bank.deps_for_write())
                pv = bank.ap.bitcast(bf)
                for s_ in range(4):
                    ins = PE.e.transpose(out=pv[:, s_ * 128:(s_ + 1) * 128], in_=onb.ap[:, s_, :], identity=ident)
                tp = PE.op(ins)
                bank.wrote(tp)
                onb.read(tp)
                yb = ya.next()
                te = aco(lambda: ACT.e.activation(out=yb.ap, in_=pv[:, 0:512], func=AF.Copy, scale=monw[:, l:l + 1]), tp, *yb.deps_for_write())
                bank.read(te)
                yb.wrote(te)
                yb.read(kb.dmar(POOL, lambda r: MIXT[hg_(r) * 128:(hg_(r) + 1) * 128, qs], lambda r: yb.ap, [te], yb.sem))
            for k_ in ("kn", "qn", "qp", "v"):
                B[k_].read(tpv)
        while not hstate["done"]:
            hpull(1)
        kb.barrier()
        kb.sb_release(mk_)

    kb.mla = mla
    def conv(l):
        mk_ = kb.sb_mark()
        PADL = 30
        G = kb.sb("G", [128, 4, PADL + SL], bf)
        cwt = kb.sb("cwt", [128, 4, 31], F32)
        Dg = kb.sb("Dg", [128, 4, 31, 128], bf)
        csm = kb.dsem()
        tcw = kb.dma(SP, cwt.rearrange("p c j -> p (c j)"), cw_d[:, l * 124:(l + 1) * 124], [], csm)
        t0 = dvo(lambda: DVE.e.memset(G[:, :, 0:PADL], 0.0))
        tD = None
        for c in range(4):
            for jj in range(31):
                tD = dvo(lambda: DVE.e.tensor_scalar(out=Dg[:, c, jj, :], in0=ident, scalar1=cwt[:, c, jj:jj + 1], scalar2=None, op0=ALU.mult), tcw, tconst)
        av = Slot(kb.sb("av", [128, 4, TT], bf), kb.dsem())
        gv = Slot(kb.sb("gv", [128, 4, TT], bf), kb.dsem())
        sgm = kb.sb("sgm", [128, 4, TT], F32)
        hsl_ = lambda r: rsl(r, 0, PADL, SL - PADL)
        t = kb.dmar(SP, lambda r: av.ap[:, :, 0:PADL], lambda r: PROJT[R_CONV:R_CONV + 512, hsl_(r)].rearrange("(c p) t -> p c t", p=128), [], av.sem)
        t = kb.dmar(SP, lambda r: gv.ap[:, :, 0:PADL], lambda r: PROJT[R_CONV + 512:R_CONV + 1024, hsl_(r)].rearrange("(c p) t -> p c t", p=128), [], av.sem)
        av.wrote(t); gv.wrote(t)
        ts_ = aco(lambda: ACT.e.activation(out=sgm[:, :, 0:PADL], in_=gv.ap[:, :, 0:PADL], func=AF.Sigmoid), t)
        t0 = dvo(lambda: DVE.e.tensor_tensor(out=sgm[:, :, 0:PADL], in0=sgm[:, :, 0:PADL], in1=av.ap[:, :, 0:PADL], op=ALU.mult), ts_, t0)
        t0 = dvo(lambda: DVE.e.tensor_scalar(out=G[:, :, 0:PADL], in0=sgm[:, :, 0:PADL], scalar1=rankf[:, 0:1], scalar2=None, op0=ALU.mult), t0, tconst)
        av.read(t0); gv.read(t0)
        tG = t0
        for j in range(NT):
            tok = slice(j * TT, (j + 1) * TT)
            gtok = lambda r: rsl(r, j * TT, TT, SL)
            t = kb.dmar(SP, lambda r: av.ap, lambda r: PROJT[R_CONV:R_CONV + 512, gtok(r)].rearrange("(c p) t -> p c t", p=128), av.deps_for_write(), av.sem); av.wrote(t)
            t = kb.dmar(SP, lambda r: gv.ap, lambda r: PROJT[R_CONV + 512:R_CONV + 1024, gtok(r)].rearrange("(c p) t -> p c t", p=128), gv.deps_for_write(), gv.sem); gv.wrote(t)
            ts_ = aco(lambda: ACT.e.activation(out=sgm, in_=gv.ap, func=AF.Sigmoid), gv.w, tG)
            gv.read(ts_)
            tG = dvo(lambda: DVE.e.tensor_tensor(out=G[:, :, PADL + j * TT: PADL + (j + 1) * TT], in0=av.ap, in1=sgm, op=ALU.mult), ts_, av.w, t0)
            av.read(tG)
        xc = Slot(kb.sb("xc", [128, 4, TT], F32))
        xq = Slot(kb.sb("xq", [128, 4, TT], F32))
        mean = kb.sb("mean", [128, TT], F32)
        var = kb.sb("var", [128, TT], F32)
        yb = Ring([Slot(kb.sb(f"yb{i}", [128, 4, TT], bf), kb.dsem()) for i in range(2)])
        for j in range(NT):
            tok = slice(j * TT, (j + 1) * TT)
            gtok = lambda r: rsl(r, j * TT, TT, SL)
            te = None
            for c in range(4):
                bank = kb.pring.next()
                tp = kb.mm_group(bank, [(Dg[:, c, jj, :], G[:, c, j * TT + jj: j * TT + jj + TT]) for jj in range(31)], [tG, tD])
                te = aco(lambda: ACT.e.activation(out=xc.ap[:, c, :], in_=bank.ap, func=AF.Identity, bias=cb[:, l * 4 + c:l * 4 + c + 1], scale=1.0), tp, tconst, *(xc.deps_for_write() if c == 0 else []))
                bank.read(te)
            xc.wrote(te)
            tq = dvo(lambda: DVE.e.tensor_tensor(out=xq.ap, in0=xc.ap, in1=xc.ap, op=ALU.mult), te, *xq.deps_for_write())
            xq.wrote(tq)
            bm = kb.pring.next()
            tpm = kb.mm_group(bm, [(ones32, xc.ap[:, c, :]) for c in range(4)], [te])
            bv = kb.pring.next()
            tpv = kb.mm_group(bv, [(ones32, xq.ap[:, c, :]) for c in range(4)], [tq])
            xq.read(tpv)
            t = dvo(lambda: DVE.e.tensor_scalar(out=mean, in0=bm.ap, scalar1=1.0 / 512, scalar2=None, op0=ALU.mult), tpm)
            bm.read(t)
            t = dvo(lambda: DVE.e.tensor_scalar(out=var, in0=bv.ap, scalar1=1.0 / 512, scalar2=LN_EPS, op0=ALU.mult, op1=ALU.add), tpv, t)
            bv.read(t)
            t = dvo(lambda: DVE.e.tensor_tensor(out=xq.ap[:, 0, :], in0=mean, in1=mean, op=ALU.mult), t)
            t = dvo(lambda: DVE.e.tensor_tensor(out=var, in0=var, in1=xq.ap[:, 0, :], op=ALU.subtract), t)
            t = aco(lambda: ACT.e.activation(out=var, in_=var, func=AF.Sqrt), t)
            t = dvo(lambda: DVE.e.reciprocal(out=var, in_=var), t)
            t = dvo(lambda: DVE.e.tensor_tensor(out=xc.ap, in0=xc.ap, in1=mean.unsqueeze(1).to_broadcast([128, 4, TT]), op=ALU.subtract), t)
            t = dvo(lambda: DVE.e.tensor_tensor(out=xc.ap, in0=xc.ap, in1=var.unsqueeze(1).to_broadcast([128, 4, TT]), op=ALU.mult), t)
            y = yb.next()
            for c in range(4):
                ta = aco(lambda: ACT.e.activation(out=y.ap[:, c, :], in_=xc.ap[:, c, :], func=AF.Silu, bias=lnb[:, l * 4 + c:l * 4 + c + 1], scale=lnw[:, l * 4 + c:l * 4 + c + 1]), t, *(y.deps_for_write() if c == 0 else []))
            xc.read(ta)
            y.wrote(ta)
            y.read(kb.dmar(POOL, lambda r: MIXT[1024:1536, gtok(r)].rearrange("(c p) t -> p c t", p=128), lambda r: y.ap, [ta], y.sem))
        kb.barrier()
        kb.sb_release(mk_)

    kb.conv = conv
    def hgrn(l, hbanks=None, as_gen=False):
        hring = Ring(hbanks) if hbanks is not None else kb.pring
        mk_ = kb.sb_mark()
        SEG = min(S, 1024)
        NSEG = S // SEG
        NCH = SEG // 64
        inb = {k_: Slot(kb.sb("in_" + k_, [128, SEG], bf), kb.dsem()) for k_ in ("q", "f", "i", "g")}
        F0 = kb.sb("F0", [128, SEG], F32)
        K32 = kb.sb("K32", [128, SEG], F32)
        B0 = kb.sb("B0", [128, SEG], F32)
        B1 = kb.sb("B1", [128, SEG], F32)
        QF = kb.sb("QF", [128, SEG], F32)
        E = kb.sb("E", [128, SEG], F32)
        QT = kb.sb("QT", [128, SEG], bf)
        KT = kb.sb("KT", [128, SEG], bf)
        QH = kb.sb("QH", [128, SEG], bf)
        KH = kb.sb("KH", [128, SEG], bf)
        EBL = kb.sb("EBL", [128, NCH], F32)
        vtok = kb.sb("vtok", [64, NCH, 128], bf)
        khtok = kb.sb("khtok", [64, NCH, 128], bf)
        Ot = kb.sb("Ot", [64, NCH, 128], F32)
        osq = kb.sb("osq", [64, NCH, 128], F32)
        oss = kb.sb("oss", [64, NCH], F32)
        On = kb.sb("On", [64, NCH, 128], bf)
        YC = Slot(kb.sb("YC", [128, SEG], bf), kb.dsem())
        S32 = kb.sb("S32", [128, 128], F32)
        Sbf = [kb.sb(f"Sbf{i}", [128, 128], bf) for i in range(2)]
        scm = Ring([Slot(kb.sb(f"scm{i}", [64, 64], bf)) for i in range(2)])
        m64b = kb.sb("m64b", [64, 64], F32)
        v3 = lambda ap: ap.rearrange("p (c k) -> p c k", k=64)
        tlast = None
        yield
        for hh in range(HHL):
            t = dvo(lambda: DVE.e.memset(S32, 0.0), tlast)
            tS = dvo(lambda: DVE.e.memset(Sbf[0], 0.0), t)
            cur = 0
            for sg_ in range(NSEG):
                tok = slice(sg_ * SEG, (sg_ + 1) * SEG)
                tin = None
                for k_, r0 in (("q", R_HQ), ("f", R_HF), ("i", R_HI), ("g", R_HG)):
                    tin = kb.dmar(SP, lambda r: inb[k_].ap, lambda r: PROJT[rsl(r, r0 + hh * 128, 128, HHL * 128), tok], [tlast, (PE.sem, PE.sem.n)], inb["q"].sem)
                t = aco(lambda: ACT.e.activation(out=F0, in_=inb["f"].ap, func=AF.Sigmoid), tin, tlast)
                t = dvo(lambda: DVE.e.tensor_scalar(out=F0, in0=F0, scalar1=oml[:, hh, l:l + 1], scalar2=lbt[:, hh, l:l + 1], op0=ALU.mult, op1=ALU.add), t, t_lb)
                tk = dvo(lambda: DVE.e.tensor_scalar(out=K32, in0=F0, scalar1=-1.0, scalar2=1.0, op0=ALU.mult, op1=ALU.add), t)
                t = aco(lambda: ACT.e.activation(out=B0, in_=F0, func=AF.Ln), t)
                tq = aco(lambda: ACT.e.activation(out=QF, in_=inb["q"].ap, func=AF.Silu), tin)
                src, dst = B0, B1
                for d in (1, 2, 4, 8, 16, 32):
                    t1 = dvo(lambda: DVE.e.tensor_tensor(out=v3(dst)[:, :, d:64], in0=v3(src)[:, :, d:64], in1=v3(src)[:, :, 0:64 - d], op=ALU.add), t)
                    t = dvo(lambda: DVE.e.tensor_copy(out=v3(dst)[:, :, 0:d], in_=v3(src)[:, :, 0:d]), t1)
                    src, dst = dst, src
                b3 = v3(B0)
                t = dvo(lambda: DVE.e.tensor_tensor(out=v3(B1), in0=b3, in1=b3[:, :, 31:32].to_broadcast([128, NCH, 64]), op=ALU.subtract), t)
                te = aco(lambda: ACT.e.activation(out=E, in_=B1, func=AF.Exp), t)
                t2 = dvo(lambda: DVE.e.tensor_tensor(out=QT, in0=QF, in1=E, op=ALU.mult), te, tq)
                te = aco(lambda: ACT.e.activation(out=E, in_=B1, func=AF.Exp, scale=-1.0), t2)
                t2 = dvo(lambda: DVE.e.tensor_tensor(out=KT, in0=K32, in1=E, op=ALU.mult), te, tk)
                te = aco(lambda: ACT.e.activation(out=E, in_=B0, func=AF.Exp), t2)
                t2 = dvo(lambda: DVE.e.tensor_tensor(out=QH, in0=QF, in1=E, op=ALU.mult), te)
                t = dvo(lambda: DVE.e.tensor_tensor(out=v3(B1), in0=b3, in1=b3[:, :, 63:64].to_broadcast([128, NCH, 64]), op=ALU.subtract), t2)
                te = aco(lambda: ACT.e.activation(out=E, in_=B1, func=AF.Exp, scale=-1.0), t)
                t2 = dvo(lambda: DVE.e.tensor_tensor(out=KH, in0=K32, in1=E, op=ALU.mult), te)
                te = aco(lambda: ACT.e.activation(out=EBL, in_=b3[:, :, 63], func=AF.Exp), t2)
                tsg = aco(lambda: ACT.e.activation(out=QF, in_=inb["g"].ap, func=AF.Silu), t2)
                tel = (ACT.sem, ACT.sem.n)
                tdv = (DVE.sem, DVE.sem.n)
                yield
                tev = None
                for (srcT, dstK, dep) in ((inb["i"].ap, vtok, tin), (KH, khtok, tdv)):
                    for c0 in range(0, NCH, 8):
                        bank = hring.next()
                        PE.wait(dep, tconst, *bank.deps_for_write())
                        pv = bank.ap.bitcast(bf)
                        for c in range(8):
                            ins = PE.e.transpose(out=pv[0:64, c * 128:(c + 1) * 128], in_=srcT[:, (c0 + c) * 64:(c0 + c + 1) * 64], identity=ident)
                        tp = PE.op(ins)
                        bank.wrote(tp)
                        tev = dvo(lambda: DVE.e.tensor_copy(out=dstK[:, c0:c0 + 8, :], in_=pv[0:64, 0:1024].rearrange("p (c e) -> p c e", e=128)), tp)
                        bank.read(tev)
                yield
                tO = None
                for c in range(NCH):
                    ch = slice(c * 64, (c + 1) * 64)
                    b1 = hring.next()
                    tp1 = kb.mm_group(b1, [(KT[:, ch], QT[:, ch])], [tdv, tev], out_ap=b1.ap[0:64, 0:64])
                    sc = scm.next()
                    tm = dvo(lambda: DVE.e.tensor_tensor(out=sc.ap, in0=b1.ap[0:64, 0:64], in1=m64, op=ALU.mult), tp1, *sc.deps_for_write())
                    b1.read(tm)
                    sc.wrote(tm)
                    b2 = hring.next()
                    tp2 = kb.mm_group(b2, [(QH[:, ch], Sbf[cur]), (sc.ap, vtok[:, c, :])], [tm, tS, tev], out_ap=b2.ap[0:64, 0:128])
                    sc.read(tp2)
                    b3_ = hring.next()
                    tp3 = kb.mm_group(b3_, [(khtok[:, c, :], vtok[:, c, :])], [tev], out_ap=b3_.ap[:, 0:128])
                    tu = dvo(lambda: DVE.e.scalar_tensor_tensor(out=S32, in0=S32, scalar=EBL[:, c:c + 1], in1=b3_.ap[:, 0:128], op0=ALU.mult, op1=ALU.add), tp3, tel)
                    b3_.read(tu)
                    cur = 1 - cur
                    tS = aco(lambda: ACT.e.activation(out=Sbf[cur], in_=S32, func=AF.Copy), tu, tp2)
                    tO = aco(lambda: ACT.e.activation(out=Ot[:, c, :], in_=b2.ap[0:64, 0:128], func=AF.Copy), tp2)
                    b2.read(tO)
                    yield
                t = dvo(lambda: DVE.e.tensor_tensor(out=osq, in0=Ot, in1=Ot, op=ALU.mult), tO)
                t = dvo(lambda: DVE.e.tensor_reduce(out=oss, in_=osq, axis=AX.X, op=ALU.add), t)
                t = dvo(lambda: DVE.e.tensor_scalar(out=oss, in0=oss, scalar1=1.0 / 128, scalar2=RMS_EPS, op0=ALU.mult, op1=ALU.add), t)
                t = aco(lambda: ACT.e.activation(out=oss, in_=oss, func=AF.Sqrt), t)
                t = dvo(lambda: DVE.e.reciprocal(out=oss, in_=oss), t)
                t = dvo(lambda: DVE.e.tensor_tensor(out=On, in0=Ot, in1=oss.to_broadcast([64, NCH, 128]), op=ALU.mult), t)
                ty = None
                for c0 in range(0, NCH, 8):
                    bank = hring.next()
                    PE.wait(t, *bank.deps_for_write())
                    pv = bank.ap.bitcast(bf)
                    for c in range(8):
                        ins = PE.e.transpose(out=pv[:, c * 64:(c + 1) * 64], in_=On[:, c0 + c, :], identity=ident[0:64, 0:64])
                    tp = PE.op(ins)
                    bank.wrote(tp)
                    ty = dvo(lambda: DVE.e.scalar_tensor_tensor(out=YC.ap[:, c0 * 64:(c0 + 8) * 64], in0=pv[:, 0:512], scalar=hnw[:, l:l + 1], in1=QF[:, c0 * 64:(c0 + 8) * 64], op0=ALU.mult, op1=ALU.mult),
                             tp, tsg, *(YC.deps_for_write() if c0 == 0 else []))
                    bank.read(ty)
                YC.wrote(ty)
                tst = kb.dmar(POOL, lambda r: MIXT[rsl(r, 1536 + hh * 128, 128, HHL * 128), tok], lambda r: YC.ap, [ty], YC.sem)
                YC.read(tst)
                tlast = [(ACT.sem, ACT.sem.n), (DVE.sem, DVE.sem.n), (PE.sem, PE.sem.n)]
                yield
        if not as_gen:
            kb.barrier()
            kb.sb_release(mk_)

    def hgrn_run(l):
        for _ in hgrn(l):
            pass

    kb.hgrn = hgrn_run
    kb.sweep = sweep
    kb.loc = locals()
    return kb, kb.loc


def finish(kb):
    kb.barrier()


NONCE = 1234567
FLIP = 0


def _layout_inputs(inp, L, S, b, r=0, pair=False):
    f32 = np.float32
    SL = S // 2 if pair else S
    lsel = slice(r, None, 2) if pair else slice(None)
    hsel = slice(2 * r, 2 * r + 2) if pair else slice(None)
    HHL = 2 if pair else HH

    def pc(a, n):
        a = np.asarray(a, f32).reshape(L, n, 128)
        return np.ascontiguousarray(a.transpose(2, 0, 1).reshape(128, L * n))

    cw = np.asarray(inp["conv_w"], f32).reshape(L, 31, 4, 128).transpose(3, 0, 2, 1).reshape(128, L * 4 * 31)
    lb = np.asarray(inp["hgrn_lower_bounds"], f32).reshape(L, HH, 128)[:, hsel].transpose(2, 1, 0).reshape(128, HHL * L)
    cm = np.zeros((4, 128, 512), f32)
    p = np.arange(128)[:, None]
    f = np.arange(512)[None, :]
    for q in range(4):
        cm[q] = (p + 128 * q <= f)
    invf = (10000.0 ** (-np.arange(0, 64, 2, dtype=f32) / 64)).astype(f32)
    invf2 = np.concatenate([invf, invf])
    w = lambda k: np.ascontiguousarray(np.asarray(inp[k], f32)[lsel])
    m = {
        "x": np.ascontiguousarray(np.asarray(inp["x"], f32)[b, r * SL:(r + 1) * SL]),
        "pos": np.ascontiguousarray(np.asarray(inp["positions"], np.int32)[b, r * SL:(r + 1) * SL][None, :]),
        "w_in": w("w_in"), "w_uq": w("w_uq"), "w_ukv": w("w_ukv"), "w_out": w("w_out"),
        "w_gate": w("w_gate"), "w_up": w("w_up"), "w_down": w("w_down"),
        "anw": pc(inp["attn_norm_w"], 16), "fnw": pc(inp["ffn_norm_w"], 16),
        "finw": np.asarray(inp["final_norm_w"], f32)[None, :].copy(),
        "qnw": pc(inp["q_norm_w"], 4), "kvnw": pc(inp["kv_norm_w"], 4),
        "monw": pc(inp["mla_out_norm_w"], 1), "hnw": pc(inp["hgrn_norm_w"], 1),
        "cw": np.ascontiguousarray(cw), "cb": pc(inp["conv_b"], 4),
        "lnw": pc(inp["conv_ln_w"], 4), "lnb": pc(inp["conv_ln_b"], 4),
        "lbraw": np.ascontiguousarray(lb),
        "ident": np.eye(128, dtype=f32),
        "cmask": np.ascontiguousarray(cm.transpose(1, 0, 2).reshape(128, 4 * 512)),
        "m64": (np.arange(64)[:, None] <= np.arange(64)[None, :]).astype(f32),
        "invf": np.stack([invf2, invf2], axis=1).astype(f32),
        "rankf": np.full((128, 1), float(r), f32),
        "fv": np.full((1, 8), NONCE, np.int32),
        "zz": np.zeros((1, 64), np.int32),
    }
    return m


def build_full(S, L, debug=False, pair=False):
    kb, _ = build(S, L, debug, pair, NONCE)
    kb.sweep(0)
    for l in range(L):
        kb.mla(l)
        kb.conv(l)
        kb.pair_barrier()
        kb.sweep(l + 1)
    if pair:
        t = kb.dmar(kb.POOL, lambda r: kb.FL[r:r + 1], lambda r: kb.loc["zz_d"], [], kb.flsem)
    finish(kb)
    return kb


def kernel(**inputs):
    x = np.asarray(inputs["x"])
    B, S, _ = x.shape
    L = np.asarray(inputs["w_in"]).shape[0]
    pair = True
    kb = build_full(S, L, pair=pair)
    in_maps = [_layout_inputs(inputs, L, S, b, r, pair) for b in range(B) for r in range(2)]
    res = run_bass_kernel_spmd(kb.nc, in_maps, core_ids=list(range(2 * B)))
    outs = [np.concatenate([np.asarray(res.results[2 * b + r]["out"]) for r in range(2)], axis=0) for b in range(B)]
    return np.stack(outs, axis=0).astype(np.float32)
```
